# Optimizing a Trainium2 kernel written in Bass

```python
import math
import jax, jax.numpy as jnp
from jax import lax
import numpy as np

D_MODEL = 2048
BATCH = 8
SEQ = 2048
DEPTH = 2

GRID_W = 64
CTX_LEN = 256
N_EVEN = (DEPTH + 1) // 2
N_ODD = DEPTH // 2
EPS = 1e-6
N_MOD = 6
D_FF = 4 * D_MODEL

HEAD_DIM = 128
NA_HEADS = 8
NA_WIDTH = NA_HEADS * HEAD_DIM
NA_KR = 8
NA_KC = 16
LRU_WIDTH = D_MODEL - NA_WIDTH
LRU_BLOCKS = 8
LRU_BLOCK = LRU_WIDTH // LRU_BLOCKS
LRU_C = 8.0
CONV_W = 4
CONV_PAD_LO = 2
AB_IN = 3 * NA_WIDTH + 2 * LRU_WIDTH
MIX_WIDTH = NA_WIDTH + LRU_WIDTH

MLA_HEADS = 16
Q_LORA = 512
KV_LORA = 512
NOPE_DIM = 128
ROPE_DIM = 64
V_DIM = 128
ROPE_BASE = 10000.0
Q_BLOCK = 128
NEG_INF = -1e30

kernel_name = "hybrid_natten_rglru_mla_prefix_dit"


def rms_norm(x, gain):
    x32 = x.astype(jnp.float32)
    y = x32 * lax.rsqrt(jnp.mean(jnp.square(x32), axis=-1, keepdims=True) + EPS)
    return (y * gain.astype(jnp.float32)).astype(x.dtype)


def modulate(h, shift, scale):
    return h * (1.0 + scale) + shift


def split_heads(t, n_heads):
    b, n, _ = t.shape
    return t.reshape(b, n, n_heads, -1).transpose(0, 2, 1, 3)


def merge_heads(t):
    b, h, n, d = t.shape
    return t.transpose(0, 2, 1, 3).reshape(b, n, h * d)


def softmax_attend(q, k, v, scale):
    s = jnp.einsum("bhqd,bhkd->bhqk", q, k).astype(jnp.float32) * scale
    p = jax.nn.softmax(s, axis=-1).astype(v.dtype)
    return jnp.einsum("bhqk,bhkd->bhqd", p, v)


def blocked_attend(q, k, v, scale):
    b, h, n, d = q.shape
    nb = n // Q_BLOCK
    qb = q.reshape(b, h, nb, Q_BLOCK, d).transpose(2, 0, 1, 3, 4)
    ob = lax.map(lambda qi: softmax_attend(qi, k, v, scale), qb)
    return ob.transpose(1, 2, 0, 3, 4).reshape(b, h, n, -1)


def squared_relu_mlp(h, w1, w2):
    return jnp.square(jax.nn.relu(h @ w1)) @ w2


def rope_tables(n):
    pos = jnp.arange(n)
    row = (pos // GRID_W).astype(jnp.float32)
    col = (pos % GRID_W).astype(jnp.float32)
    axis_dim = ROPE_DIM // 2
    inv = jnp.power(ROPE_BASE, -jnp.arange(0, axis_dim, 2, dtype=jnp.float32) / axis_dim)
    ang = jnp.stack([row[:, None] * inv, col[:, None] * inv], axis=1)
    return jnp.cos(ang), jnp.sin(ang)


def apply_rope(t, cos, sin):
    t32 = t.astype(jnp.float32).reshape(t.shape[:-1] + (2, 2, ROPE_DIM // 4))
    x1, x2 = t32[..., 0, :], t32[..., 1, :]
    out = jnp.stack([x1 * cos - x2 * sin, x1 * sin + x2 * cos], axis=-2)
    return out.reshape(t.shape).astype(t.dtype)


def neighbourhood_attention(q, k, v, q_c, k_c, v_c, rpb):
    b, h, s, d = q.shape
    rows = s // GRID_W
    kr = min(NA_KR, rows)
    scale = d ** -0.5
    r = jnp.arange(rows)
    row_start = jnp.clip(r - kr // 2, 0, rows - kr)
    row_idx = row_start[:, None] + jnp.arange(kr)[None, :]
    col = jnp.arange(GRID_W)
    col_start = jnp.clip(col - NA_KC // 2, 0, GRID_W - NA_KC)
    col_ok = (col[None, :] >= col_start[:, None]) & (col[None, :] < col_start[:, None] + NA_KC)
    dr = row_idx - r[:, None] + (NA_KR - 1)
    dc = jnp.clip(col[None, :] - col[:, None], -(NA_KC - 1), NA_KC - 1) + (NA_KC - 1)
    bias = rpb[:, dr[:, None, :, None], dc[None, :, None, :]]

    qg = q.reshape(b, h, rows, GRID_W, d)
    kg = k.reshape(b, h, rows, GRID_W, d)
    vg = v.reshape(b, h, rows, GRID_W, d)
    k_win = jnp.take(kg, row_idx, axis=2)
    v_win = jnp.take(vg, row_idx, axis=2)
    s_win = (jnp.einsum("bhrqd,bhrkcd->bhrqkc", qg, k_win).astype(jnp.float32) * scale
             + bias[None].astype(jnp.float32))
    s_win = jnp.where(col_ok[:, None, :], s_win, NEG_INF)
    s_ctx = jnp.einsum("bhrqd,bhnd->bhrqn", qg, k_c).astype(jnp.float32) * scale
    n_win = kr * GRID_W
    s_all = jnp.concatenate([s_win.reshape(b, h, rows, GRID_W, n_win), s_ctx], axis=-1)
    p = jax.nn.softmax(s_all, axis=-1).astype(v.dtype)
    p_win = p[..., :n_win].reshape(b, h, rows, GRID_W, kr, GRID_W)
    p_ctx = p[..., n_win:]
    o = (jnp.einsum("bhrqkc,bhrkcd->bhrqd", p_win, v_win)
         + jnp.einsum("bhrqn,bhnd->bhrqd", p_ctx, v_c))
    o_lat = o.reshape(b, h, s, d)
    o_ctx = None if q_c is None else softmax_attend(q_c, k_c, v_c, scale)
    return o_lat, o_ctx


def depthwise_conv(x, w, bias):
    n = x.shape[1]
    xp = jnp.pad(x, ((0, 0), (CONV_PAD_LO, CONV_W - 1 - CONV_PAD_LO), (0, 0)))
    y = bias
    for tap in range(CONV_W):
        y = y + xp[:, tap:tap + n] * w[tap]
    return y


def rglru_coeffs(xc, wa, ba, wx, bx, lam):
    b, n, _ = xc.shape
    xb = xc.reshape(b, n, LRU_BLOCKS, LRU_BLOCK)
    gate_r = jax.nn.sigmoid(jnp.einsum("btni,nij->btnj", xb, wa).reshape(b, n, -1) + ba)
    gate_i = jax.nn.sigmoid(jnp.einsum("btni,nij->btnj", xb, wx).reshape(b, n, -1) + bx)
    log_a = -LRU_C * gate_r.astype(jnp.float32) * jax.nn.softplus(-lam.astype(jnp.float32))
    a = jnp.exp(log_a)
    u = jnp.sqrt(-jnp.expm1(2.0 * log_a)) * (gate_i * xc).astype(jnp.float32)
    return a, u


def linear_scan(a, u, reverse):
    def combine(left, right):
        a_l, u_l = left
        a_r, u_r = right
        return a_l * a_r, a_r * u_l + u_r
    _, hs = lax.associative_scan(combine, (a, u), reverse=reverse, axis=1)
    return hs


def bidirectional_rglru(u_lat, u_ctx, conv_w, conv_b, wa, ba, wx, bx, lam, want_ctx):
    xc_l = depthwise_conv(u_lat, conv_w, conv_b)
    xc_c = depthwise_conv(u_ctx, conv_w, conv_b)
    outs_l, outs_c = [], []
    for d, rev in ((0, False), (1, True)):
        a_c, v_c = rglru_coeffs(xc_c, wa[d], ba[d], wx[d], bx[d], lam[d])
        h_c = linear_scan(a_c, v_c, rev)
        end = 0 if rev else -1
        h0 = h_c[:, end]
        a_l, v_l = rglru_coeffs(xc_l, wa[d], ba[d], wx[d], bx[d], lam[d])
        start = -1 if rev else 0
        v_l = v_l.at[:, start].add(a_l[:, start] * h0)
        outs_l.append(linear_scan(a_l, v_l, rev))
        outs_c.append(h_c)
    y_lat = (outs_l[0] + outs_l[1]).astype(u_lat.dtype)
    y_ctx = (outs_c[0] + outs_c[1]).astype(u_ctx.dtype) if want_ctx else None
    return y_lat, y_ctx


def na_rglru_mixer(h_lat, h_ctx, w_in, w_out, rpb, conv_w, conv_b, wa, ba, wx, bx, lam, want_ctx):
    cuts = [NA_WIDTH, 2 * NA_WIDTH, 3 * NA_WIDTH, 3 * NA_WIDTH + LRU_WIDTH]
    q_l, k_l, v_l, u_l, g_l = jnp.split(h_lat @ w_in, cuts, axis=-1)
    w_q, w_k, w_v, w_u, w_g = jnp.split(w_in, cuts, axis=1)
    k_c, v_c, u_c = h_ctx @ w_k, h_ctx @ w_v, h_ctx @ w_u
    q_c = split_heads(h_ctx @ w_q, NA_HEADS) if want_ctx else None
    na_l, na_c = neighbourhood_attention(
        split_heads(q_l, NA_HEADS), split_heads(k_l, NA_HEADS), split_heads(v_l, NA_HEADS),
        q_c, split_heads(k_c, NA_HEADS), split_heads(v_c, NA_HEADS), rpb)
    r_l, r_c = bidirectional_rglru(u_l, u_c, conv_w, conv_b, wa, ba, wx, bx, lam, want_ctx)
    y_lat = jnp.concatenate([merge_heads(na_l), r_l * jax.nn.gelu(g_l)], axis=-1) @ w_out
    y_ctx = None
    if want_ctx:
        y_ctx = jnp.concatenate([merge_heads(na_c), r_c * jax.nn.gelu(h_ctx @ w_g)], axis=-1) @ w_out
    return y_lat, y_ctx


def mla_queries(h, w_dq, q_norm, w_uq, rope):
    q = split_heads(rms_norm(h @ w_dq, q_norm) @ w_uq, MLA_HEADS)
    if rope is not None:
        q = jnp.concatenate([q[..., :NOPE_DIM], apply_rope(q[..., NOPE_DIM:], *rope)], axis=-1)
    return q


def mla_keys_values(h, w_dkv, kv_norm, w_ukv, rope):
    p = h @ w_dkv
    c_kv, k_pe = p[..., :KV_LORA], p[..., KV_LORA:]
    kv = split_heads(rms_norm(c_kv, kv_norm) @ w_ukv, MLA_HEADS)
    k_nope, v = kv[..., :NOPE_DIM], kv[..., NOPE_DIM:]
    k_pe = k_pe[:, None]
    if rope is not None:
        k_pe = apply_rope(k_pe, *rope)
    k = jnp.concatenate([k_nope, jnp.broadcast_to(k_pe, k_nope.shape[:-1] + (ROPE_DIM,))], axis=-1)
    return k, v


def mla_mixer(h_lat, h_ctx, w_dq, w_dkv, q_norm, w_uq, kv_norm, w_ukv, w_o, want_ctx):
    scale = (NOPE_DIM + ROPE_DIM) ** -0.5
    rope = rope_tables(h_lat.shape[1])
    q_l = mla_queries(h_lat, w_dq, q_norm, w_uq, rope)
    k_l, v_l = mla_keys_values(h_lat, w_dkv, kv_norm, w_ukv, rope)
    k_c, v_c = mla_keys_values(h_ctx, w_dkv, kv_norm, w_ukv, None)
    k_all = jnp.concatenate([k_l, k_c], axis=2)
    v_all = jnp.concatenate([v_l, v_c], axis=2)
    y_lat = merge_heads(blocked_attend(q_l, k_all, v_all, scale)) @ w_o
    y_ctx = None
    if want_ctx:
        q_c = mla_queries(h_ctx, w_dq, q_norm, w_uq, None)
        y_ctx = merge_heads(softmax_attend(q_c, k_c, v_c, scale)) @ w_o
    return y_lat, y_ctx


def _dense(key, shape, fan_in, gain=1.0):
    return gain * jax.random.normal(key, shape, jnp.float32) * fan_in ** -0.5


def _normal(key, shape, std=1.0):
    return std * jax.random.normal(key, shape, jnp.float32)


def setup_inputs(seed: int = 0) -> dict:
    key = jax.random.key(seed)
    ks = jax.random.split(key, 28)
    D = D_MODEL
    a_base = jax.random.uniform(ks[19], (N_EVEN, 2, LRU_WIDTH), jnp.float32, 0.9, 0.999) ** (1.0 / LRU_C)
    return {
        "x": _normal(ks[0], (BATCH, SEQ, D)),
        "c": _normal(ks[1], (BATCH, D)),
        "ctx": _normal(ks[2], (BATCH, CTX_LEN, D)),
        "c_ctx": _normal(ks[3], (D,)),
        "mod_w": _dense(ks[4], (DEPTH, D, N_MOD * D), D, 0.5),
        "mod_b": _normal(ks[5], (DEPTH, N_MOD * D), 0.02),
        "norm_mix": 1.0 + _normal(ks[6], (DEPTH, D), 0.02),
        "norm_mlp": 1.0 + _normal(ks[7], (DEPTH, D), 0.02),
        "mlp_w1": _dense(ks[8], (DEPTH, D, D_FF), D),
        "mlp_w2": _dense(ks[9], (DEPTH, D_FF, D), D_FF),
        "ab_w_in": _dense(ks[10], (N_EVEN, D, AB_IN), D),
        "ab_w_out": _dense(ks[11], (N_EVEN, MIX_WIDTH, D), MIX_WIDTH),
        "na_rpb": _normal(ks[12], (N_EVEN, NA_HEADS, 2 * NA_KR - 1, 2 * NA_KC - 1), 0.1),
        "lru_conv_w": _dense(ks[13], (N_EVEN, CONV_W, LRU_WIDTH), CONV_W),
        "lru_conv_b": _normal(ks[14], (N_EVEN, LRU_WIDTH), 0.02),
        "lru_wa": _dense(ks[15], (N_EVEN, 2, LRU_BLOCKS, LRU_BLOCK, LRU_BLOCK), LRU_BLOCK),
        "lru_ba": _normal(ks[16], (N_EVEN, 2, LRU_WIDTH), 0.02),
        "lru_wx": _dense(ks[17], (N_EVEN, 2, LRU_BLOCKS, LRU_BLOCK, LRU_BLOCK), LRU_BLOCK),
        "lru_bx": _normal(ks[18], (N_EVEN, 2, LRU_WIDTH), 0.02),
        "lru_lambda": jnp.log(a_base) - jnp.log1p(-a_base),
        "mla_w_dq": _dense(ks[20], (N_ODD, D, Q_LORA), D),
        "mla_w_dkv": _dense(ks[21], (N_ODD, D, KV_LORA + ROPE_DIM), D),
        "mla_q_norm": 1.0 + _normal(ks[22], (N_ODD, Q_LORA), 0.02),
        "mla_w_uq": _dense(ks[23], (N_ODD, Q_LORA, MLA_HEADS * (NOPE_DIM + ROPE_DIM)), Q_LORA),
        "mla_kv_norm": 1.0 + _normal(ks[24], (N_ODD, KV_LORA), 0.02),
        "mla_w_ukv": _dense(ks[25], (N_ODD, KV_LORA, MLA_HEADS * (NOPE_DIM + V_DIM)), KV_LORA),
        "mla_w_o": _dense(ks[26], (N_ODD, MLA_HEADS * V_DIM, D), MLA_HEADS * V_DIM),
        "final_norm": 1.0 + _normal(ks[27], (D,), 0.02),
    }


def reference(x, c, ctx, c_ctx, mod_w, mod_b, norm_mix, norm_mlp, mlp_w1, mlp_w2,
              ab_w_in, ab_w_out, na_rpb, lru_conv_w, lru_conv_b, lru_wa, lru_ba, lru_wx, lru_bx,
              lru_lambda, mla_w_dq, mla_w_dkv, mla_q_norm, mla_w_uq, mla_kv_norm, mla_w_ukv,
              mla_w_o, final_norm):
    z = ctx
    cond_lat = jax.nn.silu(c)
    cond_ctx = jax.nn.silu(c_ctx)
    for layer in range(DEPTH):
        last = layer == DEPTH - 1
        want_ctx = not last
        j = layer // 2
        m_l = jnp.split((cond_lat @ mod_w[layer] + mod_b[layer])[:, None, :], N_MOD, axis=-1)
        m_c = jnp.split(cond_ctx @ mod_w[layer] + mod_b[layer], N_MOD, axis=-1)
        h_lat = modulate(rms_norm(x, norm_mix[layer]), m_l[0], m_l[1])
        h_ctx = modulate(rms_norm(z, norm_mix[layer]), m_c[0], m_c[1])
        if layer % 2 == 0:
            y_lat, y_ctx = na_rglru_mixer(
                h_lat, h_ctx, ab_w_in[j], ab_w_out[j], na_rpb[j], lru_conv_w[j], lru_conv_b[j],
                lru_wa[j], lru_ba[j], lru_wx[j], lru_bx[j], lru_lambda[j], want_ctx)
        else:
            y_lat, y_ctx = mla_mixer(
                h_lat, h_ctx, mla_w_dq[j], mla_w_dkv[j], mla_q_norm[j], mla_w_uq[j],
                mla_kv_norm[j], mla_w_ukv[j], mla_w_o[j], want_ctx)
        x = x + m_l[2] * y_lat
        x = x + m_l[5] * squared_relu_mlp(
            modulate(rms_norm(x, norm_mlp[layer]), m_l[3], m_l[4]), mlp_w1[layer], mlp_w2[layer])
        if want_ctx:
            z = z + m_c[2] * y_ctx
            z = z + m_c[5] * squared_relu_mlp(
                modulate(rms_norm(z, norm_mlp[layer]), m_c[3], m_c[4]), mlp_w1[layer], mlp_w2[layer])
    return rms_norm(x, final_norm)
```

```python
import numpy as np
from contextlib import ExitStack
import concourse.bass as bass
import concourse.mybir as mybir
from concourse.bass_utils import run_bass_kernel_spmd
from concourse.alu_op_type import AluOpType as ALU

F32 = mybir.dt.float32
BF16 = mybir.dt.bfloat16
AF = mybir.ActivationFunctionType

D = 2048
S = 2048
C = 256
T = S + C
DFF = 8192
EPS = 1e-6
NCORES = 8

SELF_SYNC = True
ENGS = ["pe", "act", "dve", "pool", "sp"]
BLK = {"pe": "tensor", "act": "scalar", "dve": "vector", "pool": "gpsimd", "sp": "sync"}
NDMA = {"sp": 12, "pool": 4, "act": 2}


class Tl:
    __slots__ = ("t", "lw", "rd", "rd_dma")

    def __init__(self, t):
        self.t = t
        self.lw = None
        self.rd = {}
        self.rd_dma = []

    def __getitem__(self, i):
        return self.t[i]


class Op:
    __slots__ = ("eng", "fn", "deps", "inc", "sem", "ticket", "prev", "dma")

    def __init__(self, eng, fn, dma):
        self.eng = eng
        self.fn = fn
        self.dma = dma
        self.inc = dma
        self.deps = ()
        self.sem = None
        self.ticket = 0
        self.prev = 0


class Prog:
    def __init__(self, nc):
        self.nc = nc
        self.sems = {e: nc.alloc_semaphore("s_" + e) for e in ["pe", "act", "dve", "pool"]}
        self.cnt = {e: 0 for e in self.sems}
        self.dsem = {e: [nc.alloc_semaphore("d_%s%d" % (e, i)) for i in range(n)] for e, n in NDMA.items()}
        self.dcnt = {e: [0] * n for e, n in NDMA.items()}
        self.drr = {e: 0 for e in NDMA}
        self.known = {e: {} for e in ENGS}
        self.ops = {e: [] for e in ENGS}
        self.stack = None
        self.nphase = 0

    def sb(self, name, shape, dt):
        return Tl(self.stack.enter_context(self.nc.sbuf_tensor("%s_%d" % (name, self.nphase), shape, dt)))

    def ps(self, name, shape, dt=F32):
        return Tl(self.stack.enter_context(self.nc.psum_tensor("P%s_%d" % (name, self.nphase), shape, dt)))

    def op(self, eng, fn, r=(), w=(), dma=False):
        o = Op(eng, fn, dma)
        deps = []
        for t in r:
            if t.lw is not None:
                deps.append(t.lw)
        for t in w:
            if t.lw is not None:
                deps.append(t.lw)
            deps.extend(t.rd.values())
            deps.extend(t.rd_dma)
        out = []
        seen = set()
        for d in deps:
            if d is o or id(d) in seen:
                continue
            seen.add(id(d))
            if d.eng == eng and not d.dma and not dma and (eng == "pe" or not SELF_SYNC):
                continue
            out.append(d)
            d.inc = True
        o.deps = out
        for t in w:
            t.lw = o
            t.rd = {}
            t.rd_dma = []
        for t in r:
            if dma:
                t.rd_dma.append(o)
            else:
                t.rd[eng] = o
        self.ops[eng].append(o)
        return o

    def dma(self, q, out, in_, r=(), w=()):
        return self.op(q, lambda e: e.dma_start(out=out, in_=in_), r=r, w=w, dma=True)

    def phase(self):
        return _Phase(self)

    def _wait(self, e, known, sem, val):
        if val <= 0:
            return
        k = id(sem)
        if known.get(k, 0) >= val:
            return
        e.wait_ge(sem, val)
        known[k] = val

    def flush(self):
        for eng in ENGS:
            for o in self.ops[eng]:
                if o.dma:
                    n = len(self.dsem[eng])
                    k = self.drr[eng] % n
                    self.drr[eng] += 1
                    o.sem = self.dsem[eng][k]
                    o.prev = self.dcnt[eng][k]
                    self.dcnt[eng][k] += 16
                    o.ticket = self.dcnt[eng][k]
                elif o.inc:
                    self.cnt[eng] += 1
                    o.sem = self.sems[eng]
                    o.ticket = self.cnt[eng]
        with self.nc.Block() as block:
            for eng in ENGS:
                ops = self.ops[eng]
                if not ops:
                    continue

                def body(e, ops=ops, eng=eng):
                    known = self.known[eng]
                    for o in ops:
                        for d in o.deps:
                            self._wait(e, known, d.sem, d.ticket)
                        if o.dma:
                            self._wait(e, known, o.sem, o.prev)
                        ins = o.fn(e)
                        if o.dma:
                            ins.then_inc(o.sem, 16)
                        elif o.inc:
                            ins.then_inc(o.sem, 1)
                    if eng in self.dsem:
                        for k, s in enumerate(self.dsem[eng]):
                            self._wait(e, known, s, self.dcnt[eng][k])

                getattr(block, BLK[eng])(body)
        self.ops = {e: [] for e in ENGS}
        self.nphase += 1


class _Phase:
    def __init__(self, P):
        self.P = P

    def __enter__(self):
        self.P.stack = ExitStack()
        self.P.stack.__enter__()
        return self.P

    def __exit__(self, et, ev, tb):
        if et is None:
            self.P.flush()
        self.P.stack.__exit__(et, ev, tb)
        self.P.stack = None
        return False


def tok_blocks(total, bs):
    out = []
    t = 0
    while t < total:
        lim = S if t < S else total
        n = min(bs, lim - t)
        out.append((t, n))
        t += n
    return out


class G:
    pass


def mm_group(P, ps_ap, pairs, r, w):
    n = len(pairs)

    def fn(e):
        ins = None
        for i, (l, rr) in enumerate(pairs):
            ins = e.matmul(ps_ap, l, rr, start=(i == 0), stop=(i == n - 1))
        return ins

    return P.op("pe", fn, r=r, w=w)


def linear_B(P, W_ap, KC, act, tblocks, col0, ncols, CG, wbufs, psums, evac, m_done=None, ctr=None, wsrc=None, wq="pool", tick=None):
    if ctr is None:
        ctr = [0, 0]
    ng = (ncols + CG - 1) // CG
    for g in range(ng):
        c0 = col0 + g * CG
        cg = min(CG, col0 + ncols - c0)
        wt = wbufs[ctr[0] % len(wbufs)]
        ctr[0] += 1
        if wsrc is not None:
            src = wsrc(g)
        else:
            src = W_ap[0:KC * 128, c0:c0 + cg].rearrange("(kc p) m -> p kc m", p=128)
        P.dma(wq, wt[:, 0:KC, 0:cg], src, w=[wt])
        for ml in range(cg // 128):
            mi = (c0 - col0) // 128 + ml
            for (t0, tn) in tblocks:
                ps = psums[ctr[1] % len(psums)]
                ctr[1] += 1
                pairs = [(wt[:, kc, ml * 128:(ml + 1) * 128], act[:, kc, t0:t0 + tn]) for kc in range(KC)]
                mm_group(P, ps[:, 0:tn], pairs, r=[wt, act], w=[ps])
                evac(mi, t0, tn, ps)
            if m_done is not None:
                m_done(mi)
            if tick is not None:
                tick()
    return ctr


def linear_A(P, W_ap, KC, act, ttiles, col0, ncols, wbufs, psums, evac, ctr=None, tick=None):
    if ctr is None:
        ctr = [0, 0]
    ng = ncols // 512
    for g in range(ng):
        c0 = col0 + g * 512
        wt = wbufs[ctr[0] % len(wbufs)]
        ctr[0] += 1
        src = W_ap[0:KC * 128, c0:c0 + 512].rearrange("(kc p) m -> p kc m", p=128)
        P.dma("pool", wt[:, 0:KC, 0:512], src, w=[wt])
        for (t0, tn) in ttiles:
            ps = psums[ctr[1] % len(psums)]
            ctr[1] += 1
            pairs = [(act[:, kc, t0:t0 + tn], wt[:, kc, 0:512]) for kc in range(KC)]
            mm_group(P, ps[0:tn, 0:512], pairs, r=[wt, act], w=[ps])
            evac(g, t0, tn, ps)
            if tick is not None:
                tick()
    return ctr


def phase_consts(P, g):
    with P.phase():
        P.dma("sp", g.identF[:], g.d_identF, w=[g.t_identF])
        P.dma("pool", g.identB[:], g.d_identF, w=[g.t_identB])
        P.dma("sp", g.vecs[:], g.d_vecs, w=[g.t_vecs])
        P.dma("sp", g.lvecs[:], g.d_lvecs, w=[g.t_lvecs])
        P.op("dve", lambda e: e.memset(g.onesB[:], 1.0), w=[g.t_onesB])


def phase_transpose_in(P, g):
    with P.phase():
        xin = [P.sb("xin%d" % i, [128, 4, D], F32) for i in range(2)]
        xs = [P.sb("xs%d" % i, [128, 16, 512], F32) for i in range(2)]
        pss = [P.ps("tp%d" % i, [128, 512]) for i in range(4)]
        groups = [(g.d_x, tg * 512, tg * 512, 4) for tg in range(4)] + [(g.d_ctx, 0, S, 2)]
        pc = 0
        for gi, (src, s0, t0, nt) in enumerate(groups):
            xi = xin[gi % 2]
            xo = xs[gi % 2]
            P.dma("sp", xi[:, 0:nt, :], src[s0:s0 + nt * 128, :].rearrange("(j p) d -> p j d", p=128), w=[xi])
            for c in range(16):
                ps = pss[pc % 4]
                pc += 1

                def fn(e, ps=ps, xi=xi, c=c, nt=nt):
                    ins = None
                    for j in range(nt):
                        ins = e.transpose(ps[:, j * 128:(j + 1) * 128], xi[:, j, c * 128:(c + 1) * 128], g.identF[:])
                    return ins

                P.op("pe", fn, r=[xi], w=[ps])
                if c % 2 == 0:
                    P.op("act", lambda e, ps=ps, xo=xo, c=c, nt=nt: e.copy(out=xo[:, c, 0:nt * 128], in_=ps[:, 0:nt * 128]),
                         r=[ps], w=[xo])
                else:
                    P.op("dve", lambda e, ps=ps, xo=xo, c=c, nt=nt: e.tensor_copy(out=xo[:, c, 0:nt * 128], in_=ps[:, 0:nt * 128]),
                         r=[ps], w=[xo])
            P.dma("pool", g.xT[:, t0:t0 + nt * 128].rearrange("(c p) t -> p c t", p=128), xo[:, :, 0:nt * 128], r=[xo])


def mod_gen(P, g, slots):
    scond = P.sb("scond", [128, 16, 2], BF16)
    wb = [P.sb("mw%d" % i, [128, 16, 512], BF16) for i in range(2)]
    mg = [P.sb("mg%d" % i, [2, 512], F32) for i in range(2)]
    pacc = [P.ps("macc%d" % i, [128, 512]) for i in range(2)]
    mtp = [P.ps("mtp%d" % i, [128, 16, 2]) for i in range(1)]
    for j in range(2):
        P.op("act", lambda e, j=j: e.activation(out=scond[:, :, j], in_=g.vecs[:, g.V_CV + j * 16:g.V_CV + (j + 1) * 16],
                                                 func=AF.Silu), w=[scond])
    gi = 0
    for (L, k) in slots:
        mt = mtp[0]
        for q in range(4):
            wt = wb[gi % 2]
            m = mg[gi % 2]
            ps = pacc[gi % 2]
            gi += 1
            c0 = k * 2048 + q * 512
            P.dma("pool", wt[:], g.d_mod_w[L, :, c0:c0 + 512].rearrange("(kc p) m -> p kc m", p=128), w=[wt])
            pairs = [(scond[:, kc, :], wt[:, kc, :]) for kc in range(16)]
            mm_group(P, ps[0:2, :], pairs, r=[wt, scond], w=[ps])
            P.op("act", lambda e, ps=ps, m=m: e.copy(out=m[0:2, :], in_=ps[0:2, :]), r=[ps], w=[m])

            def fn(e, m=m, q=q, mt=mt):
                ins = None
                for cc in range(4):
                    ins = e.matmul(mt[:, q * 4 + cc, :], m[0:2, cc * 128:(cc + 1) * 128], g.identF[0:2, 0:2], start=True, stop=True)
                return ins
            P.op("pe", fn, r=[m], w=[mt])
            yield
        for j in range(2):
            P.op("dve", lambda e, L=L, k=k, j=j, mt=mt: e.tensor_tensor(
                out=g.MOD[:, L, k, j, :], in0=mt[:, :, j],
                in1=g.vecs[:, g.V_MODB + L * 96 + k * 16:g.V_MODB + L * 96 + (k + 1) * 16], op=ALU.add), r=[mt], w=[g.t_MOD])
        if k in (1, 4):
            voff = g.V_NMIX if k == 1 else g.V_NMLP
            for j in range(2):
                P.op("dve", lambda e, L=L, j=j, k=k, voff=voff: e.scalar_tensor_tensor(
                    out=g.MOD[:, L, k, j, :], in0=g.MOD[:, L, k, j, :], scalar=1.0,
                    in1=g.vecs[:, voff + L * 16:voff + (L + 1) * 16], op0=ALU.add, op1=ALU.mult),
                    r=[g.t_MOD], w=[g.t_MOD])
        yield


def phase_modulation(P, g, slots):
    with P.phase():
        for _ in mod_gen(P, g, slots):
            pass


def phase_norm(P, g, src, dst, nC, tblocks, A_fn, B_fn):
    Dn = nC * 128
    with P.phase():
        xb = [P.sb("nx%d" % i, [128, nC, 512], F32) for i in range(3)]
        hb = [P.sb("nh%d" % i, [128, nC, 512], BF16) for i in range(2)]
        sq = [P.sb("nsq%d" % i, [128, 512], BF16) for i in range(3)]
        tmp = [P.sb("ntmp%d" % i, [128, 512], F32) for i in range(3)]
        rs = [P.sb("nrs%d" % i, [128, 512], F32) for i in range(2)]
        rstd = [P.sb("nrstd%d" % i, [128, 512], F32) for i in range(2)]
        ssp = [P.ps("nss%d" % i, [128, 512]) for i in range(2)]

        def stepA(bi, t0, tn):
            x_ = xb[bi % 3]
            ss = ssp[bi % 2]
            P.dma("sp", x_[:, :, 0:tn], src[0:Dn, t0:t0 + tn].rearrange("(c p) t -> p c t", p=128), w=[x_])
            for c in range(nC):
                s_ = sq[c % 3]
                P.op("act", lambda e, s_=s_, x_=x_, c=c, tn=tn: e.activation(out=s_[:, 0:tn], in_=x_[:, c, 0:tn], func=AF.Square),
                     r=[x_], w=[s_])
                P.op("pe", lambda e, ss=ss, s_=s_, c=c, tn=tn: e.matmul(ss[:, 0:tn], g.onesB[:], s_[:, 0:tn], start=(c == 0), stop=(c == nC - 1)),
                     r=[s_], w=[ss])
            r_ = rs[bi % 2]
            rd_ = rstd[bi % 2]
            P.op("act", lambda e, r_=r_, ss=ss, tn=tn: e.activation(out=r_[:, 0:tn], in_=ss[:, 0:tn], func=AF.Sqrt,
                                                                      bias=g.vecs[:, g.V_EPS:g.V_EPS + 1], scale=1.0 / Dn),
                 r=[ss], w=[r_])
            P.op("dve", lambda e, r_=r_, rd_=rd_, tn=tn: e.reciprocal(out=rd_[:, 0:tn], in_=r_[:, 0:tn]), r=[r_], w=[rd_])

        def stepB(bi, t0, tn):
            j = 0 if t0 < S else 1
            x_ = xb[bi % 3]
            h_ = hb[bi % 2]
            rd_ = rstd[bi % 2]
            for c in range(nC):
                if B_fn is None:
                    P.op("dve", lambda e, h_=h_, x_=x_, rd_=rd_, c=c, tn=tn, j=j: e.scalar_tensor_tensor(
                        out=h_[:, c, 0:tn], in0=x_[:, c, 0:tn], scalar=A_fn(c, j), in1=rd_[:, 0:tn],
                        op0=ALU.mult, op1=ALU.mult), r=[x_, rd_], w=[h_])
                    continue
                t_ = tmp[c % 3]
                P.op("dve", lambda e, t_=t_, x_=x_, rd_=rd_, c=c, tn=tn, j=j: e.scalar_tensor_tensor(
                    out=t_[:, 0:tn], in0=x_[:, c, 0:tn], scalar=A_fn(c, j), in1=rd_[:, 0:tn],
                    op0=ALU.mult, op1=ALU.mult), r=[x_, rd_], w=[t_])
                P.op("act", lambda e, t_=t_, h_=h_, c=c, tn=tn, j=j: e.activation(
                    out=h_[:, c, 0:tn], in_=t_[:, 0:tn], func=AF.Identity, bias=B_fn(c, j), scale=1.0),
                    r=[t_], w=[h_])
            P.dma("pool", dst[0:Dn, t0:t0 + tn].rearrange("(c p) t -> p c t", p=128), h_[:, :, 0:tn], r=[h_])

        nb = len(tblocks)
        for i in range(nb + 1):
            if i < nb:
                stepA(i, *tblocks[i])
            if i >= 1:
                stepB(i - 1, *tblocks[i - 1])


def norm_mod(P, g, L, kA, kB, ttot):
    phase_norm(P, g, g.xT, g.hT, 16, blocks(0, ttot, 512),
               lambda c, j: g.MOD[:, L, kA, j, c:c + 1], lambda c, j: g.MOD[:, L, kB, j, c:c + 1])


def blocks(t_start, t_end, bs):
    out = []
    t = t_start
    while t < t_end:
        lim = S if t < S else t_end
        lim = min(lim, t_end)
        n = min(bs, lim - t)
        out.append((t, n))
        t += n
    return out


def phase_inproj0(P, g):
    SC = 128.0 ** -0.5
    with P.phase():
        act = P.sb("h", [128, 16, T], BF16)
        wb = [P.sb("w%d" % i, [128, 16, 512], BF16) for i in range(2)]
        pss = [P.ps("ps%d" % i, [128, 512]) for i in range(4)]
        stb = [P.sb("stb%d" % i, [128, T], BF16) for i in range(2)]
        stf = [P.sb("stf%d" % i, [128, T], F32) for i in range(2)]
        vst = [P.sb("vst%d" % i, [128, 512], BF16) for i in range(3)]
        P.dma("sp", act[:], g.hT.rearrange("(c p) t -> p c t", p=128), w=[act])
        tb = blocks(0, T, 512)
        ctr = [0, 0]
        cnt = [0]
        mgen = mod_gen(P, g, [(0, 2), (0, 3), (0, 4), (0, 5)] + [(1, k) for k in range(6)])
        tk = [0]

        def tick():
            tk[0] += 1
            if tk[0] % 3 != 0:
                next(mgen, None)

        def mk(kind, dst, stl):
            def evac(mi, t0, tn, ps):
                st = stl[mi % 2]
                cnt[0] += 1
                if kind == "q":
                    P.op("act", lambda e: e.mul(out=st[:, t0:t0 + tn], in_=ps[:, 0:tn], mul=SC), r=[ps], w=[st])
                elif kind == "g":
                    P.op("act", lambda e: e.activation(out=st[:, t0:t0 + tn], in_=ps[:, 0:tn], func=AF.Gelu), r=[ps], w=[st])
                elif cnt[0] % 2 == 0:
                    P.op("act", lambda e: e.copy(out=st[:, t0:t0 + tn], in_=ps[:, 0:tn]), r=[ps], w=[st])
                else:
                    P.op("dve", lambda e: e.tensor_copy(out=st[:, t0:t0 + tn], in_=ps[:, 0:tn]), r=[ps], w=[st])

            def m_done(mi):
                st = stl[mi % 2]
                P.dma("sp", dst[mi * 128:(mi + 1) * 128, :], st[:, :], r=[st])
            return evac, m_done

        W = g.d_w_in
        for kind, col0, dst, stl in (("q", 0, g.qT, stb), ("k", 1024, g.kT, stb), ("u", 3072, g.uT, stf), ("g", 4096, g.ggT, stf)):
            ev, md = mk(kind, dst, stl)
            linear_B(P, W, 16, act, tb, col0, 1024, 512, wb, pss, ev, md, ctr, tick=tick)
        vc = [0]

        def evac_v(gi, t0, tn, ps):
            st = vst[vc[0] % 3]
            vc[0] += 1
            if vc[0] % 2 == 0:
                P.op("act", lambda e: e.copy(out=st[0:tn, :], in_=ps[0:tn, :]), r=[ps], w=[st])
            else:
                P.op("dve", lambda e: e.tensor_copy(out=st[0:tn, :], in_=ps[0:tn, :]), r=[ps], w=[st])
            P.dma("sp", g.v0[t0:t0 + tn, gi * 512:(gi + 1) * 512], st[0:tn, :], r=[st])
        linear_A(P, W, 16, act, [(i * 128, 128) for i in range(T // 128)], 2048, 1024, wb, pss, evac_v, ctr, tick=tick)
        for _ in mgen:
            pass


def na_chunks(r):
    rs = min(max(r - 4, 0), 24)
    if rs % 2 == 0:
        return [(rs + 2 * j, rs + 2 * j - r + 7) for j in range(4)]
    out = [(rs - 1, 14)]
    for j in range(3):
        kr0 = rs + 1 + 2 * j
        out.append((kr0, kr0 - r + 7))
    out.append((rs + 7, 15))
    return out


def na_body(P, g):
    bias = P.sb("nab", [128, 8, 16, 64], BF16)
    P.dma("pool", bias[:], g.d_nabias.rearrange("p (h d q) -> p h d q", h=8, d=16), w=[bias])
    qh = [P.sb("q%d" % i, [128, T], BF16) for i in range(2)]
    kh = [P.sb("k%d" % i, [128, T], BF16) for i in range(2)]
    vh = [P.sb("v%d" % i, [128, 18, 128], BF16) for i in range(2)]
    oh = [P.sb("o%d" % i, [128, T], BF16) for i in range(2)]
    sps = [P.ps("s%d" % i, [128, 512]) for i in range(3)]
    ops_ = [P.ps("o%d" % i, [128, 512]) for i in range(2)]
    pts = [P.sb("pt%d" % i, [128, 512], BF16) for i in range(3)]
    rdn = [P.sb("rd%d" % i, [128, 256], F32) for i in range(2)]
    A, B = [], []
    it = 0
    for h in range(8):
        q_, k_, v_, o_ = qh[h % 2], kh[h % 2], vh[h % 2], oh[h % 2]
        for r in range(33):
            sp, op_, pt, rd = sps[it % 3], ops_[it % 2], pts[it % 3], rdn[it % 2]
            it += 1
            if r < 32:
                chs = na_chunks(r)
                nw = len(chs)
                ncol = (nw + 2) * 64

                def stepA(h=h, r=r, chs=chs, nw=nw, ncol=ncol, sp=sp, pt=pt, q_=q_, k_=k_, v_=v_):
                    if r == 0:
                        P.dma("sp", q_[:], g.qT[h * 128:(h + 1) * 128, :], w=[q_])
                        P.dma("sp", k_[:], g.kT[h * 128:(h + 1) * 128, :], w=[k_])
                        P.dma("sp", v_[:], g.v0[:, h * 128:(h + 1) * 128].rearrange("(c p) d -> p c d", p=128), w=[v_])

                    def fs(e):
                        ins = None
                        qa = q_[:, r * 64:(r + 1) * 64]
                        for i, (kr0, bi) in enumerate(chs):
                            e.matmul(sp[:, i * 64:(i + 1) * 64], k_[:, kr0 * 64:kr0 * 64 + 128], qa, start=True, stop=False)
                            ins = e.matmul(sp[:, i * 64:(i + 1) * 64], g.identB[:], bias[:, h, bi, :], start=False, stop=True)
                        for j in range(2):
                            ins = e.matmul(sp[:, (nw + j) * 64:(nw + j + 1) * 64], k_[:, S + j * 128:S + (j + 1) * 128], qa,
                                           start=True, stop=True)
                        return ins
                    P.op("pe", fs, r=[q_, k_, bias], w=[sp])
                    P.op("act", lambda e: e.activation(out=pt[:, 0:ncol], in_=sp[:, 0:ncol], func=AF.Exp), r=[sp], w=[pt])

                def stepB(h=h, r=r, chs=chs, nw=nw, op_=op_, pt=pt, v_=v_, rd=rd, o_=o_):
                    def fo(e):
                        ins = None
                        cl = [kr0 // 2 for (kr0, _) in chs] + [16, 17]
                        n = len(cl)
                        for i, c in enumerate(cl):
                            e.matmul(op_[:, 0:64], v_[:, c, :], pt[:, i * 64:(i + 1) * 64], start=(i == 0), stop=(i == n - 1))
                        for i, c in enumerate(cl):
                            ins = e.matmul(op_[:, 64:128], g.onesB[:], pt[:, i * 64:(i + 1) * 64], start=(i == 0), stop=(i == n - 1))
                        return ins
                    P.op("pe", fo, r=[pt, v_], w=[op_])
                    P.op("dve", lambda e: e.reciprocal(out=rd[:, 0:64], in_=op_[:, 64:128]), r=[op_], w=[rd])
                    P.op("dve", lambda e: e.tensor_tensor(out=o_[:, r * 64:(r + 1) * 64], in0=op_[:, 0:64], in1=rd[:, 0:64], op=ALU.mult),
                         r=[op_, rd], w=[o_])
            else:
                def stepA(sp=sp, pt=pt, q_=q_, k_=k_):
                    def fsc(e):
                        ins = None
                        for j in range(2):
                            ins = e.matmul(sp[:, j * 256:(j + 1) * 256], k_[:, S + j * 128:S + (j + 1) * 128], q_[:, S:T], start=True, stop=True)
                        return ins
                    P.op("pe", fsc, r=[q_, k_], w=[sp])
                    P.op("act", lambda e: e.activation(out=pt[:, 0:512], in_=sp[:, 0:512], func=AF.Exp), r=[sp], w=[pt])

                def stepB(h=h, op_=op_, pt=pt, v_=v_, rd=rd, o_=o_):
                    def foc(e):
                        ins = None
                        for j in range(2):
                            e.matmul(op_[:, 0:256], v_[:, 16 + j, :], pt[:, j * 256:(j + 1) * 256], start=(j == 0), stop=(j == 1))
                        for j in range(2):
                            ins = e.matmul(op_[:, 256:512], g.onesB[:], pt[:, j * 256:(j + 1) * 256], start=(j == 0), stop=(j == 1))
                        return ins
                    P.op("pe", foc, r=[pt, v_], w=[op_])
                    P.op("dve", lambda e: e.reciprocal(out=rd[:, 0:256], in_=op_[:, 256:512]), r=[op_], w=[rd])
                    P.op("dve", lambda e: e.tensor_tensor(out=o_[:, S:T], in0=op_[:, 0:256], in1=rd[:, 0:256], op=ALU.mult),
                         r=[op_, rd], w=[o_])
                    P.dma("sp", g.mixT[h * 128:(h + 1) * 128, :], o_[:], r=[o_])
            A.append(stepA)
            B.append(stepB)
    n = len(A)
    for i in range(n + 1):
        if i < n:
            A[i]()
        if i >= 1:
            B[i - 1]()
        yield


def phase_na_lru(P, g):
    with P.phase():
        ga = na_body(P, g)
        gb = lru_body(P, g)
        gc = convert_gen(P, g, 0)
        alive_a = alive_b = alive_c = True
        it = 0
        while alive_a or alive_b or alive_c:
            it += 1
            if alive_c and it % 2 == 0:
                try:
                    next(gc)
                except StopIteration:
                    alive_c = False
            if alive_a:
                try:
                    next(ga)
                except StopIteration:
                    alive_a = False
            for _ in range(2):
                if alive_b:
                    try:
                        next(gb)
                    except StopIteration:
                        alive_b = False


def make_nabias(rpb):
    NEG = np.float32(-30000.0)
    kc = np.arange(64)[:, None]
    qc = np.arange(64)[None, :]
    cs = np.clip(qc - 8, 0, 48)
    ok = (kc >= cs) & (kc < cs + 16)
    dc = np.clip(kc - qc, -15, 15) + 15
    out = np.full((2, 64, 8, 16, 64), NEG, np.float32)
    for h in range(8):
        def blk(dr):
            if dr < -7 or dr > 7:
                return np.full((64, 64), NEG, np.float32)
            return np.where(ok, rpb[h, dr + 7][dc], NEG).astype(np.float32)
        for d in range(14):
            for a in range(2):
                out[a, :, h, d, :] = blk(d - 7 + a)
        out[1, :, h, 14, :] = blk(-4)
        out[0, :, h, 15, :] = blk(3)
    return np.ascontiguousarray(out.reshape(128, 8 * 16 * 64))


def lru_body(P, g):
    wg = P.sb("wg", [128, 2, 2, 8, 128], BF16)
    P.dma("pool", wg[:, 0], g.d_lru_wa.rearrange("d n i j -> i d n j"), w=[wg])
    P.dma("pool", wg[:, 1], g.d_lru_wx.rearrange("d n i j -> i d n j"), w=[wg])
    lv = g.lvecs
    nsp = P.sb("nsp", [128, 16], F32)
    e1 = P.sb("e1", [128, 16], F32)
    hb_ = P.sb("hbias", [128, 32], F32)
    c25 = P.sb("c25", [128, 1], F32)
    P.op("act", lambda e: e.activation(out=e1[:], in_=lv[:, g.LV_LAM:g.LV_LAM + 16], func=AF.Exp, scale=-1.0), w=[e1])
    P.op("act", lambda e: e.activation(out=nsp[:], in_=e1[:], func=AF.Ln, bias=g.vecs[:, g.V_ONE:g.V_ONE + 1], scale=1.0),
         r=[e1], w=[nsp])
    P.op("dve", lambda e: e.tensor_scalar(out=nsp[:], in0=nsp[:], scalar1=-4.0, scalar2=None, op0=ALU.mult), r=[nsp], w=[nsp])
    P.op("dve", lambda e: e.tensor_scalar(out=hb_[:], in0=lv[:, g.LV_BA:g.LV_BA + 32], scalar1=0.5, scalar2=None, op0=ALU.mult), w=[hb_])
    P.op("dve", lambda e: e.memset(c25[:], 0.25), w=[c25])
    yield

    def f32t(name, n=2):
        return [P.sb("%s%d" % (name, i), [128, T], F32) for i in range(n)]
    u_, gg_, xc_ = f32t("u", 2), f32t("gg", 1), f32t("xc", 1)
    xcb_ = [P.sb("xcb%d" % i, [128, T], BF16) for i in range(1)]
    gr_, gi_, a_ = f32t("gr", 2), f32t("gi", 2), f32t("a", 2)
    hd_ = f32t("hd", 2)
    y_ = [P.sb("y%d" % i, [128, T], BF16) for i in range(2)]
    pss = [P.ps("lps%d" % i, [128, 512]) for i in range(3)]
    tb = blocks(0, T, 512)
    pc = 0

    def rev(ap_t, lo, hi):
        a = ap_t[:, lo:hi]
        return bass.AP(a.tensor, a.offset + (hi - lo - 1), [list(a.ap[0]), [-1, hi - lo]])

    P.dma("sp", u_[0][:], g.uT[0:128, :], w=[u_[0]])
    for n in range(8):
        u, gg, xc, xcb, y = u_[n % 2], gg_[0], xc_[0], xcb_[0], y_[n % 2]
        if n + 1 < 8:
            P.dma("sp", u_[(n + 1) % 2][:], g.uT[(n + 1) * 128:(n + 2) * 128, :], w=[u_[(n + 1) % 2]])
        cw = lambda tap, n=n: lv[:, g.LV_CW + tap * 8 + n:g.LV_CW + tap * 8 + n + 1]
        P.op("act", lambda e, xc=xc, u=u, n=n, cw=cw: e.activation(out=xc[:], in_=u[:], func=AF.Identity,
                                                                    bias=lv[:, g.LV_CB + n:g.LV_CB + n + 1], scale=cw(2)),
             r=[u], w=[xc])
        yield
        for tap in (0, 1, 3):
            off = tap - 2
            for (s0, s1) in ((0, S), (S, T)):
                d0, d1 = max(s0, s0 - off), min(s1, s1 - off)
                P.op("dve", lambda e, xc=xc, u=u, tap=tap, off=off, d0=d0, d1=d1, cw=cw: e.scalar_tensor_tensor(
                    out=xc[:, d0:d1], in0=u[:, d0 + off:d1 + off], scalar=cw(tap), in1=xc[:, d0:d1],
                    op0=ALU.mult, op1=ALU.add), r=[u, xc], w=[xc])
            yield
        P.op("pool", lambda e, xc=xc, xcb=xcb: e.tensor_copy(out=xcb[:], in_=xc[:]), r=[xc], w=[xcb])
        yield
        for d in range(2):
            hd, gr, gi, a = hd_[d], gr_[d], gi_[d], a_[d]
            for gt, dst, boff in ((0, gr, 0), (1, gi, 16)):
                for (t0, tn) in tb:
                    ps = pss[pc % 3]
                    pc += 1
                    P.op("pe", lambda e, ps=ps, gt=gt, d=d, n=n, xcb=xcb, t0=t0, tn=tn: e.matmul(
                        ps[:, 0:tn], wg[:, gt, d, n, :], xcb[:, t0:t0 + tn], start=True, stop=True), r=[wg, xcb], w=[ps])
                    P.op("act", lambda e, ps=ps, dst=dst, boff=boff, d=d, n=n, t0=t0, tn=tn: e.activation(
                        out=dst[:, t0:t0 + tn], in_=ps[:, 0:tn], func=AF.Tanh,
                        bias=hb_[:, boff + d * 8 + n:boff + d * 8 + n + 1], scale=0.5), r=[ps, hb_], w=[dst])
                    yield
            P.op("act", lambda e, a=a, gr=gr, d=d, n=n: e.activation(out=a[:], in_=gr[:], func=AF.Exp,
                                                                      bias=nsp[:, d * 8 + n:d * 8 + n + 1],
                                                                      scale=nsp[:, d * 8 + n:d * 8 + n + 1]), r=[gr, nsp], w=[a])
            P.op("dve", lambda e, gi=gi, xc=xc: e.scalar_tensor_tensor(out=gi[:], in0=gi[:], scalar=1.0, in1=xc[:],
                                                                       op0=ALU.add, op1=ALU.mult), r=[gi, xc], w=[gi])
            yield
            P.op("pool", lambda e, a=a, gr=gr: e.tensor_tensor(out=gr[:], in0=a[:], in1=a[:], op=ALU.mult), r=[a, gr], w=[gr])
            yield
            P.op("act", lambda e, gr=gr: e.activation(out=gr[:], in_=gr[:], func=AF.Sqrt, bias=c25[:, 0:1], scale=-0.25),
                 r=[gr, c25], w=[gr])
            yield
            P.op("pool", lambda e, gr=gr, gi=gi: e.tensor_tensor(out=gi[:], in0=gi[:], in1=gr[:], op=ALU.mult),
                 r=[gr, gi], w=[gi])
            yield
            vv = gi
            if d == 0:
                P.op("dve", lambda e, hd=hd, a=a, vv=vv: e.tensor_tensor_scan(
                    out=hd[:, S:T], data0=a[:, S:T], data1=vv[:, S:T], initial=0.0, op0=ALU.mult, op1=ALU.add),
                    r=[a, vv], w=[hd])
                P.op("dve", lambda e, hd=hd, a=a, vv=vv: e.tensor_tensor_scan(
                    out=hd[:, 0:S], data0=a[:, 0:S], data1=vv[:, 0:S], initial=hd[:, T - 1:T], op0=ALU.mult, op1=ALU.add),
                    r=[a, vv, hd], w=[hd])
            else:
                P.op("dve", lambda e, hd=hd, a=a, vv=vv: e.tensor_tensor_scan(
                    out=rev(hd, S, T), data0=rev(a, S, T), data1=rev(vv, S, T), initial=0.0, op0=ALU.mult, op1=ALU.add),
                    r=[a, vv], w=[hd])
                P.op("dve", lambda e, hd=hd, a=a, vv=vv: e.tensor_tensor_scan(
                    out=rev(hd, 0, S), data0=rev(a, 0, S), data1=rev(vv, 0, S), initial=hd[:, S:S + 1], op0=ALU.mult, op1=ALU.add),
                    r=[a, vv, hd], w=[hd])
            yield
        P.dma("sp", gg[:], g.ggT[n * 128:(n + 1) * 128, :], w=[gg])
        P.op("dve", lambda e: e.tensor_tensor(out=hd_[0][:], in0=hd_[0][:], in1=hd_[1][:], op=ALU.add),
             r=[hd_[0], hd_[1]], w=[hd_[0]])
        yield
        P.op("pool", lambda e, y=y, gg=gg: e.tensor_tensor(out=y[:], in0=hd_[0][:], in1=gg[:], op=ALU.mult),
             r=[hd_[0], gg], w=[y])
        P.dma("sp", g.mixT[1024 + n * 128:1024 + (n + 1) * 128, :], y[:], r=[y])
        yield


def phase_outproj(P, g, L, W_ap, src, ttot):
    with P.phase():
        act = P.sb("a", [128, 16, ttot], BF16)
        wb = [P.sb("w%d" % i, [128, 16, 512], BF16) for i in range(2)]
        pss = [P.ps("ps%d" % i, [128, 512]) for i in range(4)]
        xr = [P.sb("xr%d" % i, [128, ttot], F32) for i in range(2)]
        P.dma("sp", act[:], src[:, 0:ttot].rearrange("(c p) t -> p c t", p=128), w=[act])
        tb = blocks(0, ttot, 512)

        def evac(mi, t0, tn, ps):
            x_ = xr[mi % 2]
            if t0 == 0:
                P.dma("sp", x_[:], g.xT[mi * 128:(mi + 1) * 128, 0:ttot], w=[x_])
            j = 0 if t0 < S else 1
            P.op("dve", lambda e: e.scalar_tensor_tensor(out=x_[:, t0:t0 + tn], in0=ps[:, 0:tn], scalar=g.MOD[:, L, 2, j, mi:mi + 1],
                                                          in1=x_[:, t0:t0 + tn], op0=ALU.mult, op1=ALU.add), r=[ps, x_], w=[x_])

        def m_done(mi):
            x_ = xr[mi % 2]
            P.dma("sp", g.xT[mi * 128:(mi + 1) * 128, 0:ttot], x_[:], r=[x_])
        linear_B(P, W_ap, 16, act, tb, 0, D, 512, wb, pss, evac, m_done)


def phase_mlp(P, g, L, ttot):
    with P.phase():
        hb = [P.sb("h%d" % i, [128, 16, 768], BF16) for i in range(1)]
        aT = P.sb("aT", [128, 64, 768], BF16)
        w1b = [P.sb("w1_%d" % i, [128, 16, 256], BF16) for i in range(3)]
        w2b = [P.sb("w2_%d" % i, [128, 64, 128], BF16) for i in range(2)]
        xr = [P.sb("xr%d" % i, [128, 768], F32) for i in range(2)]
        rl = [P.sb("rl%d" % i, [128, 384], F32) for i in range(3)]
        pss = [P.ps("ps%d" % i, [128, 512]) for i in range(6)]
        passes = []
        t = 0
        while t < ttot:
            n = min(768, ttot - t)
            passes.append((t, n))
            t += n
        c1 = [0, 0]
        c2 = [0, 0]
        rc = [0]
        for pi, (p0, pn) in enumerate(passes):
            h_ = hb[0]
            P.dma("sp", h_[:, :, 0:pn], g.hT[:, p0:p0 + pn].rearrange("(c p) t -> p c t", p=128), w=[h_])
            tbl = [(t0 - p0, tn) for (t0, tn) in blocks(p0, p0 + pn, 384)]

            def evac1(mi, t0, tn, ps):
                r_ = rl[rc[0] % 3]
                rc[0] += 1
                P.op("act", lambda e: e.activation(out=r_[:, 0:tn], in_=ps[:, 0:tn], func=AF.Relu), r=[ps], w=[r_])
                P.op("dve", lambda e: e.tensor_tensor(out=aT[:, mi, t0:t0 + tn], in0=r_[:, 0:tn], in1=r_[:, 0:tn], op=ALU.mult),
                     r=[r_], w=[aT])
            c1[1] = c2[1] = max(c1[1], c2[1])
            linear_B(P, None, 16, h_, tbl, 0, DFF, 256, w1b, pss, evac1, None, c1, wsrc=lambda gi: g.W1r[L, gi])

            def evac2(mi, t0, tn, ps, p0=p0, pn=pn):
                x_ = xr[mi % 2]
                if t0 == 0:
                    P.dma("sp", x_[:, 0:pn], g.xT[mi * 128:(mi + 1) * 128, p0:p0 + pn], w=[x_])
                j = 0 if p0 + t0 < S else 1
                P.op("dve", lambda e: e.scalar_tensor_tensor(out=x_[:, t0:t0 + tn], in0=ps[:, 0:tn], scalar=g.MOD[:, L, 5, j, mi:mi + 1],
                                                              in1=x_[:, t0:t0 + tn], op0=ALU.mult, op1=ALU.add), r=[ps, x_], w=[x_])

            def m_done2(mi, p0=p0, pn=pn):
                x_ = xr[mi % 2]
                P.dma("sp", g.xT[mi * 128:(mi + 1) * 128, p0:p0 + pn], x_[:, 0:pn], r=[x_])
            c1[1] = c2[1] = max(c1[1], c2[1])
            linear_B(P, None, 64, aT, tbl, 0, D, 128, w2b, pss, evac2, m_done2, c2, wsrc=lambda gi: g.W2r[L, gi])


def convert_gen(P, g, L):
    for gi in range(32):
        P.dma("pool", g.W1r[L, gi], g.d_w1[L, :, gi * 256:(gi + 1) * 256].rearrange("(kc p) m -> p kc m", p=128))
        yield
    for mi in range(16):
        for q4 in range(4):
            P.dma("pool", g.W2r[L, mi, :, q4 * 16:(q4 + 1) * 16, :],
                  g.d_w2[L, q4 * 2048:(q4 + 1) * 2048, mi * 128:(mi + 1) * 128].rearrange("(kc p) m -> p kc m", p=128))
            yield


def convert_mlp_weights(P, g, L):
    for _ in convert_gen(P, g, L):
        pass


def phase_mla_down(P, g):
    with P.phase():
        act = P.sb("h", [128, 16, T], BF16)
        wb = [P.sb("w%d" % i, [128, 16, 512], BF16) for i in range(2)]
        wpe = P.sb("wpe", [128, 16, 2, 64], BF16)
        rope = P.sb("rope", [64, 2, S], F32)
        pss = [P.ps("ps%d" % i, [128, 512]) for i in range(6)]
        stf = [P.sb("stf%d" % i, [128, T], F32) for i in range(2)]
        t1 = [P.sb("t1_%d" % i, [64, 512], F32) for i in range(2)]
        t2 = [P.sb("t2_%d" % i, [64, 512], F32) for i in range(2)]
        kst = P.sb("kst", [64, T], BF16)
        P.dma("sp", act[:], g.hT.rearrange("(c p) t -> p c t", p=128), w=[act])
        P.dma("sp", rope[:], g.d_rope.rearrange("p (k t) -> p k t", k=2), w=[rope])
        P.dma("pool", wpe[:, :, 0, :], g.d_w_dkv[:, 512:576].rearrange("(kc p) m -> p kc m", p=128), w=[wpe])
        P.dma("pool", wpe[:, :, 1, :], g.d_w_dkv_sw.rearrange("(kc p) m -> p kc m", p=128), w=[wpe])
        ctr = [0, 0]
        cnt = [0]

        def mk(dst, ttot):
            def evac(mi, t0, tn, ps):
                st = stf[mi % 2]
                cnt[0] += 1
                if cnt[0] % 2 == 0:
                    P.op("act", lambda e: e.copy(out=st[:, t0:t0 + tn], in_=ps[:, 0:tn]), r=[ps], w=[st])
                else:
                    P.op("dve", lambda e: e.tensor_copy(out=st[:, t0:t0 + tn], in_=ps[:, 0:tn]), r=[ps], w=[st])

            def m_done(mi):
                st = stf[mi % 2]
                P.dma("sp", dst[mi * 128:(mi + 1) * 128, 0:ttot], st[:, 0:ttot], r=[st])
            return evac, m_done
        ev, md = mk(g.cqT, S)
        linear_B(P, g.d_w_dq, 16, act, blocks(0, S, 512), 0, 512, 512, wb, pss, ev, md, ctr)
        ev, md = mk(g.ckvT, T)
        linear_B(P, g.d_w_dkv, 16, act, blocks(0, T, 512), 0, 512, 512, wb, pss, ev, md, ctr)
        for bi, (t0, tn) in enumerate(blocks(0, T, 512)):
            psA = pss[ctr[1] % 6]
            ctr[1] += 1
            mm_group(P, psA[0:64, 0:tn], [(wpe[:, kc, 0, :], act[:, kc, t0:t0 + tn]) for kc in range(16)], r=[wpe, act], w=[psA])
            if t0 >= S:
                P.op("act", lambda e, psA=psA, t0=t0, tn=tn: e.copy(out=kst[:, t0:t0 + tn], in_=psA[0:64, 0:tn]), r=[psA], w=[kst])
                continue
            psB = pss[ctr[1] % 6]
            ctr[1] += 1
            mm_group(P, psB[0:64, 0:tn], [(wpe[:, kc, 1, :], act[:, kc, t0:t0 + tn]) for kc in range(16)], r=[wpe, act], w=[psB])
            a_, b_ = t1[bi % 2], t2[bi % 2]
            P.op("dve", lambda e, a_=a_, psA=psA, t0=t0, tn=tn: e.tensor_tensor(out=a_[:, 0:tn], in0=psA[0:64, 0:tn], in1=rope[:, 0, t0:t0 + tn],
                                                                               op=ALU.mult), r=[psA, rope], w=[a_])
            P.op("dve", lambda e, b_=b_, psB=psB, t0=t0, tn=tn: e.tensor_tensor(out=b_[:, 0:tn], in0=psB[0:64, 0:tn], in1=rope[:, 1, t0:t0 + tn],
                                                                               op=ALU.mult), r=[psB, rope], w=[b_])
            P.op("pool", lambda e, a_=a_, b_=b_, t0=t0, tn=tn: e.tensor_tensor(out=kst[:, t0:t0 + tn], in0=a_[:, 0:tn], in1=b_[:, 0:tn], op=ALU.add),
                 r=[a_, b_], w=[kst])
        P.dma("sp", g.kpeT[:, :], kst[:], r=[kst])


def phase_mla_up(P, g):
    SC = 192.0 ** -0.5
    with P.phase():
        aq = P.sb("aq", [128, 4, S], BF16)
        akv = P.sb("akv", [128, 4, T], BF16)
        wq = P.sb("wq", [128, 4, 3072], BF16)
        wqs = P.sb("wqs", [128, 4, 1024], BF16)
        wkv = P.sb("wkv", [128, 4, 4096], BF16)
        rope = P.sb("rope", [64, 2, S], F32)
        pss = [P.ps("ps%d" % i, [128, 512]) for i in range(6)]
        qn_st = [P.sb("qn%d" % i, [128, S], BF16) for i in range(2)]
        qr_st = [P.sb("qr%d" % i, [64, S], BF16) for i in range(2)]
        kn_st = [P.sb("kn%d" % i, [128, T], BF16) for i in range(2)]
        vst = [P.sb("vst%d" % i, [128, 512], BF16) for i in range(3)]
        t1 = [P.sb("t1_%d" % i, [64, 512], F32) for i in range(2)]
        t2 = [P.sb("t2_%d" % i, [64, 512], F32) for i in range(2)]
        P.dma("sp", aq[:], g.cqnT.rearrange("(c p) t -> p c t", p=128), w=[aq])
        P.dma("sp", akv[:], g.ckvnT.rearrange("(c p) t -> p c t", p=128), w=[akv])
        P.dma("sp", rope[:], g.d_rope.rearrange("p (k t) -> p k t", k=2), w=[rope])
        P.dma("pool", wq[:], g.d_w_uq.rearrange("(kc p) m -> p kc m", p=128), w=[wq])
        P.dma("pool", wqs[:], g.d_w_uq_sw.rearrange("(kc p) m -> p kc m", p=128), w=[wqs])
        P.dma("pool", wkv[:], g.d_w_ukv.rearrange("(kc p) m -> p kc m", p=128), w=[wkv])
        P.op("dve", lambda e: e.tensor_scalar(out=rope[:], in0=rope[:], scalar1=SC, scalar2=None, op0=ALU.mult), r=[rope], w=[rope])
        pc = 0
        it = 0
        for h in range(16):
            qn, qr, kn = qn_st[h % 2], qr_st[h % 2], kn_st[h % 2]
            for (t0, tn) in blocks(0, S, 512):
                ps = pss[pc % 6]
                pc += 1
                mm_group(P, ps[:, 0:tn], [(wq[:, kc, h * 192:h * 192 + 128], aq[:, kc, t0:t0 + tn]) for kc in range(4)], r=[wq, aq], w=[ps])
                P.op("act", lambda e, ps=ps, qn=qn, t0=t0, tn=tn: e.mul(out=qn[:, t0:t0 + tn], in_=ps[:, 0:tn], mul=SC), r=[ps], w=[qn])
                psA = pss[pc % 6]
                pc += 1
                mm_group(P, psA[0:64, 0:tn], [(wq[:, kc, h * 192 + 128:h * 192 + 192], aq[:, kc, t0:t0 + tn]) for kc in range(4)],
                         r=[wq, aq], w=[psA])
                psB = pss[pc % 6]
                pc += 1
                mm_group(P, psB[0:64, 0:tn], [(wqs[:, kc, h * 64:(h + 1) * 64], aq[:, kc, t0:t0 + tn]) for kc in range(4)],
                         r=[wqs, aq], w=[psB])
                a_, b_ = t1[it % 2], t2[it % 2]
                it += 1
                P.op("dve", lambda e, a_=a_, psA=psA, t0=t0, tn=tn: e.tensor_tensor(out=a_[:, 0:tn], in0=psA[0:64, 0:tn], in1=rope[:, 0, t0:t0 + tn],
                                                                                   op=ALU.mult), r=[psA, rope], w=[a_])
                P.op("dve", lambda e, b_=b_, psB=psB, t0=t0, tn=tn: e.tensor_tensor(out=b_[:, 0:tn], in0=psB[0:64, 0:tn], in1=rope[:, 1, t0:t0 + tn],
                                                                                   op=ALU.mult), r=[psB, rope], w=[b_])
                P.op("pool", lambda e, a_=a_, b_=b_, qr=qr, t0=t0, tn=tn: e.tensor_tensor(out=qr[:, t0:t0 + tn], in0=a_[:, 0:tn], in1=b_[:, 0:tn], op=ALU.add),
                     r=[a_, b_], w=[qr])
            P.dma("sp", g.qnT[h * 128:(h + 1) * 128, :], qn[:], r=[qn])
            P.dma("sp", g.qrT[h * 64:(h + 1) * 64, :], qr[:], r=[qr])
            for bi, (t0, tn) in enumerate(blocks(0, T, 512)):
                ps = pss[pc % 6]
                pc += 1
                mm_group(P, ps[:, 0:tn], [(wkv[:, kc, h * 256:h * 256 + 128], akv[:, kc, t0:t0 + tn]) for kc in range(4)], r=[wkv, akv], w=[ps])
                if bi % 2 == 0:
                    P.op("act", lambda e, ps=ps, kn=kn, t0=t0, tn=tn: e.copy(out=kn[:, t0:t0 + tn], in_=ps[:, 0:tn]), r=[ps], w=[kn])
                else:
                    P.op("dve", lambda e, ps=ps, kn=kn, t0=t0, tn=tn: e.tensor_copy(out=kn[:, t0:t0 + tn], in_=ps[:, 0:tn]), r=[ps], w=[kn])
            P.dma("sp", g.knT[h * 128:(h + 1) * 128, :], kn[:], r=[kn])
        vc = 0
        for hg in range(4):
            for tt in range(T // 128):
                ps = pss[pc % 6]
                pc += 1
                pairs = []
                for kc in range(4):
                    rhs = wkv[:, kc, hg * 1024:(hg + 1) * 1024].rearrange("p (h two d) -> p h two d", h=4, two=2)[:, :, 1, :]
                    pairs.append((akv[:, kc, tt * 128:(tt + 1) * 128], rhs))
                mm_group(P, ps[:, 0:512].rearrange("p (h d) -> p h d", h=4), pairs, r=[wkv, akv], w=[ps])
                st = vst[vc % 3]
                vc += 1
                if vc % 2 == 0:
                    P.op("act", lambda e, ps=ps, st=st: e.copy(out=st[:], in_=ps[:, 0:512]), r=[ps], w=[st])
                else:
                    P.op("dve", lambda e, ps=ps, st=st: e.tensor_copy(out=st[:], in_=ps[:, 0:512]), r=[ps], w=[st])
                P.dma("sp", g.v1[tt * 128:(tt + 1) * 128, hg * 512:(hg + 1) * 512], st[:], r=[st])


def phase_mla_attn(P, g):
    with P.phase():
        kpe = P.sb("kpe", [128, T], BF16)
        P.op("dve", lambda e: e.memset(kpe[64:128, :], 0.0), w=[kpe])
        P.dma("sp", kpe[0:64, :], g.kpeT[:, :], w=[kpe])
        convert_mlp_weights(P, g, 1)
        qn_ = [P.sb("qn%d" % i, [128, S], BF16) for i in range(2)]
        qr_ = [P.sb("qr%d" % i, [128, S], BF16) for i in range(2)]
        for t_ in qr_:
            P.op("dve", lambda e, t_=t_: e.memset(t_[64:128, :], 0.0), w=[t_])
        kn_ = [P.sb("kn%d" % i, [128, T], BF16) for i in range(2)]
        v_ = [P.sb("v%d" % i, [128, 18, 128], BF16) for i in range(2)]
        o_ = [P.sb("o%d" % i, [128, S], BF16) for i in range(2)]
        sps = [P.ps("s%d" % i, [128, 512]) for i in range(4)]
        ops_ = [P.ps("o%d" % i, [128, 512]) for i in range(2)]
        dps = [P.ps("d%d" % i, [128, 512]) for i in range(2)]
        pts = [P.sb("pt%d" % i, [128, 512], BF16) for i in range(4)]
        rdn = [P.sb("rd%d" % i, [128, 512], F32) for i in range(2)]
        A, B = [], []
        it = 0
        for h in range(16):
            qn, qr, kn, v, o = qn_[h % 2], qr_[h % 2], kn_[h % 2], v_[h % 2], o_[h % 2]
            for qb in range(4):
                op_, dp, rd = ops_[(h * 4 + qb) % 2], dps[(h * 4 + qb) % 2], rdn[(h * 4 + qb) % 2]
                for kc in range(18):
                    sp, pt = sps[it % 4], pts[it % 4]
                    it += 1

                    def stepA(h=h, qb=qb, kc=kc, sp=sp, pt=pt, qn=qn, qr=qr, kn=kn, v=v):
                        if qb == 0 and kc == 0:
                            P.dma("sp", qn[:], g.qnT[h * 128:(h + 1) * 128, :], w=[qn])
                            P.dma("sp", qr[0:64, :], g.qrT[h * 64:(h + 1) * 64, :], w=[qr])
                            P.dma("sp", kn[:], g.knT[h * 128:(h + 1) * 128, :], w=[kn])
                            P.dma("sp", v[:], g.v1[:, h * 128:(h + 1) * 128].rearrange("(c p) d -> p c d", p=128), w=[v])

                        def fs(e):
                            e.matmul(sp[:, :], kn[:, kc * 128:(kc + 1) * 128], qn[:, qb * 512:(qb + 1) * 512], start=True, stop=False)
                            return e.matmul(sp[:, :], kpe[:, kc * 128:(kc + 1) * 128], qr[:, qb * 512:(qb + 1) * 512], start=False, stop=True)
                        P.op("pe", fs, r=[kn, qn, kpe, qr], w=[sp])
                        P.op("act", lambda e: e.activation(out=pt[:], in_=sp[:], func=AF.Exp), r=[sp], w=[pt])

                    def stepB(h=h, qb=qb, kc=kc, pt=pt, v=v, o=o, op_=op_, dp=dp, rd=rd):
                        def fo(e):
                            e.matmul(op_[:], v[:, kc, :], pt[:], start=(kc == 0), stop=(kc == 17))
                            return e.matmul(dp[:], g.onesB[:], pt[:], start=(kc == 0), stop=(kc == 17))
                        P.op("pe", fo, r=[pt, v], w=[op_, dp])
                        if kc == 17:
                            P.op("dve", lambda e: e.reciprocal(out=rd[:], in_=dp[:]), r=[dp], w=[rd])
                            P.op("dve", lambda e: e.tensor_tensor(out=o[:, qb * 512:(qb + 1) * 512], in0=op_[:], in1=rd[:], op=ALU.mult),
                                 r=[op_, rd], w=[o])
                            if qb == 3:
                                P.dma("sp", g.attT[h * 128:(h + 1) * 128, :], o[:], r=[o])
                    A.append(stepA)
                    B.append(stepB)
        n = len(A)
        LAG = 2
        for i in range(n + LAG):
            if i < n:
                A[i]()
            if i >= LAG:
                B[i - LAG]()


def phase_final(P, g):
    with P.phase():
        xb = [P.sb("fx%d" % i, [128, 16, 512], F32) for i in range(2)]
        sq = [P.sb("fsq%d" % i, [128, 512], BF16) for i in range(2)]
        rs = [P.sb("frs%d" % i, [128, 512], F32) for i in range(2)]
        rstd = [P.sb("frstd%d" % i, [128, 512], F32) for i in range(2)]
        ot = [P.sb("fo%d" % i, [128, D], F32) for i in range(2)]
        ssp = [P.ps("fss%d" % i, [128, 512]) for i in range(2)]
        tps = [P.ps("ftp%d" % i, [128, 512]) for i in range(4)]
        pc = 0
        oc = 0
        for bi, (t0, tn) in enumerate(blocks(0, S, 512)):
            x_ = xb[bi % 2]
            ss = ssp[bi % 2]
            P.dma("sp", x_[:, :, 0:tn], g.xT[:, t0:t0 + tn].rearrange("(c p) t -> p c t", p=128), w=[x_])
            for c in range(16):
                s_ = sq[c % 2]
                P.op("act", lambda e, s_=s_, x_=x_, c=c, tn=tn: e.activation(out=s_[:, 0:tn], in_=x_[:, c, 0:tn], func=AF.Square),
                     r=[x_], w=[s_])
                P.op("pe", lambda e, ss=ss, s_=s_, c=c, tn=tn: e.matmul(ss[:, 0:tn], g.onesB[:], s_[:, 0:tn], start=(c == 0), stop=(c == 15)),
                     r=[s_], w=[ss])
            r_ = rs[bi % 2]
            rd_ = rstd[bi % 2]
            P.op("act", lambda e, r_=r_, ss=ss, tn=tn: e.activation(out=r_[:, 0:tn], in_=ss[:, 0:tn], func=AF.Sqrt,
                                                                      bias=g.vecs[:, g.V_EPS:g.V_EPS + 1], scale=1.0 / D),
                 r=[ss], w=[r_])
            P.op("dve", lambda e, r_=r_, rd_=rd_, tn=tn: e.reciprocal(out=rd_[:, 0:tn], in_=r_[:, 0:tn]), r=[r_], w=[rd_])
            for c in range(16):
                P.op("dve", lambda e, x_=x_, rd_=rd_, c=c, tn=tn: e.scalar_tensor_tensor(
                    out=x_[:, c, 0:tn], in0=x_[:, c, 0:tn], scalar=g.vecs[:, g.V_NFIN + c:g.V_NFIN + c + 1], in1=rd_[:, 0:tn],
                    op0=ALU.mult, op1=ALU.mult), r=[x_, rd_], w=[x_])
            for jt in range(tn // 128):
                o_ = ot[oc % 2]
                oc += 1
                for cg in range(4):
                    ps = tps[pc % 4]
                    pc += 1

                    def fn(e, ps=ps, x_=x_, jt=jt, cg=cg):
                        ins = None
                        for cc in range(4):
                            c = cg * 4 + cc
                            ins = e.transpose(ps[:, cc * 128:(cc + 1) * 128], x_[:, c, jt * 128:(jt + 1) * 128], g.identF[:])
                        return ins
                    P.op("pe", fn, r=[x_], w=[ps])
                    if cg % 2 == 0:
                        P.op("act", lambda e, ps=ps, o_=o_, cg=cg: e.copy(out=o_[:, cg * 512:(cg + 1) * 512], in_=ps[:, :]), r=[ps], w=[o_])
                    else:
                        P.op("dve", lambda e, ps=ps, o_=o_, cg=cg: e.tensor_copy(out=o_[:, cg * 512:(cg + 1) * 512], in_=ps[:, :]), r=[ps], w=[o_])
                P.dma("pool", g.d_out[t0 + jt * 128:t0 + (jt + 1) * 128, :], o_[:], r=[o_])


DEBUG_OUT = set()
STOP_AFTER = None


def build_program():
    nc = bass.Bass("TRN2", target_bir_lowering=False)
    g = G()

    def din(name, shape, dt=F32):
        return nc.dram_tensor(name, shape, dt, kind="ExternalInput").ap()

    def scratch(name, shape, dt):
        kind = "ExternalOutput" if name in DEBUG_OUT else "Internal"
        return nc.dram_tensor(name, shape, dt, kind=kind).ap()

    g.d_x = din("x", [S, D])
    g.d_ctx = din("ctx", [C, D])
    g.d_mod_w = din("mod_w", [2, D, 6 * D])
    g.d_identF = din("identF", [128, 128])
    g.d_w_in = din("ab_w_in", [D, 5120])
    g.d_w_out = din("ab_w_out", [D, D])
    g.d_w1 = din("mlp_w1", [2, D, DFF])
    g.d_w2 = din("mlp_w2", [2, DFF, D])
    g.d_nabias = din("nabias", [128, 8 * 16 * 64])
    g.d_lru_wa = din("lru_wa", [2, 8, 128, 128])
    g.d_lru_wx = din("lru_wx", [2, 8, 128, 128])
    g.d_w_dq = din("mla_w_dq", [D, 512])
    g.d_w_dkv = din("mla_w_dkv", [D, 576])
    g.d_w_uq = din("mla_w_uq", [512, 3072])
    g.d_w_ukv = din("mla_w_ukv", [512, 4096])
    g.d_w_o = din("mla_w_o", [D, D])
    g.d_rope = din("rope", [64, 2 * S])
    g.d_w_dkv_sw = din("w_dkv_sw", [D, 64])
    g.d_w_uq_sw = din("w_uq_sw", [512, 1024])
    off = 0
    for nm, n in (("V_CV", 32), ("V_MODB", 192), ("V_NMIX", 32), ("V_NMLP", 32), ("V_NFIN", 16), ("V_EPS", 1), ("V_ONE", 1),
                  ("V_QN", 4), ("V_KVN", 4)):
        setattr(g, nm, off)
        off += n
    g.NV = off
    g.d_vecs = din("vecs", [128, g.NV])
    off = 0
    for nm, n in (("LV_LAM", 16), ("LV_CW", 32), ("LV_CB", 8), ("LV_BA", 16), ("LV_BX", 16)):
        setattr(g, nm, off)
        off += n
    g.NLV = off
    g.d_lvecs = din("lvecs", [128, g.NLV])
    g.d_out = nc.dram_tensor("out", [S, D], F32, kind="ExternalOutput").ap()

    g.xT = scratch("xT", [D, T], F32)
    g.hT = scratch("hT", [D, T], BF16)
    g.qT = scratch("qT", [1024, T], BF16)
    g.kT = scratch("kT", [1024, T], BF16)
    g.v0 = scratch("v0", [T, 1024], BF16)
    g.uT = scratch("uT", [1024, T], F32)
    g.ggT = scratch("ggT", [1024, T], F32)
    g.mixT = scratch("mixT", [D, T], BF16)
    g.cqT = scratch("cqT", [512, S], F32)
    g.ckvT = scratch("ckvT", [640, T], F32)
    g.cqnT = scratch("cqnT", [512, S], BF16)
    g.ckvnT = scratch("ckvnT", [512, T], BF16)
    g.kpeT = scratch("kpeT", [64, T], BF16)
    g.qnT = scratch("qnT", [2048, S], BF16)
    g.qrT = scratch("qrT", [1024, S], BF16)
    g.knT = scratch("knT", [2048, T], BF16)
    g.v1 = scratch("v1", [T, 2048], BF16)
    g.attT = scratch("attT", [D, S], BF16)
    g.W1r = scratch("W1r", [2, 32, 128, 16, 256], BF16)
    g.W2r = scratch("W2r", [2, 16, 128, 64, 128], BF16)

    es = ExitStack()
    with es:
        def psb(name, shape, dt):
            t = es.enter_context(nc.sbuf_tensor(name, shape, dt))
            return t, Tl(t)
        g.identF, g.t_identF = psb("identF_sb", [128, 128], F32)
        g.identB, g.t_identB = psb("identB_sb", [128, 128], BF16)
        g.onesB, g.t_onesB = psb("onesB_sb", [128, 128], BF16)
        g.vecs, g.t_vecs = psb("vecs_sb", [128, g.NV], F32)
        g.lvecs, g.t_lvecs = psb("lvecs_sb", [128, g.NLV], F32)
        g.MOD, g.t_MOD = psb("MOD_sb", [128, 2, 6, 2, 16], F32)
        P = Prog(nc)
        steps = [
            ("consts", lambda: phase_consts(P, g)),
            ("tin", lambda: phase_transpose_in(P, g)),
            ("mod", lambda: phase_modulation(P, g, [(0, 0), (0, 1)])),
            ("norm0a", lambda: norm_mod(P, g, 0, 1, 0, T)),
            ("inproj0", lambda: phase_inproj0(P, g)),
            ("nalru", lambda: phase_na_lru(P, g)),
            ("outproj0", lambda: phase_outproj(P, g, 0, g.d_w_out, g.mixT, T)),
            ("norm0b", lambda: norm_mod(P, g, 0, 4, 3, T)),
            ("mlp0", lambda: phase_mlp(P, g, 0, T)),
            ("norm1a", lambda: norm_mod(P, g, 1, 1, 0, T)),
            ("mladown", lambda: phase_mla_down(P, g)),
            ("mlanq", lambda: phase_norm(P, g, g.cqT, g.cqnT, 4, blocks(0, S, 512), lambda c, j: g.vecs[:, g.V_QN + c:g.V_QN + c + 1], None)),
            ("mlankv", lambda: phase_norm(P, g, g.ckvT, g.ckvnT, 4, blocks(0, T, 512), lambda c, j: g.vecs[:, g.V_KVN + c:g.V_KVN + c + 1], None)),
            ("mlaup", lambda: phase_mla_up(P, g)),
            ("mlaattn", lambda: phase_mla_attn(P, g)),
            ("outproj1", lambda: phase_outproj(P, g, 1, g.d_w_o, g.attT, S)),
            ("norm1b", lambda: norm_mod(P, g, 1, 4, 3, S)),
            ("mlp1", lambda: phase_mlp(P, g, 1, S)),
            ("final", lambda: phase_final(P, g)),
        ]
        for name, fn in steps:
            fn()
            if STOP_AFTER == name:
                break
    return nc


def pack_vecs(inp, b):
    def fm(v):
        return np.ascontiguousarray(np.asarray(v, np.float32).reshape(-1, 128).T)
    cols = [fm(inp["c"][b]), fm(inp["c_ctx"])]
    for L in range(2):
        cols.append(fm(inp["mod_b"][L]))
    for L in range(2):
        cols.append(fm(inp["norm_mix"][L]))
    for L in range(2):
        cols.append(fm(inp["norm_mlp"][L]))
    cols.append(fm(inp["final_norm"]))
    cols.append(np.full((128, 1), EPS, np.float32))
    cols.append(np.ones((128, 1), np.float32))
    cols.append(fm(inp["mla_q_norm"][0]))
    cols.append(fm(inp["mla_kv_norm"][0]))
    return np.ascontiguousarray(np.concatenate(cols, axis=1))


def pack_lvecs(inp):
    def fm(v):
        return np.ascontiguousarray(np.asarray(v, np.float32).reshape(-1, 128).T)
    cols = [fm(inp["lru_lambda"][0].reshape(-1))]
    cols.append(fm(inp["lru_conv_w"][0].reshape(-1)))
    cols.append(fm(inp["lru_conv_b"][0]))
    cols.append(fm(inp["lru_ba"][0].reshape(-1)))
    cols.append(fm(inp["lru_bx"][0].reshape(-1)))
    return np.ascontiguousarray(np.concatenate(cols, axis=1))


def rope_table():
    pos = np.arange(S)
    row = (pos // 64).astype(np.float32)
    col = (pos % 64).astype(np.float32)
    inv = np.power(np.float32(10000.0), -np.arange(0, 32, 2, dtype=np.float32) / np.float32(32)).astype(np.float32)
    out = np.zeros((64, 2, S), np.float32)
    for axis, pv in enumerate((row, col)):
        ang = (pv[None, :] * inv[:, None]).astype(np.float32)
        for half in range(2):
            p0 = axis * 32 + half * 16
            out[p0:p0 + 16, 0, :] = np.cos(ang)
            out[p0:p0 + 16, 1, :] = -np.sin(ang) if half == 0 else np.sin(ang)
    return np.ascontiguousarray(out.reshape(64, 2 * S))


def make_in_maps(inp, cores):
    ident = np.eye(128, dtype=np.float32)
    nab = make_nabias(np.asarray(inp["na_rpb"][0], np.float32))
    lv = pack_lvecs(inp)
    rope = rope_table()
    perm = np.array([(p + 16) if (p % 32) < 16 else (p - 16) for p in range(64)])
    w_dkv_sw = np.ascontiguousarray(inp["mla_w_dkv"][0][:, 512 + perm])
    uq = inp["mla_w_uq"][0].reshape(512, 16, 192)
    w_uq_sw = np.ascontiguousarray(uq[:, :, 128 + perm].reshape(512, 1024))
    shared = {
        "w_dkv_sw": w_dkv_sw, "w_uq_sw": w_uq_sw,
        "mod_w": inp["mod_w"], "identF": ident, "ab_w_in": inp["ab_w_in"][0], "ab_w_out": inp["ab_w_out"][0],
        "mlp_w1": inp["mlp_w1"], "mlp_w2": inp["mlp_w2"], "nabias": nab, "lvecs": lv,
        "lru_wa": inp["lru_wa"][0], "lru_wx": inp["lru_wx"][0],
        "mla_w_dq": inp["mla_w_dq"][0], "mla_w_dkv": inp["mla_w_dkv"][0], "mla_w_uq": inp["mla_w_uq"][0],
        "mla_w_ukv": inp["mla_w_ukv"][0], "mla_w_o": inp["mla_w_o"][0], "rope": rope,
    }
    maps = []
    for b in cores:
        m = dict(shared)
        m["x"] = np.ascontiguousarray(inp["x"][b])
        m["ctx"] = np.ascontiguousarray(inp["ctx"][b])
        m["vecs"] = pack_vecs(inp, b)
        maps.append(m)
    return maps


def kernel(**inputs):
    inp = {k: np.asarray(v) for k, v in inputs.items()}
    nc = build_program()
    maps = make_in_maps(inp, list(range(NCORES)))
    res = run_bass_kernel_spmd(nc, maps, core_ids=list(range(NCORES)))
    out = np.stack([res.results[i]["out"] for i in range(NCORES)], axis=0)
    return out.astype(np.float32)
```

```python
import numpy as np
from contextlib import ExitStack
import concourse.bass as bass
import concourse.mybir as mybir
from concourse.bass_utils import run_bass_kernel_spmd
from concourse.alu_op_type import AluOpType as ALU

F32 = mybir.dt.float32
BF16 = mybir.dt.bfloat16
AF = mybir.ActivationFunctionType

D = 2048
S = 2048
C = 256
T = S + C
DFF = 8192
EPS = 1e-6
NCORES = 8

SELF_SYNC = True
ENGS = ["pe", "act", "dve", "pool", "sp"]
BLK = {"pe": "tensor", "act": "scalar", "dve": "vector", "pool": "gpsimd", "sp": "sync"}
NDMA = {"sp": 12, "pool": 4, "act": 2}


class Tl:
    __slots__ = ("t", "lw", "rd", "rd_dma")

    def __init__(self, t):
        self.t = t
        self.lw = None
        self.rd = {}
        self.rd_dma = []

    def __getitem__(self, i):
        return self.t[i]


class Op:
    __slots__ = ("eng", "fn", "deps", "inc", "sem", "ticket", "prev", "dma")

    def __init__(self, eng, fn, dma):
        self.eng = eng
        self.fn = fn
        self.dma = dma
        self.inc = dma
        self.deps = ()
        self.sem = None
        self.ticket = 0
        self.prev = 0


class Prog:
    def __init__(self, nc):
        self.nc = nc
        self.sems = {e: nc.alloc_semaphore("s_" + e) for e in ["pe", "act", "dve", "pool"]}
        self.cnt = {e: 0 for e in self.sems}
        self.dsem = {e: [nc.alloc_semaphore("d_%s%d" % (e, i)) for i in range(n)] for e, n in NDMA.items()}
        self.dcnt = {e: [0] * n for e, n in NDMA.items()}
        self.drr = {e: 0 for e in NDMA}
        self.known = {e: {} for e in ENGS}
        self.ops = {e: [] for e in ENGS}
        self.stack = None
        self.nphase = 0

    def sb(self, name, shape, dt):
        return Tl(self.stack.enter_context(self.nc.sbuf_tensor("%s_%d" % (name, self.nphase), shape, dt)))

    def ps(self, name, shape, dt=F32):
        return Tl(self.stack.enter_context(self.nc.psum_tensor("P%s_%d" % (name, self.nphase), shape, dt)))

    def op(self, eng, fn, r=(), w=(), dma=False):
        o = Op(eng, fn, dma)
        deps = []
        for t in r:
            if t.lw is not None:
                deps.append(t.lw)
        for t in w:
            if t.lw is not None:
                deps.append(t.lw)
            deps.extend(t.rd.values())
            deps.extend(t.rd_dma)
        out = []
        seen = set()
        for d in deps:
            if d is o or id(d) in seen:
                continue
            seen.add(id(d))
            if d.eng == eng and not d.dma and not dma and (eng == "pe" or not SELF_SYNC):
                continue
            out.append(d)
            d.inc = True
        o.deps = out
        for t in w:
            t.lw = o
            t.rd = {}
            t.rd_dma = []
        for t in r:
            if dma:
                t.rd_dma.append(o)
            else:
                t.rd[eng] = o
        self.ops[eng].append(o)
        return o

    def dma(self, q, out, in_, r=(), w=()):
        return self.op(q, lambda e: e.dma_start(out=out, in_=in_), r=r, w=w, dma=True)

    def phase(self):
        return _Phase(self)

    def _wait(self, e, known, sem, val):
        if val <= 0:
            return
        k = id(sem)
        if known.get(k, 0) >= val:
            return
        e.wait_ge(sem, val)
        known[k] = val

    def flush(self):
        for eng in ENGS:
            for o in self.ops[eng]:
                if o.dma:
                    n = len(self.dsem[eng])
                    k = self.drr[eng] % n
                    self.drr[eng] += 1
                    o.sem = self.dsem[eng][k]
                    o.prev = self.dcnt[eng][k]
                    self.dcnt[eng][k] += 16
                    o.ticket = self.dcnt[eng][k]
                elif o.inc:
                    self.cnt[eng] += 1
                    o.sem = self.sems[eng]
                    o.ticket = self.cnt[eng]
        with self.nc.Block() as block:
            for eng in ENGS:
                ops = self.ops[eng]
                if not ops:
                    continue

                def body(e, ops=ops, eng=eng):
                    known = self.known[eng]
                    for o in ops:
                        for d in o.deps:
                            self._wait(e, known, d.sem, d.ticket)
                        if o.dma:
                            self._wait(e, known, o.sem, o.prev)
                        ins = o.fn(e)
                        if o.dma:
                            ins.then_inc(o.sem, 16)
                        elif o.inc:
                            ins.then_inc(o.sem, 1)
                    if eng in self.dsem:
                        for k, s in enumerate(self.dsem[eng]):
                            self._wait(e, known, s, self.dcnt[eng][k])

                getattr(block, BLK[eng])(body)
        self.ops = {e: [] for e in ENGS}
        self.nphase += 1


class _Phase:
    def __init__(self, P):
        self.P = P

    def __enter__(self):
        self.P.stack = ExitStack()
        self.P.stack.__enter__()
        return self.P

    def __exit__(self, et, ev, tb):
        if et is None:
            self.P.flush()
        self.P.stack.__exit__(et, ev, tb)
        self.P.stack = None
        return False


def tok_blocks(total, bs):
    out = []
    t = 0
    while t < total:
        lim = S if t < S else total
        n = min(bs, lim - t)
        out.append((t, n))
        t += n
    return out


class G:
    pass


def mm_group(P, ps_ap, pairs, r, w):
    n = len(pairs)

    def fn(e):
        ins = None
        for i, (l, rr) in enumerate(pairs):
            ins = e.matmul(ps_ap, l, rr, start=(i == 0), stop=(i == n - 1))
        return ins

    return P.op("pe", fn, r=r, w=w)


def linear_B(P, W_ap, KC, act, tblocks, col0, ncols, CG, wbufs, psums, evac, m_done=None, ctr=None, wsrc=None, wq="pool", tick=None):
    if ctr is None:
        ctr = [0, 0]
    ng = (ncols + CG - 1) // CG
    for g in range(ng):
        c0 = col0 + g * CG
        cg = min(CG, col0 + ncols - c0)
        wt = wbufs[ctr[0] % len(wbufs)]
        ctr[0] += 1
        if wsrc is not None:
            src = wsrc(g)
        else:
            src = W_ap[0:KC * 128, c0:c0 + cg].rearrange("(kc p) m -> p kc m", p=128)
        P.dma(wq, wt[:, 0:KC, 0:cg], src, w=[wt])
        for ml in range(cg // 128):
            mi = (c0 - col0) // 128 + ml
            for (t0, tn) in tblocks:
                ps = psums[ctr[1] % len(psums)]
                ctr[1] += 1
                pairs = [(wt[:, kc, ml * 128:(ml + 1) * 128], act[:, kc, t0:t0 + tn]) for kc in range(KC)]
                mm_group(P, ps[:, 0:tn], pairs, r=[wt, act], w=[ps])
                evac(mi, t0, tn, ps)
            if m_done is not None:
                m_done(mi)
            if tick is not None:
                tick()
    return ctr


def linear_A(P, W_ap, KC, act, ttiles, col0, ncols, wbufs, psums, evac, ctr=None, tick=None):
    if ctr is None:
        ctr = [0, 0]
    ng = ncols // 512
    for g in range(ng):
        c0 = col0 + g * 512
        wt = wbufs[ctr[0] % len(wbufs)]
        ctr[0] += 1
        src = W_ap[0:KC * 128, c0:c0 + 512].rearrange("(kc p) m -> p kc m", p=128)
        P.dma("pool", wt[:, 0:KC, 0:512], src, w=[wt])
        for (t0, tn) in ttiles:
            ps = psums[ctr[1] % len(psums)]
            ctr[1] += 1
            pairs = [(act[:, kc, t0:t0 + tn], wt[:, kc, 0:512]) for kc in range(KC)]
            mm_group(P, ps[0:tn, 0:512], pairs, r=[wt, act], w=[ps])
            evac(g, t0, tn, ps)
            if tick is not None:
                tick()
    return ctr


def phase_consts(P, g):
    with P.phase():
        P.dma("sp", g.identF[:], g.d_identF, w=[g.t_identF])
        P.dma("pool", g.identB[:], g.d_identF, w=[g.t_identB])
        P.dma("sp", g.vecs[:], g.d_vecs, w=[g.t_vecs])
        P.dma("sp", g.lvecs[:], g.d_lvecs, w=[g.t_lvecs])
        P.op("dve", lambda e: e.memset(g.onesB[:], 1.0), w=[g.t_onesB])


def phase_transpose_in(P, g):
    with P.phase():
        xin = [P.sb("xin%d" % i, [128, 4, D], F32) for i in range(2)]
        xs = [P.sb("xs%d" % i, [128, 16, 512], F32) for i in range(2)]
        pss = [P.ps("tp%d" % i, [128, 512]) for i in range(4)]
        groups = [(g.d_x, tg * 512, tg * 512, 4) for tg in range(4)] + [(g.d_ctx, 0, S, 2)]
        pc = 0
        for gi, (src, s0, t0, nt) in enumerate(groups):
            xi = xin[gi % 2]
            xo = xs[gi % 2]
            P.dma("sp", xi[:, 0:nt, :], src[s0:s0 + nt * 128, :].rearrange("(j p) d -> p j d", p=128), w=[xi])
            for c in range(16):
                ps = pss[pc % 4]
                pc += 1

                def fn(e, ps=ps, xi=xi, c=c, nt=nt):
                    ins = None
                    for j in range(nt):
                        ins = e.transpose(ps[:, j * 128:(j + 1) * 128], xi[:, j, c * 128:(c + 1) * 128], g.identF[:])
                    return ins

                P.op("pe", fn, r=[xi], w=[ps])
                if c % 2 == 0:
                    P.op("act", lambda e, ps=ps, xo=xo, c=c, nt=nt: e.copy(out=xo[:, c, 0:nt * 128], in_=ps[:, 0:nt * 128]),
                         r=[ps], w=[xo])
                else:
                    P.op("dve", lambda e, ps=ps, xo=xo, c=c, nt=nt: e.tensor_copy(out=xo[:, c, 0:nt * 128], in_=ps[:, 0:nt * 128]),
                         r=[ps], w=[xo])
            P.dma("pool", g.xT[:, t0:t0 + nt * 128].rearrange("(c p) t -> p c t", p=128), xo[:, :, 0:nt * 128], r=[xo])


def mod_gen(P, g, slots):
    scond = P.sb("scond", [128, 16, 2], BF16)
    wb = [P.sb("mw%d" % i, [128, 16, 512], BF16) for i in range(2)]
    mtp = [P.ps("mtp%d" % i, [128, 16, 2]) for i in range(2)]
    for j in range(2):
        P.op("act", lambda e, j=j: e.activation(out=scond[:, :, j], in_=g.vecs[:, g.V_CV + j * 16:g.V_CV + (j + 1) * 16],
                                                 func=AF.Silu), w=[scond])
    gi = 0
    for si, (L, k) in enumerate(slots):
        mt = mtp[si % 2]
        for q in range(4):
            wt = wb[gi % 2]
            gi += 1
            c0 = k * 2048 + q * 512
            P.dma("pool", wt[:], g.d_mod_w[L, :, c0:c0 + 512].rearrange("(kc p) m -> p kc m", p=128), w=[wt])

            def fn(e, wt=wt, q=q, mt=mt):
                ins = None
                for cc in range(4):
                    for kc in range(16):
                        ins = e.matmul(mt[:, q * 4 + cc, :], wt[:, kc, cc * 128:(cc + 1) * 128], scond[:, kc, :],
                                       start=(kc == 0), stop=(kc == 15))
                return ins
            P.op("pe", fn, r=[wt, scond], w=[mt])
            yield
        for j in range(2):
            P.op("dve", lambda e, L=L, k=k, j=j, mt=mt: e.tensor_tensor(
                out=g.MOD[:, L, k, j, :], in0=mt[:, :, j],
                in1=g.vecs[:, g.V_MODB + L * 96 + k * 16:g.V_MODB + L * 96 + (k + 1) * 16], op=ALU.add), r=[mt], w=[g.t_MOD])
        if k in (1, 4):
            voff = g.V_NMIX if k == 1 else g.V_NMLP
            for j in range(2):
                P.op("dve", lambda e, L=L, j=j, k=k, voff=voff: e.scalar_tensor_tensor(
                    out=g.MOD[:, L, k, j, :], in0=g.MOD[:, L, k, j, :], scalar=1.0,
                    in1=g.vecs[:, voff + L * 16:voff + (L + 1) * 16], op0=ALU.add, op1=ALU.mult),
                    r=[g.t_MOD], w=[g.t_MOD])
        yield


def phase_modulation(P, g, slots):
    with P.phase():
        for _ in mod_gen(P, g, slots):
            pass


def phase_norm(P, g, src, dst, nC, tblocks, A_fn, B_fn):
    Dn = nC * 128
    with P.phase():
        xb = [P.sb("nx%d" % i, [128, nC, 512], F32) for i in range(3)]
        hb = [P.sb("nh%d" % i, [128, nC, 512], BF16) for i in range(2)]
        sq = [P.sb("nsq%d" % i, [128, 512], BF16) for i in range(3)]
        tmp = [P.sb("ntmp%d" % i, [128, 512], F32) for i in range(3)]
        rs = [P.sb("nrs%d" % i, [128, 512], F32) for i in range(2)]
        rstd = [P.sb("nrstd%d" % i, [128, 512], F32) for i in range(2)]
        ssp = [P.ps("nss%d" % i, [128, 512]) for i in range(2)]

        def stepA(bi, t0, tn):
            x_ = xb[bi % 3]
            ss = ssp[bi % 2]
            P.dma("sp", x_[:, :, 0:tn], src[0:Dn, t0:t0 + tn].rearrange("(c p) t -> p c t", p=128), w=[x_])
            for c in range(nC):
                s_ = sq[c % 3]
                P.op("act", lambda e, s_=s_, x_=x_, c=c, tn=tn: e.activation(out=s_[:, 0:tn], in_=x_[:, c, 0:tn], func=AF.Square),
                     r=[x_], w=[s_])
                P.op("pe", lambda e, ss=ss, s_=s_, c=c, tn=tn: e.matmul(ss[:, 0:tn], g.onesB[:], s_[:, 0:tn], start=(c == 0), stop=(c == nC - 1)),
                     r=[s_], w=[ss])
            r_ = rs[bi % 2]
            rd_ = rstd[bi % 2]
            P.op("act", lambda e, r_=r_, ss=ss, tn=tn: e.activation(out=r_[:, 0:tn], in_=ss[:, 0:tn], func=AF.Sqrt,
                                                                      bias=g.vecs[:, g.V_EPS:g.V_EPS + 1], scale=1.0 / Dn),
                 r=[ss], w=[r_])
            P.op("dve", lambda e, r_=r_, rd_=rd_, tn=tn: e.reciprocal(out=rd_[:, 0:tn], in_=r_[:, 0:tn]), r=[r_], w=[rd_])

        def stepB(bi, t0, tn):
            j = 0 if t0 < S else 1
            x_ = xb[bi % 3]
            h_ = hb[bi % 2]
            rd_ = rstd[bi % 2]
            for c in range(nC):
                if B_fn is None:
                    P.op("dve", lambda e, h_=h_, x_=x_, rd_=rd_, c=c, tn=tn, j=j: e.scalar_tensor_tensor(
                        out=h_[:, c, 0:tn], in0=x_[:, c, 0:tn], scalar=A_fn(c, j), in1=rd_[:, 0:tn],
                        op0=ALU.mult, op1=ALU.mult), r=[x_, rd_], w=[h_])
                    continue
                t_ = tmp[c % 3]
                P.op("dve", lambda e, t_=t_, x_=x_, rd_=rd_, c=c, tn=tn, j=j: e.scalar_tensor_tensor(
                    out=t_[:, 0:tn], in0=x_[:, c, 0:tn], scalar=A_fn(c, j), in1=rd_[:, 0:tn],
                    op0=ALU.mult, op1=ALU.mult), r=[x_, rd_], w=[t_])
                P.op("act", lambda e, t_=t_, h_=h_, c=c, tn=tn, j=j: e.activation(
                    out=h_[:, c, 0:tn], in_=t_[:, 0:tn], func=AF.Identity, bias=B_fn(c, j), scale=1.0),
                    r=[t_], w=[h_])
            P.dma("pool", dst[0:Dn, t0:t0 + tn].rearrange("(c p) t -> p c t", p=128), h_[:, :, 0:tn], r=[h_])

        nb = len(tblocks)
        for i in range(nb + 1):
            if i < nb:
                stepA(i, *tblocks[i])
            if i >= 1:
                stepB(i - 1, *tblocks[i - 1])


def norm_mod(P, g, L, kA, kB, ttot):
    phase_norm(P, g, g.xT, g.hT, 16, blocks(0, ttot, 512),
               lambda c, j: g.MOD[:, L, kA, j, c:c + 1], lambda c, j: g.MOD[:, L, kB, j, c:c + 1])


def blocks(t_start, t_end, bs):
    out = []
    t = t_start
    while t < t_end:
        lim = S if t < S else t_end
        lim = min(lim, t_end)
        n = min(bs, lim - t)
        out.append((t, n))
        t += n
    return out


def phase_inproj0(P, g):
    SC = 128.0 ** -0.5
    with P.phase():
        act = P.sb("h", [128, 16, T], BF16)
        wb = [P.sb("w%d" % i, [128, 16, 512], BF16) for i in range(2)]
        pss = [P.ps("ps%d" % i, [128, 512]) for i in range(4)]
        stb = [P.sb("stb%d" % i, [128, T], BF16) for i in range(2)]
        stf = [P.sb("stf%d" % i, [128, T], F32) for i in range(2)]
        vst = [P.sb("vst%d" % i, [128, 512], BF16) for i in range(3)]
        P.dma("sp", act[:], g.hT.rearrange("(c p) t -> p c t", p=128), w=[act])
        tb = blocks(0, T, 512)
        ctr = [0, 0]
        cnt = [0]
        mgen = mod_gen(P, g, [(0, 2), (0, 3), (0, 4), (0, 5)] + [(1, k) for k in range(6)])
        tk = [0]

        def tick():
            tk[0] += 1
            if tk[0] % 3 != 0:
                next(mgen, None)

        def mk(kind, dst, stl):
            def evac(mi, t0, tn, ps):
                st = stl[mi % 2]
                cnt[0] += 1
                if kind == "q":
                    P.op("act", lambda e: e.mul(out=st[:, t0:t0 + tn], in_=ps[:, 0:tn], mul=SC), r=[ps], w=[st])
                elif kind == "g":
                    P.op("act", lambda e: e.activation(out=st[:, t0:t0 + tn], in_=ps[:, 0:tn], func=AF.Gelu), r=[ps], w=[st])
                elif cnt[0] % 2 == 0:
                    P.op("act", lambda e: e.copy(out=st[:, t0:t0 + tn], in_=ps[:, 0:tn]), r=[ps], w=[st])
                else:
                    P.op("dve", lambda e: e.tensor_copy(out=st[:, t0:t0 + tn], in_=ps[:, 0:tn]), r=[ps], w=[st])

            def m_done(mi):
                st = stl[mi % 2]
                P.dma("sp", dst[mi * 128:(mi + 1) * 128, :], st[:, :], r=[st])
            return evac, m_done

        W = g.d_w_in
        for kind, col0, dst, stl in (("q", 0, g.qT, stb), ("k", 1024, g.kT, stb), ("u", 3072, g.uT, stf), ("g", 4096, g.ggT, stf)):
            ev, md = mk(kind, dst, stl)
            linear_B(P, W, 16, act, tb, col0, 1024, 512, wb, pss, ev, md, ctr, tick=tick)
        vc = [0]

        def evac_v(gi, t0, tn, ps):
            st = vst[vc[0] % 3]
            vc[0] += 1
            if vc[0] % 2 == 0:
                P.op("act", lambda e: e.copy(out=st[0:tn, :], in_=ps[0:tn, :]), r=[ps], w=[st])
            else:
                P.op("dve", lambda e: e.tensor_copy(out=st[0:tn, :], in_=ps[0:tn, :]), r=[ps], w=[st])
            P.dma("sp", g.v0[t0:t0 + tn, gi * 512:(gi + 1) * 512], st[0:tn, :], r=[st])
        linear_A(P, W, 16, act, [(i * 128, 128) for i in range(T // 128)], 2048, 1024, wb, pss, evac_v, ctr, tick=tick)
        for _ in mgen:
            pass


def na_chunks(r):
    rs = min(max(r - 4, 0), 24)
    if rs % 2 == 0:
        return [(rs + 2 * j, rs + 2 * j - r + 7) for j in range(4)]
    out = [(rs - 1, 14)]
    for j in range(3):
        kr0 = rs + 1 + 2 * j
        out.append((kr0, kr0 - r + 7))
    out.append((rs + 7, 15))
    return out


def na_body(P, g):
    bias = P.sb("nab", [128, 8, 16, 64], BF16)
    P.dma("pool", bias[:], g.d_nabias.rearrange("p (h d q) -> p h d q", h=8, d=16), w=[bias])
    qh = [P.sb("q%d" % i, [128, T], BF16) for i in range(2)]
    kh = [P.sb("k%d" % i, [128, T], BF16) for i in range(2)]
    vh = [P.sb("v%d" % i, [128, 18, 128], BF16) for i in range(2)]
    oh = [P.sb("o%d" % i, [128, T], BF16) for i in range(2)]
    sps = [P.ps("s%d" % i, [128, 512]) for i in range(3)]
    ops_ = [P.ps("o%d" % i, [128, 512]) for i in range(2)]
    pts = [P.sb("pt%d" % i, [128, 512], BF16) for i in range(3)]
    rdn = [P.sb("rd%d" % i, [128, 256], F32) for i in range(2)]
    A, B = [], []
    it = 0
    for h in range(8):
        q_, k_, v_, o_ = qh[h % 2], kh[h % 2], vh[h % 2], oh[h % 2]
        for r in range(33):
            sp, op_, pt, rd = sps[it % 3], ops_[it % 2], pts[it % 3], rdn[it % 2]
            it += 1
            if r < 32:
                chs = na_chunks(r)
                nw = len(chs)
                ncol = (nw + 2) * 64

                def stepA(h=h, r=r, chs=chs, nw=nw, ncol=ncol, sp=sp, pt=pt, q_=q_, k_=k_, v_=v_):
                    if r == 0:
                        P.dma("sp", q_[:], g.qT[h * 128:(h + 1) * 128, :], w=[q_])
                        P.dma("sp", k_[:], g.kT[h * 128:(h + 1) * 128, :], w=[k_])
                        P.dma("sp", v_[:], g.v0[:, h * 128:(h + 1) * 128].rearrange("(c p) d -> p c d", p=128), w=[v_])

                    def fs(e):
                        ins = None
                        qa = q_[:, r * 64:(r + 1) * 64]
                        for i, (kr0, bi) in enumerate(chs):
                            e.matmul(sp[:, i * 64:(i + 1) * 64], k_[:, kr0 * 64:kr0 * 64 + 128], qa, start=True, stop=False)
                            ins = e.matmul(sp[:, i * 64:(i + 1) * 64], g.identB[:], bias[:, h, bi, :], start=False, stop=True)
                        for j in range(2):
                            ins = e.matmul(sp[:, (nw + j) * 64:(nw + j + 1) * 64], k_[:, S + j * 128:S + (j + 1) * 128], qa,
                                           start=True, stop=True)
                        return ins
                    P.op("pe", fs, r=[q_, k_, bias], w=[sp])
                    P.op("act", lambda e: e.activation(out=pt[:, 0:ncol], in_=sp[:, 0:ncol], func=AF.Exp), r=[sp], w=[pt])

                def stepB(h=h, r=r, chs=chs, nw=nw, op_=op_, pt=pt, v_=v_, rd=rd, o_=o_):
                    def fo(e):
                        ins = None
                        cl = [kr0 // 2 for (kr0, _) in chs] + [16, 17]
                        n = len(cl)
                        for i, c in enumerate(cl):
                            e.matmul(op_[:, 0:64], v_[:, c, :], pt[:, i * 64:(i + 1) * 64], start=(i == 0), stop=(i == n - 1))
                        for i, c in enumerate(cl):
                            ins = e.matmul(op_[:, 64:128], g.onesB[:], pt[:, i * 64:(i + 1) * 64], start=(i == 0), stop=(i == n - 1))
                        return ins
                    P.op("pe", fo, r=[pt, v_], w=[op_])
                    P.op("dve", lambda e: e.reciprocal(out=rd[:, 0:64], in_=op_[:, 64:128]), r=[op_], w=[rd])
                    P.op("dve", lambda e: e.tensor_tensor(out=o_[:, r * 64:(r + 1) * 64], in0=op_[:, 0:64], in1=rd[:, 0:64], op=ALU.mult),
                         r=[op_, rd], w=[o_])
            else:
                def stepA(sp=sp, pt=pt, q_=q_, k_=k_):
                    def fsc(e):
                        ins = None
                        for j in range(2):
                            ins = e.matmul(sp[:, j * 256:(j + 1) * 256], k_[:, S + j * 128:S + (j + 1) * 128], q_[:, S:T], start=True, stop=True)
                        return ins
                    P.op("pe", fsc, r=[q_, k_], w=[sp])
                    P.op("act", lambda e: e.activation(out=pt[:, 0:512], in_=sp[:, 0:512], func=AF.Exp), r=[sp], w=[pt])

                def stepB(h=h, op_=op_, pt=pt, v_=v_, rd=rd, o_=o_):
                    def foc(e):
                        ins = None
                        for j in range(2):
                            e.matmul(op_[:, 0:256], v_[:, 16 + j, :], pt[:, j * 256:(j + 1) * 256], start=(j == 0), stop=(j == 1))
                        for j in range(2):
                            ins = e.matmul(op_[:, 256:512], g.onesB[:], pt[:, j * 256:(j + 1) * 256], start=(j == 0), stop=(j == 1))
                        return ins
                    P.op("pe", foc, r=[pt, v_], w=[op_])
                    P.op("dve", lambda e: e.reciprocal(out=rd[:, 0:256], in_=op_[:, 256:512]), r=[op_], w=[rd])
                    P.op("dve", lambda e: e.tensor_tensor(out=o_[:, S:T], in0=op_[:, 0:256], in1=rd[:, 0:256], op=ALU.mult),
                         r=[op_, rd], w=[o_])
                    P.dma("sp", g.mixT[h * 128:(h + 1) * 128, :], o_[:], r=[o_])
            A.append(stepA)
            B.append(stepB)
    n = len(A)
    for i in range(n + 1):
        if i < n:
            A[i]()
        if i >= 1:
            B[i - 1]()
        yield


def phase_na_lru(P, g):
    with P.phase():
        ga = na_body(P, g)
        gb = lru_body(P, g)
        gc = convert_gen(P, g, 0)
        alive_a = alive_b = alive_c = True
        it = 0
        while alive_a or alive_b or alive_c:
            it += 1
            if alive_c and it % 2 == 0:
                try:
                    next(gc)
                except StopIteration:
                    alive_c = False
            if alive_a:
                try:
                    next(ga)
                except StopIteration:
                    alive_a = False
            for _ in range(2):
                if alive_b:
                    try:
                        next(gb)
                    except StopIteration:
                        alive_b = False


def make_nabias(rpb):
    NEG = np.float32(-30000.0)
    kc = np.arange(64)[:, None]
    qc = np.arange(64)[None, :]
    cs = np.clip(qc - 8, 0, 48)
    ok = (kc >= cs) & (kc < cs + 16)
    dc = np.clip(kc - qc, -15, 15) + 15
    out = np.full((2, 64, 8, 16, 64), NEG, np.float32)
    for h in range(8):
        def blk(dr):
            if dr < -7 or dr > 7:
                return np.full((64, 64), NEG, np.float32)
            return np.where(ok, rpb[h, dr + 7][dc], NEG).astype(np.float32)
        for d in range(14):
            for a in range(2):
                out[a, :, h, d, :] = blk(d - 7 + a)
        out[1, :, h, 14, :] = blk(-4)
        out[0, :, h, 15, :] = blk(3)
    return np.ascontiguousarray(out.reshape(128, 8 * 16 * 64))


def lru_body(P, g):
    wg = P.sb("wg", [128, 2, 2, 8, 128], BF16)
    P.dma("pool", wg[:, 0], g.d_lru_wa.rearrange("d n i j -> i d n j"), w=[wg])
    P.dma("pool", wg[:, 1], g.d_lru_wx.rearrange("d n i j -> i d n j"), w=[wg])
    lv = g.lvecs
    nsp = P.sb("nsp", [128, 16], F32)
    e1 = P.sb("e1", [128, 16], F32)
    hb_ = P.sb("hbias", [128, 32], F32)
    c25 = P.sb("c25", [128, 1], F32)
    P.op("act", lambda e: e.activation(out=e1[:], in_=lv[:, g.LV_LAM:g.LV_LAM + 16], func=AF.Exp, scale=-1.0), w=[e1])
    P.op("act", lambda e: e.activation(out=nsp[:], in_=e1[:], func=AF.Ln, bias=g.vecs[:, g.V_ONE:g.V_ONE + 1], scale=1.0),
         r=[e1], w=[nsp])
    P.op("dve", lambda e: e.tensor_scalar(out=nsp[:], in0=nsp[:], scalar1=-4.0, scalar2=None, op0=ALU.mult), r=[nsp], w=[nsp])
    P.op("dve", lambda e: e.tensor_scalar(out=hb_[:], in0=lv[:, g.LV_BA:g.LV_BA + 32], scalar1=0.5, scalar2=None, op0=ALU.mult), w=[hb_])
    P.op("dve", lambda e: e.memset(c25[:], 0.25), w=[c25])
    yield

    def f32t(name, n=2):
        return [P.sb("%s%d" % (name, i), [128, T], F32) for i in range(n)]
    u_, gg_, xc_ = f32t("u", 2), f32t("gg", 1), f32t("xc", 1)
    xcb_ = [P.sb("xcb%d" % i, [128, T], BF16) for i in range(1)]
    gr_, gi_, a_ = f32t("gr", 2), f32t("gi", 2), f32t("a", 2)
    hd_ = f32t("hd", 2)
    y_ = [P.sb("y%d" % i, [128, T], BF16) for i in range(2)]
    pss = [P.ps("lps%d" % i, [128, 512]) for i in range(3)]
    tb = blocks(0, T, 512)
    pc = 0

    def rev(ap_t, lo, hi):
        a = ap_t[:, lo:hi]
        return bass.AP(a.tensor, a.offset + (hi - lo - 1), [list(a.ap[0]), [-1, hi - lo]])

    P.dma("sp", u_[0][:], g.uT[0:128, :], w=[u_[0]])
    for n in range(8):
        u, gg, xc, xcb, y = u_[n % 2], gg_[0], xc_[0], xcb_[0], y_[n % 2]
        if n + 1 < 8:
            P.dma("sp", u_[(n + 1) % 2][:], g.uT[(n + 1) * 128:(n + 2) * 128, :], w=[u_[(n + 1) % 2]])
        cw = lambda tap, n=n: lv[:, g.LV_CW + tap * 8 + n:g.LV_CW + tap * 8 + n + 1]
        P.op("act", lambda e, xc=xc, u=u, n=n, cw=cw: e.activation(out=xc[:], in_=u[:], func=AF.Identity,
                                                                    bias=lv[:, g.LV_CB + n:g.LV_CB + n + 1], scale=cw(2)),
             r=[u], w=[xc])
        yield
        for tap in (0, 1, 3):
            off = tap - 2
            for (s0, s1) in ((0, S), (S, T)):
                d0, d1 = max(s0, s0 - off), min(s1, s1 - off)
                P.op("dve", lambda e, xc=xc, u=u, tap=tap, off=off, d0=d0, d1=d1, cw=cw: e.scalar_tensor_tensor(
                    out=xc[:, d0:d1], in0=u[:, d0 + off:d1 + off], scalar=cw(tap), in1=xc[:, d0:d1],
                    op0=ALU.mult, op1=ALU.add), r=[u, xc], w=[xc])
            yield
        P.op("act", lambda e, xc=xc, xcb=xcb: e.copy(out=xcb[:], in_=xc[:]), r=[xc], w=[xcb])
        yield
        for d in range(2):
            hd, gr, gi, a = hd_[d], gr_[d], gi_[d], a_[d]
            for gt, dst, boff in ((0, gr, 0), (1, gi, 16)):
                for (t0, tn) in tb:
                    ps = pss[pc % 3]
                    pc += 1
                    P.op("pe", lambda e, ps=ps, gt=gt, d=d, n=n, xcb=xcb, t0=t0, tn=tn: e.matmul(
                        ps[:, 0:tn], wg[:, gt, d, n, :], xcb[:, t0:t0 + tn], start=True, stop=True), r=[wg, xcb], w=[ps])
                    P.op("act", lambda e, ps=ps, dst=dst, boff=boff, d=d, n=n, t0=t0, tn=tn: e.activation(
                        out=dst[:, t0:t0 + tn], in_=ps[:, 0:tn], func=AF.Tanh,
                        bias=hb_[:, boff + d * 8 + n:boff + d * 8 + n + 1], scale=0.5), r=[ps, hb_], w=[dst])
                    yield
            P.op("act", lambda e, a=a, gr=gr, d=d, n=n: e.activation(out=a[:], in_=gr[:], func=AF.Exp,
                                                                      bias=nsp[:, d * 8 + n:d * 8 + n + 1],
                                                                      scale=nsp[:, d * 8 + n:d * 8 + n + 1]), r=[gr, nsp], w=[a])
            P.op("dve", lambda e, gi=gi, xc=xc: e.scalar_tensor_tensor(out=gi[:], in0=gi[:], scalar=1.0, in1=xc[:],
                                                                       op0=ALU.add, op1=ALU.mult), r=[gi, xc], w=[gi])
            yield
            P.op("dve", lambda e, a=a, gr=gr: e.tensor_tensor(out=gr[:], in0=a[:], in1=a[:], op=ALU.mult), r=[a, gr], w=[gr])
            yield
            P.op("act", lambda e, gr=gr: e.activation(out=gr[:], in_=gr[:], func=AF.Sqrt, bias=c25[:, 0:1], scale=-0.25),
                 r=[gr, c25], w=[gr])
            yield
            P.op("pool", lambda e, gr=gr, gi=gi: e.tensor_tensor(out=gi[:], in0=gi[:], in1=gr[:], op=ALU.mult),
                 r=[gr, gi], w=[gi])
            yield
            vv = gi
            if d == 0:
                P.op("dve", lambda e, hd=hd, a=a, vv=vv: e.tensor_tensor_scan(
                    out=hd[:, S:T], data0=a[:, S:T], data1=vv[:, S:T], initial=0.0, op0=ALU.mult, op1=ALU.add),
                    r=[a, vv], w=[hd])
                P.op("dve", lambda e, hd=hd, a=a, vv=vv: e.tensor_tensor_scan(
                    out=hd[:, 0:S], data0=a[:, 0:S], data1=vv[:, 0:S], initial=hd[:, T - 1:T], op0=ALU.mult, op1=ALU.add),
                    r=[a, vv, hd], w=[hd])
            else:
                P.op("dve", lambda e, hd=hd, a=a, vv=vv: e.tensor_tensor_scan(
                    out=rev(hd, S, T), data0=rev(a, S, T), data1=rev(vv, S, T), initial=0.0, op0=ALU.mult, op1=ALU.add),
                    r=[a, vv], w=[hd])
                P.op("dve", lambda e, hd=hd, a=a, vv=vv: e.tensor_tensor_scan(
                    out=rev(hd, 0, S), data0=rev(a, 0, S), data1=rev(vv, 0, S), initial=hd[:, S:S + 1], op0=ALU.mult, op1=ALU.add),
                    r=[a, vv, hd], w=[hd])
            yield
        P.dma("sp", gg[:], g.ggT[n * 128:(n + 1) * 128, :], w=[gg])
        P.op("dve", lambda e: e.tensor_tensor(out=hd_[0][:], in0=hd_[0][:], in1=hd_[1][:], op=ALU.add),
             r=[hd_[0], hd_[1]], w=[hd_[0]])
        yield
        P.op("pool", lambda e, y=y, gg=gg: e.tensor_tensor(out=y[:], in0=hd_[0][:], in1=gg[:], op=ALU.mult),
             r=[hd_[0], gg], w=[y])
        P.dma("sp", g.mixT[1024 + n * 128:1024 + (n + 1) * 128, :], y[:], r=[y])
        yield


def phase_outproj(P, g, L, W_ap, src, ttot):
    with P.phase():
        act = P.sb("a", [128, 16, ttot], BF16)
        wb = [P.sb("w%d" % i, [128, 16, 512], BF16) for i in range(2)]
        pss = [P.ps("ps%d" % i, [128, 512]) for i in range(4)]
        xr = [P.sb("xr%d" % i, [128, ttot], F32) for i in range(2)]
        P.dma("sp", act[:], src[:, 0:ttot].rearrange("(c p) t -> p c t", p=128), w=[act])
        tb = blocks(0, ttot, 512)

        def evac(mi, t0, tn, ps):
            x_ = xr[mi % 2]
            if t0 == 0:
                P.dma("sp", x_[:], g.xT[mi * 128:(mi + 1) * 128, 0:ttot], w=[x_])
            j = 0 if t0 < S else 1
            P.op("dve", lambda e: e.scalar_tensor_tensor(out=x_[:, t0:t0 + tn], in0=ps[:, 0:tn], scalar=g.MOD[:, L, 2, j, mi:mi + 1],
                                                          in1=x_[:, t0:t0 + tn], op0=ALU.mult, op1=ALU.add), r=[ps, x_], w=[x_])

        def m_done(mi):
            x_ = xr[mi % 2]
            P.dma("sp", g.xT[mi * 128:(mi + 1) * 128, 0:ttot], x_[:], r=[x_])
        linear_B(P, W_ap, 16, act, tb, 0, D, 512, wb, pss, evac, m_done)


def phase_mlp(P, g, L, ttot):
    with P.phase():
        hb = [P.sb("h%d" % i, [128, 16, 768], BF16) for i in range(1)]
        aT = P.sb("aT", [128, 64, 768], BF16)
        w1b = [P.sb("w1_%d" % i, [128, 16, 256], BF16) for i in range(3)]
        w2b = [P.sb("w2_%d" % i, [128, 64, 128], BF16) for i in range(2)]
        xr = [P.sb("xr%d" % i, [128, 768], F32) for i in range(2)]
        rl = [P.sb("rl%d" % i, [128, 384], F32) for i in range(3)]
        pss = [P.ps("ps%d" % i, [128, 512]) for i in range(6)]
        passes = []
        t = 0
        while t < ttot:
            n = min(768, ttot - t)
            passes.append((t, n))
            t += n
        c1 = [0, 0]
        c2 = [0, 0]
        rc = [0]
        for pi, (p0, pn) in enumerate(passes):
            h_ = hb[0]
            P.dma("sp", h_[:, :, 0:pn], g.hT[:, p0:p0 + pn].rearrange("(c p) t -> p c t", p=128), w=[h_])
            tbl = [(t0 - p0, tn) for (t0, tn) in blocks(p0, p0 + pn, 384)]

            def evac1(mi, t0, tn, ps):
                r_ = rl[rc[0] % 3]
                rc[0] += 1
                P.op("act", lambda e: e.activation(out=r_[:, 0:tn], in_=ps[:, 0:tn], func=AF.Relu), r=[ps], w=[r_])
                P.op("dve", lambda e: e.tensor_tensor(out=aT[:, mi, t0:t0 + tn], in0=r_[:, 0:tn], in1=r_[:, 0:tn], op=ALU.mult),
                     r=[r_], w=[aT])
            c1[1] = c2[1] = max(c1[1], c2[1])
            linear_B(P, None, 16, h_, tbl, 0, DFF, 256, w1b, pss, evac1, None, c1, wsrc=lambda gi: g.W1r[L, gi])

            def evac2(mi, t0, tn, ps, p0=p0, pn=pn):
                x_ = xr[mi % 2]
                if t0 == 0:
                    P.dma("sp", x_[:, 0:pn], g.xT[mi * 128:(mi + 1) * 128, p0:p0 + pn], w=[x_])
                j = 0 if p0 + t0 < S else 1
                P.op("dve", lambda e: e.scalar_tensor_tensor(out=x_[:, t0:t0 + tn], in0=ps[:, 0:tn], scalar=g.MOD[:, L, 5, j, mi:mi + 1],
                                                              in1=x_[:, t0:t0 + tn], op0=ALU.mult, op1=ALU.add), r=[ps, x_], w=[x_])

            def m_done2(mi, p0=p0, pn=pn):
                x_ = xr[mi % 2]
                P.dma("sp", g.xT[mi * 128:(mi + 1) * 128, p0:p0 + pn], x_[:, 0:pn], r=[x_])
            c1[1] = c2[1] = max(c1[1], c2[1])
            linear_B(P, None, 64, aT, tbl, 0, D, 128, w2b, pss, evac2, m_done2, c2, wsrc=lambda gi: g.W2r[L, gi])


def convert_gen(P, g, L):
    for gi in range(32):
        P.dma("pool", g.W1r[L, gi], g.d_w1[L, :, gi * 256:(gi + 1) * 256].rearrange("(kc p) m -> p kc m", p=128))
        yield
    for mi in range(16):
        for q4 in range(4):
            P.dma("pool", g.W2r[L, mi, :, q4 * 16:(q4 + 1) * 16, :],
                  g.d_w2[L, q4 * 2048:(q4 + 1) * 2048, mi * 128:(mi + 1) * 128].rearrange("(kc p) m -> p kc m", p=128))
            yield


def convert_mlp_weights(P, g, L):
    for _ in convert_gen(P, g, L):
        pass


def phase_mla_down(P, g):
    with P.phase():
        act = P.sb("h", [128, 16, T], BF16)
        wb = [P.sb("w%d" % i, [128, 16, 512], BF16) for i in range(2)]
        wpe = P.sb("wpe", [128, 16, 2, 64], BF16)
        rope = P.sb("rope", [64, 2, S], F32)
        pss = [P.ps("ps%d" % i, [128, 512]) for i in range(6)]
        stf = [P.sb("stf%d" % i, [128, T], F32) for i in range(2)]
        t1 = [P.sb("t1_%d" % i, [64, 512], F32) for i in range(2)]
        t2 = [P.sb("t2_%d" % i, [64, 512], F32) for i in range(2)]
        kst = P.sb("kst", [64, T], BF16)
        P.dma("sp", act[:], g.hT.rearrange("(c p) t -> p c t", p=128), w=[act])
        P.dma("sp", rope[:], g.d_rope.rearrange("p (k t) -> p k t", k=2), w=[rope])
        P.dma("pool", wpe[:, :, 0, :], g.d_w_dkv[:, 512:576].rearrange("(kc p) m -> p kc m", p=128), w=[wpe])
        P.dma("pool", wpe[:, :, 1, :], g.d_w_dkv_sw.rearrange("(kc p) m -> p kc m", p=128), w=[wpe])
        ctr = [0, 0]
        cnt = [0]

        def mk(dst, ttot):
            def evac(mi, t0, tn, ps):
                st = stf[mi % 2]
                cnt[0] += 1
                if cnt[0] % 2 == 0:
                    P.op("act", lambda e: e.copy(out=st[:, t0:t0 + tn], in_=ps[:, 0:tn]), r=[ps], w=[st])
                else:
                    P.op("dve", lambda e: e.tensor_copy(out=st[:, t0:t0 + tn], in_=ps[:, 0:tn]), r=[ps], w=[st])

            def m_done(mi):
                st = stf[mi % 2]
                P.dma("sp", dst[mi * 128:(mi + 1) * 128, 0:ttot], st[:, 0:ttot], r=[st])
            return evac, m_done
        ev, md = mk(g.cqT, S)
        linear_B(P, g.d_w_dq, 16, act, blocks(0, S, 512), 0, 512, 512, wb, pss, ev, md, ctr)
        ev, md = mk(g.ckvT, T)
        linear_B(P, g.d_w_dkv, 16, act, blocks(0, T, 512), 0, 512, 512, wb, pss, ev, md, ctr)
        for bi, (t0, tn) in enumerate(blocks(0, T, 512)):
            psA = pss[ctr[1] % 6]
            ctr[1] += 1
            mm_group(P, psA[0:64, 0:tn], [(wpe[:, kc, 0, :], act[:, kc, t0:t0 + tn]) for kc in range(16)], r=[wpe, act], w=[psA])
            if t0 >= S:
                P.op("act", lambda e, psA=psA, t0=t0, tn=tn: e.copy(out=kst[:, t0:t0 + tn], in_=psA[0:64, 0:tn]), r=[psA], w=[kst])
                continue
            psB = pss[ctr[1] % 6]
            ctr[1] += 1
            mm_group(P, psB[0:64, 0:tn], [(wpe[:, kc, 1, :], act[:, kc, t0:t0 + tn]) for kc in range(16)], r=[wpe, act], w=[psB])
            a_, b_ = t1[bi % 2], t2[bi % 2]
            P.op("dve", lambda e, a_=a_, psA=psA, t0=t0, tn=tn: e.tensor_tensor(out=a_[:, 0:tn], in0=psA[0:64, 0:tn], in1=rope[:, 0, t0:t0 + tn],
                                                                               op=ALU.mult), r=[psA, rope], w=[a_])
            P.op("dve", lambda e, b_=b_, psB=psB, t0=t0, tn=tn: e.tensor_tensor(out=b_[:, 0:tn], in0=psB[0:64, 0:tn], in1=rope[:, 1, t0:t0 + tn],
                                                                               op=ALU.mult), r=[psB, rope], w=[b_])
            P.op("pool", lambda e, a_=a_, b_=b_, t0=t0, tn=tn: e.tensor_tensor(out=kst[:, t0:t0 + tn], in0=a_[:, 0:tn], in1=b_[:, 0:tn], op=ALU.add),
                 r=[a_, b_], w=[kst])
        P.dma("sp", g.kpeT[:, :], kst[:], r=[kst])


def phase_mla_up(P, g):
    SC = 192.0 ** -0.5
    with P.phase():
        aq = P.sb("aq", [128, 4, S], BF16)
        akv = P.sb("akv", [128, 4, T], BF16)
        wq = P.sb("wq", [128, 4, 3072], BF16)
        wqs = P.sb("wqs", [128, 4, 1024], BF16)
        wkv = P.sb("wkv", [128, 4, 4096], BF16)
        rope = P.sb("rope", [64, 2, S], F32)
        pss = [P.ps("ps%d" % i, [128, 512]) for i in range(6)]
        qn_st = [P.sb("qn%d" % i, [128, S], BF16) for i in range(2)]
        qr_st = [P.sb("qr%d" % i, [64, S], BF16) for i in range(2)]
        kn_st = [P.sb("kn%d" % i, [128, T], BF16) for i in range(2)]
        vst = [P.sb("vst%d" % i, [128, 512], BF16) for i in range(3)]
        t1 = [P.sb("t1_%d" % i, [64, 512], F32) for i in range(2)]
        t2 = [P.sb("t2_%d" % i, [64, 512], F32) for i in range(2)]
        P.dma("sp", aq[:], g.cqnT.rearrange("(c p) t -> p c t", p=128), w=[aq])
        P.dma("sp", akv[:], g.ckvnT.rearrange("(c p) t -> p c t", p=128), w=[akv])
        P.dma("sp", rope[:], g.d_rope.rearrange("p (k t) -> p k t", k=2), w=[rope])
        P.dma("pool", wq[:], g.d_w_uq.rearrange("(kc p) m -> p kc m", p=128), w=[wq])
        P.dma("pool", wqs[:], g.d_w_uq_sw.rearrange("(kc p) m -> p kc m", p=128), w=[wqs])
        P.dma("pool", wkv[:], g.d_w_ukv.rearrange("(kc p) m -> p kc m", p=128), w=[wkv])
        P.op("dve", lambda e: e.tensor_scalar(out=rope[:], in0=rope[:], scalar1=SC, scalar2=None, op0=ALU.mult), r=[rope], w=[rope])
        pc = 0
        it = 0
        for h in range(16):
            qn, qr, kn = qn_st[h % 2], qr_st[h % 2], kn_st[h % 2]
            for (t0, tn) in blocks(0, S, 512):
                ps = pss[pc % 6]
                pc += 1
                mm_group(P, ps[:, 0:tn], [(wq[:, kc, h * 192:h * 192 + 128], aq[:, kc, t0:t0 + tn]) for kc in range(4)], r=[wq, aq], w=[ps])
                P.op("act", lambda e, ps=ps, qn=qn, t0=t0, tn=tn: e.mul(out=qn[:, t0:t0 + tn], in_=ps[:, 0:tn], mul=SC), r=[ps], w=[qn])
                psA = pss[pc % 6]
                pc += 1
                mm_group(P, psA[0:64, 0:tn], [(wq[:, kc, h * 192 + 128:h * 192 + 192], aq[:, kc, t0:t0 + tn]) for kc in range(4)],
                         r=[wq, aq], w=[psA])
                psB = pss[pc % 6]
                pc += 1
                mm_group(P, psB[0:64, 0:tn], [(wqs[:, kc, h * 64:(h + 1) * 64], aq[:, kc, t0:t0 + tn]) for kc in range(4)],
                         r=[wqs, aq], w=[psB])
                a_, b_ = t1[it % 2], t2[it % 2]
                it += 1
                P.op("dve", lambda e, a_=a_, psA=psA, t0=t0, tn=tn: e.tensor_tensor(out=a_[:, 0:tn], in0=psA[0:64, 0:tn], in1=rope[:, 0, t0:t0 + tn],
                                                                                   op=ALU.mult), r=[psA, rope], w=[a_])
                P.op("dve", lambda e, b_=b_, psB=psB, t0=t0, tn=tn: e.tensor_tensor(out=b_[:, 0:tn], in0=psB[0:64, 0:tn], in1=rope[:, 1, t0:t0 + tn],
                                                                                   op=ALU.mult), r=[psB, rope], w=[b_])
                P.op("pool", lambda e, a_=a_, b_=b_, qr=qr, t0=t0, tn=tn: e.tensor_tensor(out=qr[:, t0:t0 + tn], in0=a_[:, 0:tn], in1=b_[:, 0:tn], op=ALU.add),
                     r=[a_, b_], w=[qr])
            P.dma("sp", g.qnT[h * 128:(h + 1) * 128, :], qn[:], r=[qn])
            P.dma("sp", g.qrT[h * 64:(h + 1) * 64, :], qr[:], r=[qr])
            for bi, (t0, tn) in enumerate(blocks(0, T, 512)):
                ps = pss[pc % 6]
                pc += 1
                mm_group(P, ps[:, 0:tn], [(wkv[:, kc, h * 256:h * 256 + 128], akv[:, kc, t0:t0 + tn]) for kc in range(4)], r=[wkv, akv], w=[ps])
                if bi % 2 == 0:
                    P.op("act", lambda e, ps=ps, kn=kn, t0=t0, tn=tn: e.copy(out=kn[:, t0:t0 + tn], in_=ps[:, 0:tn]), r=[ps], w=[kn])
                else:
                    P.op("dve", lambda e, ps=ps, kn=kn, t0=t0, tn=tn: e.tensor_copy(out=kn[:, t0:t0 + tn], in_=ps[:, 0:tn]), r=[ps], w=[kn])
            P.dma("sp", g.knT[h * 128:(h + 1) * 128, :], kn[:], r=[kn])
        vc = 0
        for hg in range(4):
            for tt in range(T // 128):
                ps = pss[pc % 6]
                pc += 1
                pairs = []
                for kc in range(4):
                    rhs = wkv[:, kc, hg * 1024:(hg + 1) * 1024].rearrange("p (h two d) -> p h two d", h=4, two=2)[:, :, 1, :]
                    pairs.append((akv[:, kc, tt * 128:(tt + 1) * 128], rhs))
                mm_group(P, ps[:, 0:512].rearrange("p (h d) -> p h d", h=4), pairs, r=[wkv, akv], w=[ps])
                st = vst[vc % 3]
                vc += 1
                if vc % 2 == 0:
                    P.op("act", lambda e, ps=ps, st=st: e.copy(out=st[:], in_=ps[:, 0:512]), r=[ps], w=[st])
                else:
                    P.op("dve", lambda e, ps=ps, st=st: e.tensor_copy(out=st[:], in_=ps[:, 0:512]), r=[ps], w=[st])
                P.dma("sp", g.v1[tt * 128:(tt + 1) * 128, hg * 512:(hg + 1) * 512], st[:], r=[st])


def phase_mla_attn(P, g):
    with P.phase():
        kpe = P.sb("kpe", [128, T], BF16)
        P.op("dve", lambda e: e.memset(kpe[64:128, :], 0.0), w=[kpe])
        P.dma("sp", kpe[0:64, :], g.kpeT[:, :], w=[kpe])
        convert_mlp_weights(P, g, 1)
        qn_ = [P.sb("qn%d" % i, [128, S], BF16) for i in range(2)]
        qr_ = [P.sb("qr%d" % i, [128, S], BF16) for i in range(2)]
        for t_ in qr_:
            P.op("dve", lambda e, t_=t_: e.memset(t_[64:128, :], 0.0), w=[t_])
        kn_ = [P.sb("kn%d" % i, [128, T], BF16) for i in range(2)]
        v_ = [P.sb("v%d" % i, [128, 18, 128], BF16) for i in range(2)]
        o_ = [P.sb("o%d" % i, [128, S], BF16) for i in range(2)]
        sps = [P.ps("s%d" % i, [128, 512]) for i in range(4)]
        ops_ = [P.ps("o%d" % i, [128, 512]) for i in range(2)]
        dps = [P.ps("d%d" % i, [128, 512]) for i in range(2)]
        pts = [P.sb("pt%d" % i, [128, 512], BF16) for i in range(4)]
        rdn = [P.sb("rd%d" % i, [128, 512], F32) for i in range(2)]
        A, B = [], []
        it = 0
        for h in range(16):
            qn, qr, kn, v, o = qn_[h % 2], qr_[h % 2], kn_[h % 2], v_[h % 2], o_[h % 2]
            for qb in range(4):
                op_, dp, rd = ops_[(h * 4 + qb) % 2], dps[(h * 4 + qb) % 2], rdn[(h * 4 + qb) % 2]
                for kc in range(18):
                    sp, pt = sps[it % 4], pts[it % 4]
                    it += 1

                    def stepA(h=h, qb=qb, kc=kc, sp=sp, pt=pt, qn=qn, qr=qr, kn=kn, v=v):
                        if qb == 0 and kc == 0:
                            P.dma("sp", qn[:], g.qnT[h * 128:(h + 1) * 128, :], w=[qn])
                            P.dma("sp", qr[0:64, :], g.qrT[h * 64:(h + 1) * 64, :], w=[qr])
                            P.dma("sp", kn[:], g.knT[h * 128:(h + 1) * 128, :], w=[kn])
                            P.dma("sp", v[:], g.v1[:, h * 128:(h + 1) * 128].rearrange("(c p) d -> p c d", p=128), w=[v])

                        def fs(e):
                            e.matmul(sp[:, :], kn[:, kc * 128:(kc + 1) * 128], qn[:, qb * 512:(qb + 1) * 512], start=True, stop=False)
                            return e.matmul(sp[:, :], kpe[:, kc * 128:(kc + 1) * 128], qr[:, qb * 512:(qb + 1) * 512], start=False, stop=True)
                        P.op("pe", fs, r=[kn, qn, kpe, qr], w=[sp])
                        P.op("act", lambda e: e.activation(out=pt[:], in_=sp[:], func=AF.Exp), r=[sp], w=[pt])

                    def stepB(h=h, qb=qb, kc=kc, pt=pt, v=v, o=o, op_=op_, dp=dp, rd=rd):
                        def fo(e):
                            e.matmul(op_[:], v[:, kc, :], pt[:], start=(kc == 0), stop=(kc == 17))
                            return e.matmul(dp[:], g.onesB[:], pt[:], start=(kc == 0), stop=(kc == 17))
                        P.op("pe", fo, r=[pt, v], w=[op_, dp])
                        if kc == 17:
                            P.op("dve", lambda e: e.reciprocal(out=rd[:], in_=dp[:]), r=[dp], w=[rd])
                            P.op("dve", lambda e: e.tensor_tensor(out=o[:, qb * 512:(qb + 1) * 512], in0=op_[:], in1=rd[:], op=ALU.mult),
                                 r=[op_, rd], w=[o])
                            if qb == 3:
                                P.dma("sp", g.attT[h * 128:(h + 1) * 128, :], o[:], r=[o])
                    A.append(stepA)
                    B.append(stepB)
        n = len(A)
        LAG = 2
        for i in range(n + LAG):
            if i < n:
                A[i]()
            if i >= LAG:
                B[i - LAG]()


def phase_final(P, g):
    with P.phase():
        xb = [P.sb("fx%d" % i, [128, 16, 512], F32) for i in range(2)]
        sq = [P.sb("fsq%d" % i, [128, 512], BF16) for i in range(2)]
        rs = [P.sb("frs%d" % i, [128, 512], F32) for i in range(2)]
        rstd = [P.sb("frstd%d" % i, [128, 512], F32) for i in range(2)]
        ot = [P.sb("fo%d" % i, [128, D], F32) for i in range(2)]
        ssp = [P.ps("fss%d" % i, [128, 512]) for i in range(2)]
        tps = [P.ps("ftp%d" % i, [128, 512]) for i in range(4)]
        pc = 0
        oc = 0
        for bi, (t0, tn) in enumerate(blocks(0, S, 512)):
            x_ = xb[bi % 2]
            ss = ssp[bi % 2]
            P.dma("sp", x_[:, :, 0:tn], g.xT[:, t0:t0 + tn].rearrange("(c p) t -> p c t", p=128), w=[x_])
            for c in range(16):
                s_ = sq[c % 2]
                P.op("act", lambda e, s_=s_, x_=x_, c=c, tn=tn: e.activation(out=s_[:, 0:tn], in_=x_[:, c, 0:tn], func=AF.Square),
                     r=[x_], w=[s_])
                P.op("pe", lambda e, ss=ss, s_=s_, c=c, tn=tn: e.matmul(ss[:, 0:tn], g.onesB[:], s_[:, 0:tn], start=(c == 0), stop=(c == 15)),
                     r=[s_], w=[ss])
            r_ = rs[bi % 2]
            rd_ = rstd[bi % 2]
            P.op("act", lambda e, r_=r_, ss=ss, tn=tn: e.activation(out=r_[:, 0:tn], in_=ss[:, 0:tn], func=AF.Sqrt,
                                                                      bias=g.vecs[:, g.V_EPS:g.V_EPS + 1], scale=1.0 / D),
                 r=[ss], w=[r_])
            P.op("dve", lambda e, r_=r_, rd_=rd_, tn=tn: e.reciprocal(out=rd_[:, 0:tn], in_=r_[:, 0:tn]), r=[r_], w=[rd_])
            for c in range(16):
                P.op("dve", lambda e, x_=x_, rd_=rd_, c=c, tn=tn: e.scalar_tensor_tensor(
                    out=x_[:, c, 0:tn], in0=x_[:, c, 0:tn], scalar=g.vecs[:, g.V_NFIN + c:g.V_NFIN + c + 1], in1=rd_[:, 0:tn],
                    op0=ALU.mult, op1=ALU.mult), r=[x_, rd_], w=[x_])
            for jt in range(tn // 128):
                o_ = ot[oc % 2]
                oc += 1
                for cg in range(4):
                    ps = tps[pc % 4]
                    pc += 1

                    def fn(e, ps=ps, x_=x_, jt=jt, cg=cg):
                        ins = None
                        for cc in range(4):
                            c = cg * 4 + cc
                            ins = e.transpose(ps[:, cc * 128:(cc + 1) * 128], x_[:, c, jt * 128:(jt + 1) * 128], g.identF[:])
                        return ins
                    P.op("pe", fn, r=[x_], w=[ps])
                    if cg % 2 == 0:
                        P.op("act", lambda e, ps=ps, o_=o_, cg=cg: e.copy(out=o_[:, cg * 512:(cg + 1) * 512], in_=ps[:, :]), r=[ps], w=[o_])
                    else:
                        P.op("dve", lambda e, ps=ps, o_=o_, cg=cg: e.tensor_copy(out=o_[:, cg * 512:(cg + 1) * 512], in_=ps[:, :]), r=[ps], w=[o_])
                P.dma("pool", g.d_out[t0 + jt * 128:t0 + (jt + 1) * 128, :], o_[:], r=[o_])


DEBUG_OUT = set()
STOP_AFTER = None


def build_program():
    nc = bass.Bass("TRN2", target_bir_lowering=False)
    g = G()

    def din(name, shape, dt=F32):
        return nc.dram_tensor(name, shape, dt, kind="ExternalInput").ap()

    def scratch(name, shape, dt):
        kind = "ExternalOutput" if name in DEBUG_OUT else "Internal"
        return nc.dram_tensor(name, shape, dt, kind=kind).ap()

    g.d_x = din("x", [S, D])
    g.d_ctx = din("ctx", [C, D])
    g.d_mod_w = din("mod_w", [2, D, 6 * D])
    g.d_identF = din("identF", [128, 128])
    g.d_w_in = din("ab_w_in", [D, 5120])
    g.d_w_out = din("ab_w_out", [D, D])
    g.d_w1 = din("mlp_w1", [2, D, DFF])
    g.d_w2 = din("mlp_w2", [2, DFF, D])
    g.d_nabias = din("nabias", [128, 8 * 16 * 64])
    g.d_lru_wa = din("lru_wa", [2, 8, 128, 128])
    g.d_lru_wx = din("lru_wx", [2, 8, 128, 128])
    g.d_w_dq = din("mla_w_dq", [D, 512])
    g.d_w_dkv = din("mla_w_dkv", [D, 576])
    g.d_w_uq = din("mla_w_uq", [512, 3072])
    g.d_w_ukv = din("mla_w_ukv", [512, 4096])
    g.d_w_o = din("mla_w_o", [D, D])
    g.d_rope = din("rope", [64, 2 * S])
    g.d_w_dkv_sw = din("w_dkv_sw", [D, 64])
    g.d_w_uq_sw = din("w_uq_sw", [512, 1024])
    off = 0
    for nm, n in (("V_CV", 32), ("V_MODB", 192), ("V_NMIX", 32), ("V_NMLP", 32), ("V_NFIN", 16), ("V_EPS", 1), ("V_ONE", 1),
                  ("V_QN", 4), ("V_KVN", 4)):
        setattr(g, nm, off)
        off += n
    g.NV = off
    g.d_vecs = din("vecs", [128, g.NV])
    off = 0
    for nm, n in (("LV_LAM", 16), ("LV_CW", 32), ("LV_CB", 8), ("LV_BA", 16), ("LV_BX", 16)):
        setattr(g, nm, off)
        off += n
    g.NLV = off
    g.d_lvecs = din("lvecs", [128, g.NLV])
    g.d_out = nc.dram_tensor("out", [S, D], F32, kind="ExternalOutput").ap()

    g.xT = scratch("xT", [D, T], F32)
    g.hT = scratch("hT", [D, T], BF16)
    g.qT = scratch("qT", [1024, T], BF16)
    g.kT = scratch("kT", [1024, T], BF16)
    g.v0 = scratch("v0", [T, 1024], BF16)
    g.uT = scratch("uT", [1024, T], F32)
    g.ggT = scratch("ggT", [1024, T], F32)
    g.mixT = scratch("mixT", [D, T], BF16)
    g.cqT = scratch("cqT", [512, S], F32)
    g.ckvT = scratch("ckvT", [640, T], F32)
    g.cqnT = scratch("cqnT", [512, S], BF16)
    g.ckvnT = scratch("ckvnT", [512, T], BF16)
    g.kpeT = scratch("kpeT", [64, T], BF16)
    g.qnT = scratch("qnT", [2048, S], BF16)
    g.qrT = scratch("qrT", [1024, S], BF16)
    g.knT = scratch("knT", [2048, T], BF16)
    g.v1 = scratch("v1", [T, 2048], BF16)
    g.attT = scratch("attT", [D, S], BF16)
    g.W1r = scratch("W1r", [2, 32, 128, 16, 256], BF16)
    g.W2r = scratch("W2r", [2, 16, 128, 64, 128], BF16)

    es = ExitStack()
    with es:
        def psb(name, shape, dt):
            t = es.enter_context(nc.sbuf_tensor(name, shape, dt))
            return t, Tl(t)
        g.identF, g.t_identF = psb("identF_sb", [128, 128], F32)
        g.identB, g.t_identB = psb("identB_sb", [128, 128], BF16)
        g.onesB, g.t_onesB = psb("onesB_sb", [128, 128], BF16)
        g.vecs, g.t_vecs = psb("vecs_sb", [128, g.NV], F32)
        g.lvecs, g.t_lvecs = psb("lvecs_sb", [128, g.NLV], F32)
        g.MOD, g.t_MOD = psb("MOD_sb", [128, 2, 6, 2, 16], F32)
        P = Prog(nc)
        steps = [
            ("consts", lambda: phase_consts(P, g)),
            ("tin", lambda: phase_transpose_in(P, g)),
            ("mod", lambda: phase_modulation(P, g, [(0, 0), (0, 1)])),
            ("norm0a", lambda: norm_mod(P, g, 0, 1, 0, T)),
            ("inproj0", lambda: phase_inproj0(P, g)),
            ("nalru", lambda: phase_na_lru(P, g)),
            ("outproj0", lambda: phase_outproj(P, g, 0, g.d_w_out, g.mixT, T)),
            ("norm0b", lambda: norm_mod(P, g, 0, 4, 3, T)),
            ("mlp0", lambda: phase_mlp(P, g, 0, T)),
            ("norm1a", lambda: norm_mod(P, g, 1, 1, 0, T)),
            ("mladown", lambda: phase_mla_down(P, g)),
            ("mlanq", lambda: phase_norm(P, g, g.cqT, g.cqnT, 4, blocks(0, S, 512), lambda c, j: g.vecs[:, g.V_QN + c:g.V_QN + c + 1], None)),
            ("mlankv", lambda: phase_norm(P, g, g.ckvT, g.ckvnT, 4, blocks(0, T, 512), lambda c, j: g.vecs[:, g.V_KVN + c:g.V_KVN + c + 1], None)),
            ("mlaup", lambda: phase_mla_up(P, g)),
            ("mlaattn", lambda: phase_mla_attn(P, g)),
            ("outproj1", lambda: phase_outproj(P, g, 1, g.d_w_o, g.attT, S)),
            ("norm1b", lambda: norm_mod(P, g, 1, 4, 3, S)),
            ("mlp1", lambda: phase_mlp(P, g, 1, S)),
            ("final", lambda: phase_final(P, g)),
        ]
        for name, fn in steps:
            fn()
            if STOP_AFTER == name:
                break
    return nc


def pack_vecs(inp, b):
    def fm(v):
        return np.ascontiguousarray(np.asarray(v, np.float32).reshape(-1, 128).T)
    cols = [fm(inp["c"][b]), fm(inp["c_ctx"])]
    for L in range(2):
        cols.append(fm(inp["mod_b"][L]))
    for L in range(2):
        cols.append(fm(inp["norm_mix"][L]))
    for L in range(2):
        cols.append(fm(inp["norm_mlp"][L]))
    cols.append(fm(inp["final_norm"]))
    cols.append(np.full((128, 1), EPS, np.float32))
    cols.append(np.ones((128, 1), np.float32))
    cols.append(fm(inp["mla_q_norm"][0]))
    cols.append(fm(inp["mla_kv_norm"][0]))
    return np.ascontiguousarray(np.concatenate(cols, axis=1))


def pack_lvecs(inp):
    def fm(v):
        return np.ascontiguousarray(np.asarray(v, np.float32).reshape(-1, 128).T)
    cols = [fm(inp["lru_lambda"][0].reshape(-1))]
    cols.append(fm(inp["lru_conv_w"][0].reshape(-1)))
    cols.append(fm(inp["lru_conv_b"][0]))
    cols.append(fm(inp["lru_ba"][0].reshape(-1)))
    cols.append(fm(inp["lru_bx"][0].reshape(-1)))
    return np.ascontiguousarray(np.concatenate(cols, axis=1))


def rope_table():
    pos = np.arange(S)
    row = (pos // 64).astype(np.float32)
    col = (pos % 64).astype(np.float32)
    inv = np.power(np.float32(10000.0), -np.arange(0, 32, 2, dtype=np.float32) / np.float32(32)).astype(np.float32)
    out = np.zeros((64, 2, S), np.float32)
    for axis, pv in enumerate((row, col)):
        ang = (pv[None, :] * inv[:, None]).astype(np.float32)
        for half in range(2):
            p0 = axis * 32 + half * 16
            out[p0:p0 + 16, 0, :] = np.cos(ang)
            out[p0:p0 + 16, 1, :] = -np.sin(ang) if half == 0 else np.sin(ang)
    return np.ascontiguousarray(out.reshape(64, 2 * S))


def make_in_maps(inp, cores):
    ident = np.eye(128, dtype=np.float32)
    nab = make_nabias(np.asarray(inp["na_rpb"][0], np.float32))
    lv = pack_lvecs(inp)
    rope = rope_table()
    perm = np.array([(p + 16) if (p % 32) < 16 else (p - 16) for p in range(64)])
    w_dkv_sw = np.ascontiguousarray(inp["mla_w_dkv"][0][:, 512 + perm])
    uq = inp["mla_w_uq"][0].reshape(512, 16, 192)
    w_uq_sw = np.ascontiguousarray(uq[:, :, 128 + perm].reshape(512, 1024))
    shared = {
        "w_dkv_sw": w_dkv_sw, "w_uq_sw": w_uq_sw,
        "mod_w": inp["mod_w"], "identF": ident, "ab_w_in": inp["ab_w_in"][0], "ab_w_out": inp["ab_w_out"][0],
        "mlp_w1": inp["mlp_w1"], "mlp_w2": inp["mlp_w2"], "nabias": nab, "lvecs": lv,
        "lru_wa": inp["lru_wa"][0], "lru_wx": inp["lru_wx"][0],
        "mla_w_dq": inp["mla_w_dq"][0], "mla_w_dkv": inp["mla_w_dkv"][0], "mla_w_uq": inp["mla_w_uq"][0],
        "mla_w_ukv": inp["mla_w_ukv"][0], "mla_w_o": inp["mla_w_o"][0], "rope": rope,
    }
    maps = []
    for b in cores:
        m = dict(shared)
        m["x"] = np.ascontiguousarray(inp["x"][b])
        m["ctx"] = np.ascontiguousarray(inp["ctx"][b])
        m["vecs"] = pack_vecs(inp, b)
        maps.append(m)
    return maps


def kernel(**inputs):
    inp = {k: np.asarray(v) for k, v in inputs.items()}
    nc = build_program()
    maps = make_in_maps(inp, list(range(NCORES)))
    res = run_bass_kernel_spmd(nc, maps, core_ids=list(range(NCORES)))
    out = np.stack([res.results[i]["out"] for i in range(NCORES)], axis=0)
    return out.astype(np.float32)
```

```python
import numpy as np
from contextlib import ExitStack
import concourse.bass as bass
import concourse.mybir as mybir
from concourse.bass_utils import run_bass_kernel_spmd
from concourse.alu_op_type import AluOpType as ALU

F32 = mybir.dt.float32
BF16 = mybir.dt.bfloat16
AF = mybir.ActivationFunctionType

D = 2048
S = 2048
C = 256
T = S + C
DFF = 8192
EPS = 1e-6
NCORES = 8

SELF_SYNC = True
ENGS = ["pe", "act", "dve", "pool", "sp"]
BLK = {"pe": "tensor", "act": "scalar", "dve": "vector", "pool": "gpsimd", "sp": "sync"}
NDMA = {"sp": 12, "pool": 4, "act": 2}


class Tl:
    __slots__ = ("t", "lw", "rd", "rd_dma")

    def __init__(self, t):
        self.t = t
        self.lw = None
        self.rd = {}
        self.rd_dma = []

    def __getitem__(self, i):
        return self.t[i]


class Op:
    __slots__ = ("eng", "fn", "deps", "inc", "sem", "ticket", "prev", "dma")

    def __init__(self, eng, fn, dma):
        self.eng = eng
        self.fn = fn
        self.dma = dma
        self.inc = dma
        self.deps = ()
        self.sem = None
        self.ticket = 0
        self.prev = 0


class Prog:
    def __init__(self, nc):
        self.nc = nc
        self.sems = {e: nc.alloc_semaphore("s_" + e) for e in ["pe", "act", "dve", "pool"]}
        self.cnt = {e: 0 for e in self.sems}
        self.dsem = {e: [nc.alloc_semaphore("d_%s%d" % (e, i)) for i in range(n)] for e, n in NDMA.items()}
        self.dcnt = {e: [0] * n for e, n in NDMA.items()}
        self.drr = {e: 0 for e in NDMA}
        self.known = {e: {} for e in ENGS}
        self.ops = {e: [] for e in ENGS}
        self.stack = None
        self.nphase = 0

    def sb(self, name, shape, dt):
        return Tl(self.stack.enter_context(self.nc.sbuf_tensor("%s_%d" % (name, self.nphase), shape, dt)))

    def ps(self, name, shape, dt=F32):
        return Tl(self.stack.enter_context(self.nc.psum_tensor("P%s_%d" % (name, self.nphase), shape, dt)))

    def op(self, eng, fn, r=(), w=(), dma=False):
        o = Op(eng, fn, dma)
        deps = []
        for t in r:
            if t.lw is not None:
                deps.append(t.lw)
        for t in w:
            if t.lw is not None:
                deps.append(t.lw)
            deps.extend(t.rd.values())
            deps.extend(t.rd_dma)
        out = []
        seen = set()
        for d in deps:
            if d is o or id(d) in seen:
                continue
            seen.add(id(d))
            if d.eng == eng and not d.dma and not dma and (eng == "pe" or not SELF_SYNC):
                continue
            out.append(d)
            d.inc = True
        o.deps = out
        for t in w:
            t.lw = o
            t.rd = {}
            t.rd_dma = []
        for t in r:
            if dma:
                t.rd_dma.append(o)
            else:
                t.rd[eng] = o
        self.ops[eng].append(o)
        return o

    def dma(self, q, out, in_, r=(), w=()):
        return self.op(q, lambda e: e.dma_start(out=out, in_=in_), r=r, w=w, dma=True)

    def phase(self):
        return _Phase(self)

    def _wait(self, e, known, sem, val):
        if val <= 0:
            return
        k = id(sem)
        if known.get(k, 0) >= val:
            return
        e.wait_ge(sem, val)
        known[k] = val

    def flush(self):
        for eng in ENGS:
            for o in self.ops[eng]:
                if o.dma:
                    n = len(self.dsem[eng])
                    k = self.drr[eng] % n
                    self.drr[eng] += 1
                    o.sem = self.dsem[eng][k]
                    o.prev = self.dcnt[eng][k]
                    self.dcnt[eng][k] += 16
                    o.ticket = self.dcnt[eng][k]
                elif o.inc:
                    self.cnt[eng] += 1
                    o.sem = self.sems[eng]
                    o.ticket = self.cnt[eng]
        with self.nc.Block() as block:
            for eng in ENGS:
                ops = self.ops[eng]
                if not ops:
                    continue

                def body(e, ops=ops, eng=eng):
                    known = self.known[eng]
                    for o in ops:
                        for d in o.deps:
                            self._wait(e, known, d.sem, d.ticket)
                        if o.dma:
                            self._wait(e, known, o.sem, o.prev)
                        ins = o.fn(e)
                        if o.dma:
                            ins.then_inc(o.sem, 16)
                        elif o.inc:
                            ins.then_inc(o.sem, 1)
                    if eng in self.dsem:
                        for k, s in enumerate(self.dsem[eng]):
                            self._wait(e, known, s, self.dcnt[eng][k])

                getattr(block, BLK[eng])(body)
        self.ops = {e: [] for e in ENGS}
        self.nphase += 1


class _Phase:
    def __init__(self, P):
        self.P = P

    def __enter__(self):
        self.P.stack = ExitStack()
        self.P.stack.__enter__()
        return self.P

    def __exit__(self, et, ev, tb):
        if et is None:
            self.P.flush()
        self.P.stack.__exit__(et, ev, tb)
        self.P.stack = None
        return False


def tok_blocks(total, bs):
    out = []
    t = 0
    while t < total:
        lim = S if t < S else total
        n = min(bs, lim - t)
        out.append((t, n))
        t += n
    return out


class G:
    pass


def mm_group(P, ps_ap, pairs, r, w):
    n = len(pairs)

    def fn(e):
        ins = None
        for i, (l, rr) in enumerate(pairs):
            ins = e.matmul(ps_ap, l, rr, start=(i == 0), stop=(i == n - 1))
        return ins

    return P.op("pe", fn, r=r, w=w)


def linear_B(P, W_ap, KC, act, tblocks, col0, ncols, CG, wbufs, psums, evac, m_done=None, ctr=None, wsrc=None, wq="pool", tick=None):
    if ctr is None:
        ctr = [0, 0]
    ng = (ncols + CG - 1) // CG
    for g in range(ng):
        c0 = col0 + g * CG
        cg = min(CG, col0 + ncols - c0)
        wt = wbufs[ctr[0] % len(wbufs)]
        ctr[0] += 1
        if wsrc is not None:
            src = wsrc(g)
        else:
            src = W_ap[0:KC * 128, c0:c0 + cg].rearrange("(kc p) m -> p kc m", p=128)
        P.dma(wq, wt[:, 0:KC, 0:cg], src, w=[wt])
        for ml in range(cg // 128):
            mi = (c0 - col0) // 128 + ml
            for (t0, tn) in tblocks:
                ps = psums[ctr[1] % len(psums)]
                ctr[1] += 1
                pairs = [(wt[:, kc, ml * 128:(ml + 1) * 128], act[:, kc, t0:t0 + tn]) for kc in range(KC)]
                mm_group(P, ps[:, 0:tn], pairs, r=[wt, act], w=[ps])
                evac(mi, t0, tn, ps)
            if m_done is not None:
                m_done(mi)
            if tick is not None:
                tick()
    return ctr


def linear_A(P, W_ap, KC, act, ttiles, col0, ncols, wbufs, psums, evac, ctr=None, tick=None):
    if ctr is None:
        ctr = [0, 0]
    ng = ncols // 512
    for g in range(ng):
        c0 = col0 + g * 512
        wt = wbufs[ctr[0] % len(wbufs)]
        ctr[0] += 1
        src = W_ap[0:KC * 128, c0:c0 + 512].rearrange("(kc p) m -> p kc m", p=128)
        P.dma("pool", wt[:, 0:KC, 0:512], src, w=[wt])
        for (t0, tn) in ttiles:
            ps = psums[ctr[1] % len(psums)]
            ctr[1] += 1
            pairs = [(act[:, kc, t0:t0 + tn], wt[:, kc, 0:512]) for kc in range(KC)]
            mm_group(P, ps[0:tn, 0:512], pairs, r=[wt, act], w=[ps])
            evac(g, t0, tn, ps)
            if tick is not None:
                tick()
    return ctr


def phase_consts(P, g):
    with P.phase():
        P.dma("sp", g.identF[:], g.d_identF, w=[g.t_identF])
        P.dma("pool", g.identB[:], g.d_identF, w=[g.t_identB])
        P.dma("sp", g.vecs[:], g.d_vecs, w=[g.t_vecs])
        P.dma("sp", g.lvecs[:], g.d_lvecs, w=[g.t_lvecs])
        P.op("dve", lambda e: e.memset(g.onesB[:], 1.0), w=[g.t_onesB])


def phase_transpose_in(P, g):
    with P.phase():
        xin = [P.sb("xin%d" % i, [128, 4, D], F32) for i in range(2)]
        xs = [P.sb("xs%d" % i, [128, 16, 512], F32) for i in range(2)]
        pss = [P.ps("tp%d" % i, [128, 512]) for i in range(4)]
        groups = [(g.d_x, tg * 512, tg * 512, 4) for tg in range(4)] + [(g.d_ctx, 0, S, 2)]
        pc = 0
        for gi, (src, s0, t0, nt) in enumerate(groups):
            xi = xin[gi % 2]
            xo = xs[gi % 2]
            P.dma("sp", xi[:, 0:nt, :], src[s0:s0 + nt * 128, :].rearrange("(j p) d -> p j d", p=128), w=[xi])
            for c in range(16):
                ps = pss[pc % 4]
                pc += 1

                def fn(e, ps=ps, xi=xi, c=c, nt=nt):
                    ins = None
                    for j in range(nt):
                        ins = e.transpose(ps[:, j * 128:(j + 1) * 128], xi[:, j, c * 128:(c + 1) * 128], g.identF[:])
                    return ins

                P.op("pe", fn, r=[xi], w=[ps])
                if c % 2 == 0:
                    P.op("act", lambda e, ps=ps, xo=xo, c=c, nt=nt: e.copy(out=xo[:, c, 0:nt * 128], in_=ps[:, 0:nt * 128]),
                         r=[ps], w=[xo])
                else:
                    P.op("dve", lambda e, ps=ps, xo=xo, c=c, nt=nt: e.tensor_copy(out=xo[:, c, 0:nt * 128], in_=ps[:, 0:nt * 128]),
                         r=[ps], w=[xo])
            P.dma("pool", g.xT[:, t0:t0 + nt * 128].rearrange("(c p) t -> p c t", p=128), xo[:, :, 0:nt * 128], r=[xo])


def mod_gen(P, g, slots, GW=512):
    scond = P.sb("scond", [128, 16, 2], BF16)
    wb = [P.sb("mw%d" % i, [128, 16, GW], BF16) for i in range(2)]
    mtp = [P.ps("mtp%d" % i, [128, 16, 2]) for i in range(2)]
    cpg = GW // 128
    for j in range(2):
        P.op("act", lambda e, j=j: e.activation(out=scond[:, :, j], in_=g.vecs[:, g.V_CV + j * 16:g.V_CV + (j + 1) * 16],
                                                 func=AF.Silu), w=[scond])
    gi = 0
    for si, (L, k) in enumerate(slots):
        mt = mtp[si % 2]
        for q in range(2048 // GW):
            wt = wb[gi % 2]
            gi += 1
            c0 = k * 2048 + q * GW
            P.dma("pool", wt[:], g.d_mod_w[L, :, c0:c0 + GW].rearrange("(kc p) m -> p kc m", p=128), w=[wt])

            def fn(e, wt=wt, q=q, mt=mt):
                ins = None
                for cc in range(cpg):
                    for kc in range(16):
                        ins = e.matmul(mt[:, q * cpg + cc, :], wt[:, kc, cc * 128:(cc + 1) * 128], scond[:, kc, :],
                                       start=(kc == 0), stop=(kc == 15))
                return ins
            P.op("pe", fn, r=[wt, scond], w=[mt])
            yield
        for j in range(2):
            P.op("dve", lambda e, L=L, k=k, j=j, mt=mt: e.tensor_tensor(
                out=g.MOD[:, L, k, j, :], in0=mt[:, :, j],
                in1=g.vecs[:, g.V_MODB + L * 96 + k * 16:g.V_MODB + L * 96 + (k + 1) * 16], op=ALU.add), r=[mt], w=[g.t_MOD])
        if k in (1, 4):
            voff = g.V_NMIX if k == 1 else g.V_NMLP
            for j in range(2):
                P.op("dve", lambda e, L=L, j=j, k=k, voff=voff: e.scalar_tensor_tensor(
                    out=g.MOD[:, L, k, j, :], in0=g.MOD[:, L, k, j, :], scalar=1.0,
                    in1=g.vecs[:, voff + L * 16:voff + (L + 1) * 16], op0=ALU.add, op1=ALU.mult),
                    r=[g.t_MOD], w=[g.t_MOD])
        yield


def phase_modulation(P, g, slots):
    with P.phase():
        for _ in mod_gen(P, g, slots):
            pass


def phase_norm(P, g, src, dst, nC, tblocks, A_fn, B_fn):
    Dn = nC * 128
    with P.phase():
        xb = [P.sb("nx%d" % i, [128, nC, 512], F32) for i in range(3)]
        hb = [P.sb("nh%d" % i, [128, nC, 512], BF16) for i in range(2)]
        sq = [P.sb("nsq%d" % i, [128, 512], BF16) for i in range(3)]
        tmp = [P.sb("ntmp%d" % i, [128, 512], F32) for i in range(3)]
        rs = [P.sb("nrs%d" % i, [128, 512], F32) for i in range(2)]
        rstd = [P.sb("nrstd%d" % i, [128, 512], F32) for i in range(2)]
        ssp = [P.ps("nss%d" % i, [128, 512]) for i in range(2)]

        def stepA(bi, t0, tn):
            x_ = xb[bi % 3]
            ss = ssp[bi % 2]
            P.dma("sp", x_[:, :, 0:tn], src[0:Dn, t0:t0 + tn].rearrange("(c p) t -> p c t", p=128), w=[x_])
            for c in range(nC):
                s_ = sq[c % 3]
                P.op("act", lambda e, s_=s_, x_=x_, c=c, tn=tn: e.activation(out=s_[:, 0:tn], in_=x_[:, c, 0:tn], func=AF.Square),
                     r=[x_], w=[s_])
                P.op("pe", lambda e, ss=ss, s_=s_, c=c, tn=tn: e.matmul(ss[:, 0:tn], g.onesB[:], s_[:, 0:tn], start=(c == 0), stop=(c == nC - 1)),
                     r=[s_], w=[ss])
            r_ = rs[bi % 2]
            rd_ = rstd[bi % 2]
            P.op("act", lambda e, r_=r_, ss=ss, tn=tn: e.activation(out=r_[:, 0:tn], in_=ss[:, 0:tn], func=AF.Sqrt,
                                                                      bias=g.vecs[:, g.V_EPS:g.V_EPS + 1], scale=1.0 / Dn),
                 r=[ss], w=[r_])
            P.op("dve", lambda e, r_=r_, rd_=rd_, tn=tn: e.reciprocal(out=rd_[:, 0:tn], in_=r_[:, 0:tn]), r=[r_], w=[rd_])

        def stepB(bi, t0, tn):
            j = 0 if t0 < S else 1
            x_ = xb[bi % 3]
            h_ = hb[bi % 2]
            rd_ = rstd[bi % 2]
            for c in range(nC):
                if B_fn is None:
                    P.op("dve", lambda e, h_=h_, x_=x_, rd_=rd_, c=c, tn=tn, j=j: e.scalar_tensor_tensor(
                        out=h_[:, c, 0:tn], in0=x_[:, c, 0:tn], scalar=A_fn(c, j), in1=rd_[:, 0:tn],
                        op0=ALU.mult, op1=ALU.mult), r=[x_, rd_], w=[h_])
                    continue
                t_ = tmp[c % 3]
                P.op("dve", lambda e, t_=t_, x_=x_, rd_=rd_, c=c, tn=tn, j=j: e.scalar_tensor_tensor(
                    out=t_[:, 0:tn], in0=x_[:, c, 0:tn], scalar=A_fn(c, j), in1=rd_[:, 0:tn],
                    op0=ALU.mult, op1=ALU.mult), r=[x_, rd_], w=[t_])
                P.op("act", lambda e, t_=t_, h_=h_, c=c, tn=tn, j=j: e.activation(
                    out=h_[:, c, 0:tn], in_=t_[:, 0:tn], func=AF.Identity, bias=B_fn(c, j), scale=1.0),
                    r=[t_], w=[h_])
            P.dma("pool", dst[0:Dn, t0:t0 + tn].rearrange("(c p) t -> p c t", p=128), h_[:, :, 0:tn], r=[h_])

        nb = len(tblocks)
        for i in range(nb + 1):
            if i < nb:
                stepA(i, *tblocks[i])
            if i >= 1:
                stepB(i - 1, *tblocks[i - 1])


def norm_mod(P, g, L, kA, kB, ttot):
    phase_norm(P, g, g.xT, g.hT, 16, blocks(0, ttot, 512),
               lambda c, j: g.MOD[:, L, kA, j, c:c + 1], lambda c, j: g.MOD[:, L, kB, j, c:c + 1])


def blocks(t_start, t_end, bs):
    out = []
    t = t_start
    while t < t_end:
        lim = S if t < S else t_end
        lim = min(lim, t_end)
        n = min(bs, lim - t)
        out.append((t, n))
        t += n
    return out


def phase_inproj0(P, g):
    SC = 128.0 ** -0.5
    with P.phase():
        act = P.sb("h", [128, 16, T], BF16)
        wb = [P.sb("w%d" % i, [128, 16, 512], BF16) for i in range(2)]
        pss = [P.ps("ps%d" % i, [128, 512]) for i in range(4)]
        stb = [P.sb("stb%d" % i, [128, T], BF16) for i in range(2)]
        stf = [P.sb("stf%d" % i, [128, T], F32) for i in range(2)]
        vst = [P.sb("vst%d" % i, [128, 512], BF16) for i in range(3)]
        P.dma("sp", act[:], g.hT.rearrange("(c p) t -> p c t", p=128), w=[act])
        tb = blocks(0, T, 512)
        ctr = [0, 0]
        cnt = [0]
        mgen = mod_gen(P, g, [(0, 2), (0, 3), (0, 4), (0, 5)])
        tk = [0]

        def tick():
            tk[0] += 1
            if tk[0] % 3 == 0:
                next(mgen, None)

        def mk(kind, dst, stl):
            def evac(mi, t0, tn, ps):
                st = stl[mi % 2]
                cnt[0] += 1
                if kind == "q":
                    P.op("act", lambda e: e.mul(out=st[:, t0:t0 + tn], in_=ps[:, 0:tn], mul=SC), r=[ps], w=[st])
                elif kind == "g":
                    P.op("act", lambda e: e.activation(out=st[:, t0:t0 + tn], in_=ps[:, 0:tn], func=AF.Gelu), r=[ps], w=[st])
                elif cnt[0] % 2 == 0:
                    P.op("act", lambda e: e.copy(out=st[:, t0:t0 + tn], in_=ps[:, 0:tn]), r=[ps], w=[st])
                else:
                    P.op("dve", lambda e: e.tensor_copy(out=st[:, t0:t0 + tn], in_=ps[:, 0:tn]), r=[ps], w=[st])

            def m_done(mi):
                st = stl[mi % 2]
                P.dma("sp", dst[mi * 128:(mi + 1) * 128, :], st[:, :], r=[st])
            return evac, m_done

        W = g.d_w_in
        for kind, col0, dst, stl in (("q", 0, g.qT, stb), ("k", 1024, g.kT, stb), ("u", 3072, g.uT, stf), ("g", 4096, g.ggT, stf)):
            ev, md = mk(kind, dst, stl)
            linear_B(P, W, 16, act, tb, col0, 1024, 512, wb, pss, ev, md, ctr, tick=tick)
        vc = [0]

        def evac_v(gi, t0, tn, ps):
            st = vst[vc[0] % 3]
            vc[0] += 1
            if vc[0] % 2 == 0:
                P.op("act", lambda e: e.copy(out=st[0:tn, :], in_=ps[0:tn, :]), r=[ps], w=[st])
            else:
                P.op("dve", lambda e: e.tensor_copy(out=st[0:tn, :], in_=ps[0:tn, :]), r=[ps], w=[st])
            P.dma("sp", g.v0[t0:t0 + tn, gi * 512:(gi + 1) * 512], st[0:tn, :], r=[st])
        linear_A(P, W, 16, act, [(i * 128, 128) for i in range(T // 128)], 2048, 1024, wb, pss, evac_v, ctr, tick=tick)
        for _ in mgen:
            pass


def na_chunks(r):
    rs = min(max(r - 4, 0), 24)
    if rs % 2 == 0:
        return [(rs + 2 * j, rs + 2 * j - r + 7) for j in range(4)]
    out = [(rs - 1, 14)]
    for j in range(3):
        kr0 = rs + 1 + 2 * j
        out.append((kr0, kr0 - r + 7))
    out.append((rs + 7, 15))
    return out


def na_body(P, g):
    bias = P.sb("nab", [128, 8, 16, 64], BF16)
    P.dma("pool", bias[:], g.d_nabias.rearrange("p (h d q) -> p h d q", h=8, d=16), w=[bias])
    qh = [P.sb("q%d" % i, [128, T], BF16) for i in range(2)]
    kh = [P.sb("k%d" % i, [128, T], BF16) for i in range(2)]
    vh = [P.sb("v%d" % i, [128, 18, 128], BF16) for i in range(2)]
    oh = [P.sb("o%d" % i, [128, T], BF16) for i in range(2)]
    sps = [P.ps("s%d" % i, [128, 512]) for i in range(3)]
    ops_ = [P.ps("o%d" % i, [128, 512]) for i in range(2)]
    pts = [P.sb("pt%d" % i, [128, 512], BF16) for i in range(3)]
    rdn = [P.sb("rd%d" % i, [128, 256], F32) for i in range(2)]
    A, B = [], []
    it = 0
    for h in range(8):
        q_, k_, v_, o_ = qh[h % 2], kh[h % 2], vh[h % 2], oh[h % 2]
        for r in range(33):
            sp, op_, pt, rd = sps[it % 3], ops_[it % 2], pts[it % 3], rdn[it % 2]
            it += 1
            if r < 32:
                chs = na_chunks(r)
                nw = len(chs)
                ncol = (nw + 2) * 64

                def stepA(h=h, r=r, chs=chs, nw=nw, ncol=ncol, sp=sp, pt=pt, q_=q_, k_=k_, v_=v_):
                    if r == 0:
                        P.dma("sp", q_[:], g.qT[h * 128:(h + 1) * 128, :], w=[q_])
                        P.dma("sp", k_[:], g.kT[h * 128:(h + 1) * 128, :], w=[k_])
                        P.dma("sp", v_[:], g.v0[:, h * 128:(h + 1) * 128].rearrange("(c p) d -> p c d", p=128), w=[v_])

                    def fs(e):
                        ins = None
                        qa = q_[:, r * 64:(r + 1) * 64]
                        for i, (kr0, bi) in enumerate(chs):
                            e.matmul(sp[:, i * 64:(i + 1) * 64], k_[:, kr0 * 64:kr0 * 64 + 128], qa, start=True, stop=False)
                            ins = e.matmul(sp[:, i * 64:(i + 1) * 64], g.identB[:], bias[:, h, bi, :], start=False, stop=True)
                        for j in range(2):
                            ins = e.matmul(sp[:, (nw + j) * 64:(nw + j + 1) * 64], k_[:, S + j * 128:S + (j + 1) * 128], qa,
                                           start=True, stop=True)
                        return ins
                    P.op("pe", fs, r=[q_, k_, bias], w=[sp])
                    P.op("act", lambda e: e.activation(out=pt[:, 0:ncol], in_=sp[:, 0:ncol], func=AF.Exp), r=[sp], w=[pt])

                def stepB(h=h, r=r, chs=chs, nw=nw, op_=op_, pt=pt, v_=v_, rd=rd, o_=o_):
                    def fo(e):
                        ins = None
                        cl = [kr0 // 2 for (kr0, _) in chs] + [16, 17]
                        n = len(cl)
                        for i, c in enumerate(cl):
                            e.matmul(op_[:, 0:64], v_[:, c, :], pt[:, i * 64:(i + 1) * 64], start=(i == 0), stop=(i == n - 1))
                        for i, c in enumerate(cl):
                            ins = e.matmul(op_[:, 64:128], g.onesB[:], pt[:, i * 64:(i + 1) * 64], start=(i == 0), stop=(i == n - 1))
                        return ins
                    P.op("pe", fo, r=[pt, v_], w=[op_])
                    P.op("dve", lambda e: e.reciprocal(out=rd[:, 0:64], in_=op_[:, 64:128]), r=[op_], w=[rd])
                    P.op("dve", lambda e: e.tensor_tensor(out=o_[:, r * 64:(r + 1) * 64], in0=op_[:, 0:64], in1=rd[:, 0:64], op=ALU.mult),
                         r=[op_, rd], w=[o_])
            else:
                def stepA(sp=sp, pt=pt, q_=q_, k_=k_):
                    def fsc(e):
                        ins = None
                        for j in range(2):
                            ins = e.matmul(sp[:, j * 256:(j + 1) * 256], k_[:, S + j * 128:S + (j + 1) * 128], q_[:, S:T], start=True, stop=True)
                        return ins
                    P.op("pe", fsc, r=[q_, k_], w=[sp])
                    P.op("act", lambda e: e.activation(out=pt[:, 0:512], in_=sp[:, 0:512], func=AF.Exp), r=[sp], w=[pt])

                def stepB(h=h, op_=op_, pt=pt, v_=v_, rd=rd, o_=o_):
                    def foc(e):
                        ins = None
                        for j in range(2):
                            e.matmul(op_[:, 0:256], v_[:, 16 + j, :], pt[:, j * 256:(j + 1) * 256], start=(j == 0), stop=(j == 1))
                        for j in range(2):
                            ins = e.matmul(op_[:, 256:512], g.onesB[:], pt[:, j * 256:(j + 1) * 256], start=(j == 0), stop=(j == 1))
                        return ins
                    P.op("pe", foc, r=[pt, v_], w=[op_])
                    P.op("dve", lambda e: e.reciprocal(out=rd[:, 0:256], in_=op_[:, 256:512]), r=[op_], w=[rd])
                    P.op("dve", lambda e: e.tensor_tensor(out=o_[:, S:T], in0=op_[:, 0:256], in1=rd[:, 0:256], op=ALU.mult),
                         r=[op_, rd], w=[o_])
                    P.dma("sp", g.mixT[h * 128:(h + 1) * 128, :], o_[:], r=[o_])
            A.append(stepA)
            B.append(stepB)
    n = len(A)
    for i in range(n + 1):
        if i < n:
            A[i]()
        if i >= 1:
            B[i - 1]()
        yield


def phase_na_lru(P, g):
    with P.phase():
        ga = na_body(P, g)
        gb = lru_body(P, g)
        gc = convert_gen(P, g, 0)
        alive_a = alive_b = alive_c = True
        it = 0
        while alive_a or alive_b or alive_c:
            it += 1
            if alive_c and it % 2 == 0:
                try:
                    next(gc)
                except StopIteration:
                    alive_c = False
            if alive_a:
                try:
                    next(ga)
                except StopIteration:
                    alive_a = False
            for _ in range(2):
                if alive_b:
                    try:
                        next(gb)
                    except StopIteration:
                        alive_b = False


def make_nabias(rpb):
    NEG = np.float32(-30000.0)
    kc = np.arange(64)[:, None]
    qc = np.arange(64)[None, :]
    cs = np.clip(qc - 8, 0, 48)
    ok = (kc >= cs) & (kc < cs + 16)
    dc = np.clip(kc - qc, -15, 15) + 15
    out = np.full((2, 64, 8, 16, 64), NEG, np.float32)
    for h in range(8):
        def blk(dr):
            if dr < -7 or dr > 7:
                return np.full((64, 64), NEG, np.float32)
            return np.where(ok, rpb[h, dr + 7][dc], NEG).astype(np.float32)
        for d in range(14):
            for a in range(2):
                out[a, :, h, d, :] = blk(d - 7 + a)
        out[1, :, h, 14, :] = blk(-4)
        out[0, :, h, 15, :] = blk(3)
    return np.ascontiguousarray(out.reshape(128, 8 * 16 * 64))


def lru_body(P, g):
    wg = P.sb("wg", [128, 2, 2, 8, 128], BF16)
    P.dma("pool", wg[:, 0], g.d_lru_wa.rearrange("d n i j -> i d n j"), w=[wg])
    P.dma("pool", wg[:, 1], g.d_lru_wx.rearrange("d n i j -> i d n j"), w=[wg])
    lv = g.lvecs
    nsp = P.sb("nsp", [128, 16], F32)
    e1 = P.sb("e1", [128, 16], F32)
    hb_ = P.sb("hbias", [128, 32], F32)
    c25 = P.sb("c25", [128, 1], F32)
    P.op("act", lambda e: e.activation(out=e1[:], in_=lv[:, g.LV_LAM:g.LV_LAM + 16], func=AF.Exp, scale=-1.0), w=[e1])
    P.op("act", lambda e: e.activation(out=nsp[:], in_=e1[:], func=AF.Ln, bias=g.vecs[:, g.V_ONE:g.V_ONE + 1], scale=1.0),
         r=[e1], w=[nsp])
    P.op("dve", lambda e: e.tensor_scalar(out=nsp[:], in0=nsp[:], scalar1=-4.0, scalar2=None, op0=ALU.mult), r=[nsp], w=[nsp])
    P.op("dve", lambda e: e.tensor_scalar(out=hb_[:], in0=lv[:, g.LV_BA:g.LV_BA + 32], scalar1=0.5, scalar2=None, op0=ALU.mult), w=[hb_])
    P.op("dve", lambda e: e.memset(c25[:], 0.25), w=[c25])
    yield

    def f32t(name, n=2):
        return [P.sb("%s%d" % (name, i), [128, T], F32) for i in range(n)]
    u_, gg_, xc_ = f32t("u", 2), f32t("gg", 1), f32t("xc", 1)
    xcb_ = [P.sb("xcb%d" % i, [128, T], BF16) for i in range(1)]
    gr_, gi_, a_ = f32t("gr", 2), f32t("gi", 2), f32t("a", 2)
    hd_ = f32t("hd", 2)
    y_ = [P.sb("y%d" % i, [128, T], BF16) for i in range(2)]
    pss = [P.ps("lps%d" % i, [128, 512]) for i in range(3)]
    tb = blocks(0, T, 512)
    pc = 0

    def rev(ap_t, lo, hi):
        a = ap_t[:, lo:hi]
        return bass.AP(a.tensor, a.offset + (hi - lo - 1), [list(a.ap[0]), [-1, hi - lo]])

    P.dma("sp", u_[0][:], g.uT[0:128, :], w=[u_[0]])
    for n in range(8):
        u, gg, xc, xcb, y = u_[n % 2], gg_[0], xc_[0], xcb_[0], y_[n % 2]
        if n + 1 < 8:
            P.dma("sp", u_[(n + 1) % 2][:], g.uT[(n + 1) * 128:(n + 2) * 128, :], w=[u_[(n + 1) % 2]])
        cw = lambda tap, n=n: lv[:, g.LV_CW + tap * 8 + n:g.LV_CW + tap * 8 + n + 1]
        P.op("act", lambda e, xc=xc, u=u, n=n, cw=cw: e.activation(out=xc[:], in_=u[:], func=AF.Identity,
                                                                    bias=lv[:, g.LV_CB + n:g.LV_CB + n + 1], scale=cw(2)),
             r=[u], w=[xc])
        yield
        for tap in (0, 1, 3):
            off = tap - 2
            for (s0, s1) in ((0, S), (S, T)):
                d0, d1 = max(s0, s0 - off), min(s1, s1 - off)
                P.op("dve", lambda e, xc=xc, u=u, tap=tap, off=off, d0=d0, d1=d1, cw=cw: e.scalar_tensor_tensor(
                    out=xc[:, d0:d1], in0=u[:, d0 + off:d1 + off], scalar=cw(tap), in1=xc[:, d0:d1],
                    op0=ALU.mult, op1=ALU.add), r=[u, xc], w=[xc])
            yield
        P.op("act", lambda e, xc=xc, xcb=xcb: e.copy(out=xcb[:], in_=xc[:]), r=[xc], w=[xcb])
        yield
        for d in range(2):
            hd, gr, gi, a = hd_[d], gr_[d], gi_[d], a_[d]
            for gt, dst, boff in ((0, gr, 0), (1, gi, 16)):
                for (t0, tn) in tb:
                    ps = pss[pc % 3]
                    pc += 1
                    P.op("pe", lambda e, ps=ps, gt=gt, d=d, n=n, xcb=xcb, t0=t0, tn=tn: e.matmul(
                        ps[:, 0:tn], wg[:, gt, d, n, :], xcb[:, t0:t0 + tn], start=True, stop=True), r=[wg, xcb], w=[ps])
                    P.op("act", lambda e, ps=ps, dst=dst, boff=boff, d=d, n=n, t0=t0, tn=tn: e.activation(
                        out=dst[:, t0:t0 + tn], in_=ps[:, 0:tn], func=AF.Tanh,
                        bias=hb_[:, boff + d * 8 + n:boff + d * 8 + n + 1], scale=0.5), r=[ps, hb_], w=[dst])
                    yield
            P.op("act", lambda e, a=a, gr=gr, d=d, n=n: e.activation(out=a[:], in_=gr[:], func=AF.Exp,
                                                                      bias=nsp[:, d * 8 + n:d * 8 + n + 1],
                                                                      scale=nsp[:, d * 8 + n:d * 8 + n + 1]), r=[gr, nsp], w=[a])
            P.op("dve", lambda e, gi=gi, xc=xc: e.scalar_tensor_tensor(out=gi[:], in0=gi[:], scalar=1.0, in1=xc[:],
                                                                       op0=ALU.add, op1=ALU.mult), r=[gi, xc], w=[gi])
            yield
            P.op("dve", lambda e, a=a, gr=gr: e.tensor_tensor(out=gr[:], in0=a[:], in1=a[:], op=ALU.mult), r=[a, gr], w=[gr])
            yield
            P.op("act", lambda e, gr=gr: e.activation(out=gr[:], in_=gr[:], func=AF.Sqrt, bias=c25[:, 0:1], scale=-0.25),
                 r=[gr, c25], w=[gr])
            yield
            P.op("pool", lambda e, gr=gr, gi=gi: e.tensor_tensor(out=gi[:], in0=gi[:], in1=gr[:], op=ALU.mult),
                 r=[gr, gi], w=[gi])
            yield
            vv = gi
            if d == 0:
                P.op("dve", lambda e, hd=hd, a=a, vv=vv: e.tensor_tensor_scan(
                    out=hd[:, S:T], data0=a[:, S:T], data1=vv[:, S:T], initial=0.0, op0=ALU.mult, op1=ALU.add),
                    r=[a, vv], w=[hd])
                P.op("dve", lambda e, hd=hd, a=a, vv=vv: e.tensor_tensor_scan(
                    out=hd[:, 0:S], data0=a[:, 0:S], data1=vv[:, 0:S], initial=hd[:, T - 1:T], op0=ALU.mult, op1=ALU.add),
                    r=[a, vv, hd], w=[hd])
            else:
                P.op("dve", lambda e, hd=hd, a=a, vv=vv: e.tensor_tensor_scan(
                    out=rev(hd, S, T), data0=rev(a, S, T), data1=rev(vv, S, T), initial=0.0, op0=ALU.mult, op1=ALU.add),
                    r=[a, vv], w=[hd])
                P.op("dve", lambda e, hd=hd, a=a, vv=vv: e.tensor_tensor_scan(
                    out=rev(hd, 0, S), data0=rev(a, 0, S), data1=rev(vv, 0, S), initial=hd[:, S:S + 1], op0=ALU.mult, op1=ALU.add),
                    r=[a, vv, hd], w=[hd])
            yield
        P.dma("sp", gg[:], g.ggT[n * 128:(n + 1) * 128, :], w=[gg])
        P.op("dve", lambda e: e.tensor_tensor(out=hd_[0][:], in0=hd_[0][:], in1=hd_[1][:], op=ALU.add),
             r=[hd_[0], hd_[1]], w=[hd_[0]])
        yield
        P.op("pool", lambda e, y=y, gg=gg: e.tensor_tensor(out=y[:], in0=hd_[0][:], in1=gg[:], op=ALU.mult),
             r=[hd_[0], gg], w=[y])
        P.dma("sp", g.mixT[1024 + n * 128:1024 + (n + 1) * 128, :], y[:], r=[y])
        yield


def phase_outproj(P, g, L, W_ap, src, ttot):
    with P.phase():
        act = P.sb("a", [128, 16, ttot], BF16)
        wb = [P.sb("w%d" % i, [128, 16, 512], BF16) for i in range(2)]
        pss = [P.ps("ps%d" % i, [128, 512]) for i in range(4)]
        xr = [P.sb("xr%d" % i, [128, ttot], F32) for i in range(2)]
        P.dma("sp", act[:], src[:, 0:ttot].rearrange("(c p) t -> p c t", p=128), w=[act])
        tb = blocks(0, ttot, 512)

        def evac(mi, t0, tn, ps):
            x_ = xr[mi % 2]
            if t0 == 0:
                P.dma("sp", x_[:], g.xT[mi * 128:(mi + 1) * 128, 0:ttot], w=[x_])
            j = 0 if t0 < S else 1
            P.op("dve", lambda e: e.scalar_tensor_tensor(out=x_[:, t0:t0 + tn], in0=ps[:, 0:tn], scalar=g.MOD[:, L, 2, j, mi:mi + 1],
                                                          in1=x_[:, t0:t0 + tn], op0=ALU.mult, op1=ALU.add), r=[ps, x_], w=[x_])

        def m_done(mi):
            x_ = xr[mi % 2]
            P.dma("sp", g.xT[mi * 128:(mi + 1) * 128, 0:ttot], x_[:], r=[x_])
        linear_B(P, W_ap, 16, act, tb, 0, D, 512, wb, pss, evac, m_done)


def phase_mlp(P, g, L, ttot):
    with P.phase():
        hb = [P.sb("h%d" % i, [128, 16, 768], BF16) for i in range(1)]
        aT = P.sb("aT", [128, 64, 768], BF16)
        w1b = [P.sb("w1_%d" % i, [128, 16, 256], BF16) for i in range(3)]
        w2b = [P.sb("w2_%d" % i, [128, 64, 128], BF16) for i in range(2)]
        xr = [P.sb("xr%d" % i, [128, 768], F32) for i in range(2)]
        rl = [P.sb("rl%d" % i, [128, 384], F32) for i in range(3)]
        pss = [P.ps("ps%d" % i, [128, 512]) for i in range(6 if L == 1 else 5)]
        passes = []
        t = 0
        while t < ttot:
            n = min(768, ttot - t)
            passes.append((t, n))
            t += n
        c1 = [0, 0]
        c2 = [0, 0]
        rc = [0]
        mgen = mod_gen(P, g, [(1, k) for k in range(6)], GW=128) if L == 0 else None
        tk = [0]

        def tick():
            tk[0] += 1
            if mgen is not None and tk[0] % 2 == 0:
                next(mgen, None)
        for pi, (p0, pn) in enumerate(passes):
            h_ = hb[0]
            P.dma("sp", h_[:, :, 0:pn], g.hT[:, p0:p0 + pn].rearrange("(c p) t -> p c t", p=128), w=[h_])
            tbl = [(t0 - p0, tn) for (t0, tn) in blocks(p0, p0 + pn, 384)]

            def evac1(mi, t0, tn, ps):
                r_ = rl[rc[0] % 3]
                rc[0] += 1
                P.op("act", lambda e: e.activation(out=r_[:, 0:tn], in_=ps[:, 0:tn], func=AF.Relu), r=[ps], w=[r_])
                P.op("dve", lambda e: e.tensor_tensor(out=aT[:, mi, t0:t0 + tn], in0=r_[:, 0:tn], in1=r_[:, 0:tn], op=ALU.mult),
                     r=[r_], w=[aT])
            c1[1] = c2[1] = max(c1[1], c2[1])
            linear_B(P, g.d_w1[L], 16, h_, tbl, 0, DFF, 256, w1b, pss, evac1, None, c1, tick=tick)

            def evac2(mi, t0, tn, ps, p0=p0, pn=pn):
                x_ = xr[mi % 2]
                if t0 == 0:
                    P.dma("sp", x_[:, 0:pn], g.xT[mi * 128:(mi + 1) * 128, p0:p0 + pn], w=[x_])
                j = 0 if p0 + t0 < S else 1
                P.op("dve", lambda e: e.scalar_tensor_tensor(out=x_[:, t0:t0 + tn], in0=ps[:, 0:tn], scalar=g.MOD[:, L, 5, j, mi:mi + 1],
                                                              in1=x_[:, t0:t0 + tn], op0=ALU.mult, op1=ALU.add), r=[ps, x_], w=[x_])

            def m_done2(mi, p0=p0, pn=pn):
                x_ = xr[mi % 2]
                P.dma("sp", g.xT[mi * 128:(mi + 1) * 128, p0:p0 + pn], x_[:, 0:pn], r=[x_])
            c1[1] = c2[1] = max(c1[1], c2[1])
            linear_B(P, None, 64, aT, tbl, 0, D, 128, w2b, pss, evac2, m_done2, c2, wsrc=lambda gi: g.W2r[L, gi], tick=tick)
        if mgen is not None:
            for _ in mgen:
                pass


def convert_gen(P, g, L):
    for mi in range(16):
        for q4 in range(4):
            P.dma("pool", g.W2r[L, mi, :, q4 * 16:(q4 + 1) * 16, :],
                  g.d_w2[L, q4 * 2048:(q4 + 1) * 2048, mi * 128:(mi + 1) * 128].rearrange("(kc p) m -> p kc m", p=128))
            yield


def convert_mlp_weights(P, g, L):
    for _ in convert_gen(P, g, L):
        pass


def phase_mla_down(P, g):
    with P.phase():
        act = P.sb("h", [128, 16, T], BF16)
        wb = [P.sb("w%d" % i, [128, 16, 512], BF16) for i in range(2)]
        wpe = P.sb("wpe", [128, 16, 2, 64], BF16)
        rope = P.sb("rope", [64, 2, S], F32)
        pss = [P.ps("ps%d" % i, [128, 512]) for i in range(6)]
        stf = [P.sb("stf%d" % i, [128, T], F32) for i in range(2)]
        t1 = [P.sb("t1_%d" % i, [64, 512], F32) for i in range(2)]
        t2 = [P.sb("t2_%d" % i, [64, 512], F32) for i in range(2)]
        kst = P.sb("kst", [64, T], BF16)
        P.dma("sp", act[:], g.hT.rearrange("(c p) t -> p c t", p=128), w=[act])
        P.dma("sp", rope[:], g.d_rope.rearrange("p (k t) -> p k t", k=2), w=[rope])
        P.dma("pool", wpe[:, :, 0, :], g.d_w_dkv[:, 512:576].rearrange("(kc p) m -> p kc m", p=128), w=[wpe])
        P.dma("pool", wpe[:, :, 1, :], g.d_w_dkv_sw.rearrange("(kc p) m -> p kc m", p=128), w=[wpe])
        ctr = [0, 0]
        cnt = [0]

        def mk(dst, ttot):
            def evac(mi, t0, tn, ps):
                st = stf[mi % 2]
                cnt[0] += 1
                if cnt[0] % 2 == 0:
                    P.op("act", lambda e: e.copy(out=st[:, t0:t0 + tn], in_=ps[:, 0:tn]), r=[ps], w=[st])
                else:
                    P.op("dve", lambda e: e.tensor_copy(out=st[:, t0:t0 + tn], in_=ps[:, 0:tn]), r=[ps], w=[st])

            def m_done(mi):
                st = stf[mi % 2]
                P.dma("sp", dst[mi * 128:(mi + 1) * 128, 0:ttot], st[:, 0:ttot], r=[st])
            return evac, m_done
        ev, md = mk(g.cqT, S)
        linear_B(P, g.d_w_dq, 16, act, blocks(0, S, 512), 0, 512, 512, wb, pss, ev, md, ctr)
        ev, md = mk(g.ckvT, T)
        linear_B(P, g.d_w_dkv, 16, act, blocks(0, T, 512), 0, 512, 512, wb, pss, ev, md, ctr)
        for bi, (t0, tn) in enumerate(blocks(0, T, 512)):
            psA = pss[ctr[1] % 6]
            ctr[1] += 1
            mm_group(P, psA[0:64, 0:tn], [(wpe[:, kc, 0, :], act[:, kc, t0:t0 + tn]) for kc in range(16)], r=[wpe, act], w=[psA])
            if t0 >= S:
                P.op("act", lambda e, psA=psA, t0=t0, tn=tn: e.copy(out=kst[:, t0:t0 + tn], in_=psA[0:64, 0:tn]), r=[psA], w=[kst])
                continue
            psB = pss[ctr[1] % 6]
            ctr[1] += 1
            mm_group(P, psB[0:64, 0:tn], [(wpe[:, kc, 1, :], act[:, kc, t0:t0 + tn]) for kc in range(16)], r=[wpe, act], w=[psB])
            a_, b_ = t1[bi % 2], t2[bi % 2]
            P.op("dve", lambda e, a_=a_, psA=psA, t0=t0, tn=tn: e.tensor_tensor(out=a_[:, 0:tn], in0=psA[0:64, 0:tn], in1=rope[:, 0, t0:t0 + tn],
                                                                               op=ALU.mult), r=[psA, rope], w=[a_])
            P.op("dve", lambda e, b_=b_, psB=psB, t0=t0, tn=tn: e.tensor_tensor(out=b_[:, 0:tn], in0=psB[0:64, 0:tn], in1=rope[:, 1, t0:t0 + tn],
                                                                               op=ALU.mult), r=[psB, rope], w=[b_])
            P.op("pool", lambda e, a_=a_, b_=b_, t0=t0, tn=tn: e.tensor_tensor(out=kst[:, t0:t0 + tn], in0=a_[:, 0:tn], in1=b_[:, 0:tn], op=ALU.add),
                 r=[a_, b_], w=[kst])
        P.dma("sp", g.kpeT[:, :], kst[:], r=[kst])


def phase_mla_up(P, g):
    SC = 192.0 ** -0.5
    with P.phase():
        aq = P.sb("aq", [128, 4, S], BF16)
        akv = P.sb("akv", [128, 4, T], BF16)
        wq = P.sb("wq", [128, 4, 3072], BF16)
        wqs = P.sb("wqs", [128, 4, 1024], BF16)
        wkv = P.sb("wkv", [128, 4, 4096], BF16)
        rope = P.sb("rope", [64, 2, S], F32)
        pss = [P.ps("ps%d" % i, [128, 512]) for i in range(6)]
        qn_st = [P.sb("qn%d" % i, [128, S], BF16) for i in range(2)]
        qr_st = [P.sb("qr%d" % i, [64, S], BF16) for i in range(2)]
        kn_st = [P.sb("kn%d" % i, [128, T], BF16) for i in range(2)]
        vst = [P.sb("vst%d" % i, [128, 512], BF16) for i in range(3)]
        t1 = [P.sb("t1_%d" % i, [64, 512], F32) for i in range(2)]
        t2 = [P.sb("t2_%d" % i, [64, 512], F32) for i in range(2)]
        P.dma("sp", aq[:], g.cqnT.rearrange("(c p) t -> p c t", p=128), w=[aq])
        P.dma("sp", akv[:], g.ckvnT.rearrange("(c p) t -> p c t", p=128), w=[akv])
        P.dma("sp", rope[:], g.d_rope.rearrange("p (k t) -> p k t", k=2), w=[rope])
        P.dma("pool", wq[:], g.d_w_uq.rearrange("(kc p) m -> p kc m", p=128), w=[wq])
        P.dma("pool", wqs[:], g.d_w_uq_sw.rearrange("(kc p) m -> p kc m", p=128), w=[wqs])
        P.dma("pool", wkv[:], g.d_w_ukv.rearrange("(kc p) m -> p kc m", p=128), w=[wkv])
        P.op("dve", lambda e: e.tensor_scalar(out=rope[:], in0=rope[:], scalar1=SC, scalar2=None, op0=ALU.mult), r=[rope], w=[rope])
        pc = 0
        it = 0
        for h in range(16):
            qn, qr, kn = qn_st[h % 2], qr_st[h % 2], kn_st[h % 2]
            for (t0, tn) in blocks(0, S, 512):
                ps = pss[pc % 6]
                pc += 1
                mm_group(P, ps[:, 0:tn], [(wq[:, kc, h * 192:h * 192 + 128], aq[:, kc, t0:t0 + tn]) for kc in range(4)], r=[wq, aq], w=[ps])
                P.op("act", lambda e, ps=ps, qn=qn, t0=t0, tn=tn: e.mul(out=qn[:, t0:t0 + tn], in_=ps[:, 0:tn], mul=SC), r=[ps], w=[qn])
                psA = pss[pc % 6]
                pc += 1
                mm_group(P, psA[0:64, 0:tn], [(wq[:, kc, h * 192 + 128:h * 192 + 192], aq[:, kc, t0:t0 + tn]) for kc in range(4)],
                         r=[wq, aq], w=[psA])
                psB = pss[pc % 6]
                pc += 1
                mm_group(P, psB[0:64, 0:tn], [(wqs[:, kc, h * 64:(h + 1) * 64], aq[:, kc, t0:t0 + tn]) for kc in range(4)],
                         r=[wqs, aq], w=[psB])
                a_, b_ = t1[it % 2], t2[it % 2]
                it += 1
                P.op("dve", lambda e, a_=a_, psA=psA, t0=t0, tn=tn: e.tensor_tensor(out=a_[:, 0:tn], in0=psA[0:64, 0:tn], in1=rope[:, 0, t0:t0 + tn],
                                                                                   op=ALU.mult), r=[psA, rope], w=[a_])
                P.op("dve", lambda e, b_=b_, psB=psB, t0=t0, tn=tn: e.tensor_tensor(out=b_[:, 0:tn], in0=psB[0:64, 0:tn], in1=rope[:, 1, t0:t0 + tn],
                                                                                   op=ALU.mult), r=[psB, rope], w=[b_])
                P.op("pool", lambda e, a_=a_, b_=b_, qr=qr, t0=t0, tn=tn: e.tensor_tensor(out=qr[:, t0:t0 + tn], in0=a_[:, 0:tn], in1=b_[:, 0:tn], op=ALU.add),
                     r=[a_, b_], w=[qr])
            P.dma("sp", g.qnT[h * 128:(h + 1) * 128, :], qn[:], r=[qn])
            P.dma("sp", g.qrT[h * 64:(h + 1) * 64, :], qr[:], r=[qr])
            for bi, (t0, tn) in enumerate(blocks(0, T, 512)):
                ps = pss[pc % 6]
                pc += 1
                mm_group(P, ps[:, 0:tn], [(wkv[:, kc, h * 256:h * 256 + 128], akv[:, kc, t0:t0 + tn]) for kc in range(4)], r=[wkv, akv], w=[ps])
                if bi % 2 == 0:
                    P.op("act", lambda e, ps=ps, kn=kn, t0=t0, tn=tn: e.copy(out=kn[:, t0:t0 + tn], in_=ps[:, 0:tn]), r=[ps], w=[kn])
                else:
                    P.op("dve", lambda e, ps=ps, kn=kn, t0=t0, tn=tn: e.tensor_copy(out=kn[:, t0:t0 + tn], in_=ps[:, 0:tn]), r=[ps], w=[kn])
            P.dma("sp", g.knT[h * 128:(h + 1) * 128, :], kn[:], r=[kn])
        vc = 0
        for hg in range(4):
            for tt in range(T // 128):
                ps = pss[pc % 6]
                pc += 1
                pairs = []
                for kc in range(4):
                    rhs = wkv[:, kc, hg * 1024:(hg + 1) * 1024].rearrange("p (h two d) -> p h two d", h=4, two=2)[:, :, 1, :]
                    pairs.append((akv[:, kc, tt * 128:(tt + 1) * 128], rhs))
                mm_group(P, ps[:, 0:512].rearrange("p (h d) -> p h d", h=4), pairs, r=[wkv, akv], w=[ps])
                st = vst[vc % 3]
                vc += 1
                if vc % 2 == 0:
                    P.op("act", lambda e, ps=ps, st=st: e.copy(out=st[:], in_=ps[:, 0:512]), r=[ps], w=[st])
                else:
                    P.op("dve", lambda e, ps=ps, st=st: e.tensor_copy(out=st[:], in_=ps[:, 0:512]), r=[ps], w=[st])
                P.dma("sp", g.v1[tt * 128:(tt + 1) * 128, hg * 512:(hg + 1) * 512], st[:], r=[st])


def phase_mla_attn(P, g):
    with P.phase():
        kpe = P.sb("kpe", [128, T], BF16)
        P.op("dve", lambda e: e.memset(kpe[64:128, :], 0.0), w=[kpe])
        P.dma("sp", kpe[0:64, :], g.kpeT[:, :], w=[kpe])
        convert_mlp_weights(P, g, 1)
        qn_ = [P.sb("qn%d" % i, [128, S], BF16) for i in range(2)]
        qr_ = [P.sb("qr%d" % i, [128, S], BF16) for i in range(2)]
        for t_ in qr_:
            P.op("dve", lambda e, t_=t_: e.memset(t_[64:128, :], 0.0), w=[t_])
        kn_ = [P.sb("kn%d" % i, [128, T], BF16) for i in range(2)]
        v_ = [P.sb("v%d" % i, [128, 18, 128], BF16) for i in range(2)]
        o_ = [P.sb("o%d" % i, [128, S], BF16) for i in range(2)]
        sps = [P.ps("s%d" % i, [128, 512]) for i in range(4)]
        ops_ = [P.ps("o%d" % i, [128, 512]) for i in range(2)]
        dps = [P.ps("d%d" % i, [128, 512]) for i in range(2)]
        pts = [P.sb("pt%d" % i, [128, 512], BF16) for i in range(4)]
        rdn = [P.sb("rd%d" % i, [128, 512], F32) for i in range(2)]
        A, B = [], []
        it = 0
        for h in range(16):
            qn, qr, kn, v, o = qn_[h % 2], qr_[h % 2], kn_[h % 2], v_[h % 2], o_[h % 2]
            for qb in range(4):
                op_, dp, rd = ops_[(h * 4 + qb) % 2], dps[(h * 4 + qb) % 2], rdn[(h * 4 + qb) % 2]
                for kc in range(18):
                    sp, pt = sps[it % 4], pts[it % 4]
                    it += 1

                    def stepA(h=h, qb=qb, kc=kc, sp=sp, pt=pt, qn=qn, qr=qr, kn=kn, v=v):
                        if qb == 0 and kc == 0:
                            P.dma("sp", qn[:], g.qnT[h * 128:(h + 1) * 128, :], w=[qn])
                            P.dma("sp", qr[0:64, :], g.qrT[h * 64:(h + 1) * 64, :], w=[qr])
                            P.dma("sp", kn[:], g.knT[h * 128:(h + 1) * 128, :], w=[kn])
                            P.dma("sp", v[:], g.v1[:, h * 128:(h + 1) * 128].rearrange("(c p) d -> p c d", p=128), w=[v])

                        def fs(e):
                            e.matmul(sp[:, :], kn[:, kc * 128:(kc + 1) * 128], qn[:, qb * 512:(qb + 1) * 512], start=True, stop=False)
                            return e.matmul(sp[:, :], kpe[:, kc * 128:(kc + 1) * 128], qr[:, qb * 512:(qb + 1) * 512], start=False, stop=True)
                        P.op("pe", fs, r=[kn, qn, kpe, qr], w=[sp])
                        P.op("act", lambda e: e.activation(out=pt[:], in_=sp[:], func=AF.Exp), r=[sp], w=[pt])

                    def stepB(h=h, qb=qb, kc=kc, pt=pt, v=v, o=o, op_=op_, dp=dp, rd=rd):
                        def fo(e):
                            e.matmul(op_[:], v[:, kc, :], pt[:], start=(kc == 0), stop=(kc == 17))
                            return e.matmul(dp[:], g.onesB[:], pt[:], start=(kc == 0), stop=(kc == 17))
                        P.op("pe", fo, r=[pt, v], w=[op_, dp])
                        if kc == 17:
                            P.op("dve", lambda e: e.reciprocal(out=rd[:], in_=dp[:]), r=[dp], w=[rd])
                            P.op("dve", lambda e: e.tensor_tensor(out=o[:, qb * 512:(qb + 1) * 512], in0=op_[:], in1=rd[:], op=ALU.mult),
                                 r=[op_, rd], w=[o])
                            if qb == 3:
                                P.dma("sp", g.attT[h * 128:(h + 1) * 128, :], o[:], r=[o])
                    A.append(stepA)
                    B.append(stepB)
        n = len(A)
        LAG = 2
        for i in range(n + LAG):
            if i < n:
                A[i]()
            if i >= LAG:
                B[i - LAG]()


def phase_final(P, g):
    with P.phase():
        xb = [P.sb("fx%d" % i, [128, 16, 512], F32) for i in range(2)]
        sq = [P.sb("fsq%d" % i, [128, 512], BF16) for i in range(2)]
        rs = [P.sb("frs%d" % i, [128, 512], F32) for i in range(2)]
        rstd = [P.sb("frstd%d" % i, [128, 512], F32) for i in range(2)]
        ot = [P.sb("fo%d" % i, [128, D], F32) for i in range(2)]
        ssp = [P.ps("fss%d" % i, [128, 512]) for i in range(2)]
        tps = [P.ps("ftp%d" % i, [128, 512]) for i in range(4)]
        pc = 0
        oc = 0
        for bi, (t0, tn) in enumerate(blocks(0, S, 512)):
            x_ = xb[bi % 2]
            ss = ssp[bi % 2]
            P.dma("sp", x_[:, :, 0:tn], g.xT[:, t0:t0 + tn].rearrange("(c p) t -> p c t", p=128), w=[x_])
            for c in range(16):
                s_ = sq[c % 2]
                P.op("act", lambda e, s_=s_, x_=x_, c=c, tn=tn: e.activation(out=s_[:, 0:tn], in_=x_[:, c, 0:tn], func=AF.Square),
                     r=[x_], w=[s_])
                P.op("pe", lambda e, ss=ss, s_=s_, c=c, tn=tn: e.matmul(ss[:, 0:tn], g.onesB[:], s_[:, 0:tn], start=(c == 0), stop=(c == 15)),
                     r=[s_], w=[ss])
            r_ = rs[bi % 2]
            rd_ = rstd[bi % 2]
            P.op("act", lambda e, r_=r_, ss=ss, tn=tn: e.activation(out=r_[:, 0:tn], in_=ss[:, 0:tn], func=AF.Sqrt,
                                                                      bias=g.vecs[:, g.V_EPS:g.V_EPS + 1], scale=1.0 / D),
                 r=[ss], w=[r_])
            P.op("dve", lambda e, r_=r_, rd_=rd_, tn=tn: e.reciprocal(out=rd_[:, 0:tn], in_=r_[:, 0:tn]), r=[r_], w=[rd_])
            for c in range(16):
                P.op("dve", lambda e, x_=x_, rd_=rd_, c=c, tn=tn: e.scalar_tensor_tensor(
                    out=x_[:, c, 0:tn], in0=x_[:, c, 0:tn], scalar=g.vecs[:, g.V_NFIN + c:g.V_NFIN + c + 1], in1=rd_[:, 0:tn],
                    op0=ALU.mult, op1=ALU.mult), r=[x_, rd_], w=[x_])
            for jt in range(tn // 128):
                o_ = ot[oc % 2]
                oc += 1
                for cg in range(4):
                    ps = tps[pc % 4]
                    pc += 1

                    def fn(e, ps=ps, x_=x_, jt=jt, cg=cg):
                        ins = None
                        for cc in range(4):
                            c = cg * 4 + cc
                            ins = e.transpose(ps[:, cc * 128:(cc + 1) * 128], x_[:, c, jt * 128:(jt + 1) * 128], g.identF[:])
                        return ins
                    P.op("pe", fn, r=[x_], w=[ps])
                    if cg % 2 == 0:
                        P.op("act", lambda e, ps=ps, o_=o_, cg=cg: e.copy(out=o_[:, cg * 512:(cg + 1) * 512], in_=ps[:, :]), r=[ps], w=[o_])
                    else:
                        P.op("dve", lambda e, ps=ps, o_=o_, cg=cg: e.tensor_copy(out=o_[:, cg * 512:(cg + 1) * 512], in_=ps[:, :]), r=[ps], w=[o_])
                P.dma("pool", g.d_out[t0 + jt * 128:t0 + (jt + 1) * 128, :], o_[:], r=[o_])


DEBUG_OUT = set()
STOP_AFTER = None


def build_program():
    nc = bass.Bass("TRN2", target_bir_lowering=False)
    g = G()

    def din(name, shape, dt=F32):
        return nc.dram_tensor(name, shape, dt, kind="ExternalInput").ap()

    def scratch(name, shape, dt):
        kind = "ExternalOutput" if name in DEBUG_OUT else "Internal"
        return nc.dram_tensor(name, shape, dt, kind=kind).ap()

    g.d_x = din("x", [S, D])
    g.d_ctx = din("ctx", [C, D])
    g.d_mod_w = din("mod_w", [2, D, 6 * D])
    g.d_identF = din("identF", [128, 128])
    g.d_w_in = din("ab_w_in", [D, 5120])
    g.d_w_out = din("ab_w_out", [D, D])
    g.d_w1 = din("mlp_w1", [2, D, DFF])
    g.d_w2 = din("mlp_w2", [2, DFF, D])
    g.d_nabias = din("nabias", [128, 8 * 16 * 64])
    g.d_lru_wa = din("lru_wa", [2, 8, 128, 128])
    g.d_lru_wx = din("lru_wx", [2, 8, 128, 128])
    g.d_w_dq = din("mla_w_dq", [D, 512])
    g.d_w_dkv = din("mla_w_dkv", [D, 576])
    g.d_w_uq = din("mla_w_uq", [512, 3072])
    g.d_w_ukv = din("mla_w_ukv", [512, 4096])
    g.d_w_o = din("mla_w_o", [D, D])
    g.d_rope = din("rope", [64, 2 * S])
    g.d_w_dkv_sw = din("w_dkv_sw", [D, 64])
    g.d_w_uq_sw = din("w_uq_sw", [512, 1024])
    off = 0
    for nm, n in (("V_CV", 32), ("V_MODB", 192), ("V_NMIX", 32), ("V_NMLP", 32), ("V_NFIN", 16), ("V_EPS", 1), ("V_ONE", 1),
                  ("V_QN", 4), ("V_KVN", 4)):
        setattr(g, nm, off)
        off += n
    g.NV = off
    g.d_vecs = din("vecs", [128, g.NV])
    off = 0
    for nm, n in (("LV_LAM", 16), ("LV_CW", 32), ("LV_CB", 8), ("LV_BA", 16), ("LV_BX", 16)):
        setattr(g, nm, off)
        off += n
    g.NLV = off
    g.d_lvecs = din("lvecs", [128, g.NLV])
    g.d_out = nc.dram_tensor("out", [S, D], F32, kind="ExternalOutput").ap()

    g.xT = scratch("xT", [D, T], F32)
    g.hT = scratch("hT", [D, T], BF16)
    g.qT = scratch("qT", [1024, T], BF16)
    g.kT = scratch("kT", [1024, T], BF16)
    g.v0 = scratch("v0", [T, 1024], BF16)
    g.uT = scratch("uT", [1024, T], F32)
    g.ggT = scratch("ggT", [1024, T], F32)
    g.mixT = scratch("mixT", [D, T], BF16)
    g.cqT = scratch("cqT", [512, S], F32)
    g.ckvT = scratch("ckvT", [640, T], F32)
    g.cqnT = scratch("cqnT", [512, S], BF16)
    g.ckvnT = scratch("ckvnT", [512, T], BF16)
    g.kpeT = scratch("kpeT", [64, T], BF16)
    g.qnT = scratch("qnT", [2048, S], BF16)
    g.qrT = scratch("qrT", [1024, S], BF16)
    g.knT = scratch("knT", [2048, T], BF16)
    g.v1 = scratch("v1", [T, 2048], BF16)
    g.attT = scratch("attT", [D, S], BF16)
    g.W2r = scratch("W2r", [2, 16, 128, 64, 128], BF16)

    es = ExitStack()
    with es:
        def psb(name, shape, dt):
            t = es.enter_context(nc.sbuf_tensor(name, shape, dt))
            return t, Tl(t)
        g.identF, g.t_identF = psb("identF_sb", [128, 128], F32)
        g.identB, g.t_identB = psb("identB_sb", [128, 128], BF16)
        g.onesB, g.t_onesB = psb("onesB_sb", [128, 128], BF16)
        g.vecs, g.t_vecs = psb("vecs_sb", [128, g.NV], F32)
        g.lvecs, g.t_lvecs = psb("lvecs_sb", [128, g.NLV], F32)
        g.MOD, g.t_MOD = psb("MOD_sb", [128, 2, 6, 2, 16], F32)
        P = Prog(nc)
        steps = [
            ("consts", lambda: phase_consts(P, g)),
            ("tin", lambda: phase_transpose_in(P, g)),
            ("mod", lambda: phase_modulation(P, g, [(0, 0), (0, 1)])),
            ("norm0a", lambda: norm_mod(P, g, 0, 1, 0, T)),
            ("inproj0", lambda: phase_inproj0(P, g)),
            ("nalru", lambda: phase_na_lru(P, g)),
            ("outproj0", lambda: phase_outproj(P, g, 0, g.d_w_out, g.mixT, T)),
            ("norm0b", lambda: norm_mod(P, g, 0, 4, 3, T)),
            ("mlp0", lambda: phase_mlp(P, g, 0, T)),
            ("norm1a", lambda: norm_mod(P, g, 1, 1, 0, T)),
            ("mladown", lambda: phase_mla_down(P, g)),
            ("mlanq", lambda: phase_norm(P, g, g.cqT, g.cqnT, 4, blocks(0, S, 512), lambda c, j: g.vecs[:, g.V_QN + c:g.V_QN + c + 1], None)),
            ("mlankv", lambda: phase_norm(P, g, g.ckvT, g.ckvnT, 4, blocks(0, T, 512), lambda c, j: g.vecs[:, g.V_KVN + c:g.V_KVN + c + 1], None)),
            ("mlaup", lambda: phase_mla_up(P, g)),
            ("mlaattn", lambda: phase_mla_attn(P, g)),
            ("outproj1", lambda: phase_outproj(P, g, 1, g.d_w_o, g.attT, S)),
            ("norm1b", lambda: norm_mod(P, g, 1, 4, 3, S)),
            ("mlp1", lambda: phase_mlp(P, g, 1, S)),
            ("final", lambda: phase_final(P, g)),
        ]
        for name, fn in steps:
            fn()
            if STOP_AFTER == name:
                break
    return nc


def pack_vecs(inp, b):
    def fm(v):
        return np.ascontiguousarray(np.asarray(v, np.float32).reshape(-1, 128).T)
    cols = [fm(inp["c"][b]), fm(inp["c_ctx"])]
    for L in range(2):
        cols.append(fm(inp["mod_b"][L]))
    for L in range(2):
        cols.append(fm(inp["norm_mix"][L]))
    for L in range(2):
        cols.append(fm(inp["norm_mlp"][L]))
    cols.append(fm(inp["final_norm"]))
    cols.append(np.full((128, 1), EPS, np.float32))
    cols.append(np.ones((128, 1), np.float32))
    cols.append(fm(inp["mla_q_norm"][0]))
    cols.append(fm(inp["mla_kv_norm"][0]))
    return np.ascontiguousarray(np.concatenate(cols, axis=1))


def pack_lvecs(inp):
    def fm(v):
        return np.ascontiguousarray(np.asarray(v, np.float32).reshape(-1, 128).T)
    cols = [fm(inp["lru_lambda"][0].reshape(-1))]
    cols.append(fm(inp["lru_conv_w"][0].reshape(-1)))
    cols.append(fm(inp["lru_conv_b"][0]))
    cols.append(fm(inp["lru_ba"][0].reshape(-1)))
    cols.append(fm(inp["lru_bx"][0].reshape(-1)))
    return np.ascontiguousarray(np.concatenate(cols, axis=1))


def rope_table():
    pos = np.arange(S)
    row = (pos // 64).astype(np.float32)
    col = (pos % 64).astype(np.float32)
    inv = np.power(np.float32(10000.0), -np.arange(0, 32, 2, dtype=np.float32) / np.float32(32)).astype(np.float32)
    out = np.zeros((64, 2, S), np.float32)
    for axis, pv in enumerate((row, col)):
        ang = (pv[None, :] * inv[:, None]).astype(np.float32)
        for half in range(2):
            p0 = axis * 32 + half * 16
            out[p0:p0 + 16, 0, :] = np.cos(ang)
            out[p0:p0 + 16, 1, :] = -np.sin(ang) if half == 0 else np.sin(ang)
    return np.ascontiguousarray(out.reshape(64, 2 * S))


def make_in_maps(inp, cores):
    ident = np.eye(128, dtype=np.float32)
    nab = make_nabias(np.asarray(inp["na_rpb"][0], np.float32))
    lv = pack_lvecs(inp)
    rope = rope_table()
    perm = np.array([(p + 16) if (p % 32) < 16 else (p - 16) for p in range(64)])
    w_dkv_sw = np.ascontiguousarray(inp["mla_w_dkv"][0][:, 512 + perm])
    uq = inp["mla_w_uq"][0].reshape(512, 16, 192)
    w_uq_sw = np.ascontiguousarray(uq[:, :, 128 + perm].reshape(512, 1024))
    shared = {
        "w_dkv_sw": w_dkv_sw, "w_uq_sw": w_uq_sw,
        "mod_w": inp["mod_w"], "identF": ident, "ab_w_in": inp["ab_w_in"][0], "ab_w_out": inp["ab_w_out"][0],
        "mlp_w1": inp["mlp_w1"], "mlp_w2": inp["mlp_w2"], "nabias": nab, "lvecs": lv,
        "lru_wa": inp["lru_wa"][0], "lru_wx": inp["lru_wx"][0],
        "mla_w_dq": inp["mla_w_dq"][0], "mla_w_dkv": inp["mla_w_dkv"][0], "mla_w_uq": inp["mla_w_uq"][0],
        "mla_w_ukv": inp["mla_w_ukv"][0], "mla_w_o": inp["mla_w_o"][0], "rope": rope,
    }
    maps = []
    for b in cores:
        m = dict(shared)
        m["x"] = np.ascontiguousarray(inp["x"][b])
        m["ctx"] = np.ascontiguousarray(inp["ctx"][b])
        m["vecs"] = pack_vecs(inp, b)
        maps.append(m)
    return maps


def kernel(**inputs):
    inp = {k: np.asarray(v) for k, v in inputs.items()}
    nc = build_program()
    maps = make_in_maps(inp, list(range(NCORES)))
    res = run_bass_kernel_spmd(nc, maps, core_ids=list(range(NCORES)))
    out = np.stack([res.results[i]["out"] for i in range(NCORES)], axis=0)
    return out.astype(np.float32)
```
